# Optimizing a Trainium2 kernel written in Bass

```python
import math
import jax, jax.numpy as jnp
from jax import lax
import numpy as np

D_MODEL = 4096
BATCH = 4
SEQ = 2048
DEPTH = 2

CHUNK = 64
Q_BLOCK = 128
A_HEADS = 16
A_QK_DIM = 64
A_V_DIM = 128
B_HEADS = 16
B_HEAD_DIM = 128
IDX_HEADS = 16
IDX_DIM = 64
TOPK_MAX = 256
N_BUCKETS = 32
MAX_DISTANCE = 128
N_HEADS_TOTAL = A_HEADS + B_HEADS
D_FF = 11008
CONV_WIDTH = 3
EPS = 1e-6

A_Q = A_HEADS * 2 * A_QK_DIM
A_K = A_HEADS * 2 * A_QK_DIM
A_V = A_HEADS * A_V_DIM
B_Q = B_HEADS * B_HEAD_DIM
B_K = B_HEAD_DIM
B_V = B_HEAD_DIM
I_Q = IDX_HEADS * IDX_DIM
I_K = IDX_DIM
I_W = IDX_HEADS
IN_SPLITS = (A_Q, A_K, A_V, B_Q, B_K, B_V, I_Q, I_K, I_W)
D_IN = A_Q + A_K + A_V + B_Q + B_K + B_V + I_Q + I_K + I_W
D_MIX = A_HEADS * A_V_DIM + B_HEADS * B_HEAD_DIM

kernel_name = "hybrid_diffattn_dsa_convglu"


def rms_norm(x, g):
    xf = x.astype(jnp.float32)
    y = xf * lax.rsqrt(jnp.mean(xf * xf, axis=-1, keepdims=True) + EPS)
    return (y * g.astype(jnp.float32)).astype(x.dtype)


def rel_bucket(rel):
    nb = N_BUCKETS // 2
    max_exact = nb // 2
    bucket = jnp.where(rel > 0, nb, 0)
    n = jnp.abs(rel)
    nf = jnp.maximum(n, 1).astype(jnp.float32)
    large = max_exact + (jnp.log(nf / max_exact) / math.log(MAX_DISTANCE / max_exact)
                         * (nb - max_exact)).astype(jnp.int32)
    large = jnp.minimum(large, nb - 1)
    return bucket + jnp.where(n < max_exact, n, large)


def chunk_mask(tq, ts):
    return (ts[None, :] // CHUNK) <= (tq[:, None] // CHUNK)


def diff_attention(q, k, v, lam, rel_table):
    s_len = q.shape[1]
    scale = A_QK_DIM ** -0.5
    outs = []
    for i in range(s_len // Q_BLOCK):
        q0, end = i * Q_BLOCK, (i + 1) * Q_BLOCK
        tq = jnp.arange(q0, end)
        ts = jnp.arange(end)
        bias = rel_table[rel_bucket(ts[None, :] - tq[:, None])][..., :A_HEADS]
        bias = jnp.transpose(bias, (2, 0, 1))[None, :, None].astype(jnp.float32)
        logits = jnp.einsum('bqhjd,bshjd->bhjqs', q[:, q0:end], k[:, :end]).astype(jnp.float32) * scale + bias
        logits = jnp.where(chunk_mask(tq, ts), logits, -jnp.inf)
        p = jax.nn.softmax(logits, axis=-1)
        attn = p[:, :, 0] - lam * p[:, :, 1]
        outs.append(jnp.einsum('bhqs,bshd->bqhd', attn.astype(v.dtype), v[:, :end]))
    return jnp.concatenate(outs, axis=1)


def dsa_attention(q, k, v, qi, ki, wi, rel_table):
    s_len = q.shape[1]
    topk = min(TOPK_MAX, s_len // 4)
    scale = B_HEAD_DIM ** -0.5
    outs = []
    for i in range(s_len // Q_BLOCK):
        q0, end = i * Q_BLOCK, (i + 1) * Q_BLOCK
        tq = jnp.arange(q0, end)
        ts = jnp.arange(end)
        idx_logits = jnp.einsum('bqhd,bsd->bqhs', qi[:, q0:end], ki[:, :end]) * (IDX_DIM ** -0.5)
        score = jnp.einsum('bqhs,bqh->bqs', jax.nn.relu(idx_logits), wi[:, q0:end] * (IDX_HEADS ** -0.5))
        score = jnp.where(chunk_mask(tq, ts)[None], score.astype(jnp.float32), -jnp.inf)
        kk = min(topk, end)
        vals, sel = lax.top_k(score, kk)
        valid = jnp.isfinite(vals)
        kg = jax.vmap(lambda a, ix: a[ix])(k[:, :end], sel)
        vg = jax.vmap(lambda a, ix: a[ix])(v[:, :end], sel)
        bias = rel_table[rel_bucket(sel - tq[None, :, None])][..., A_HEADS:]
        bias = jnp.swapaxes(bias, 2, 3).astype(jnp.float32)
        logits = jnp.einsum('bqhd,bqkd->bqhk', q[:, q0:end], kg).astype(jnp.float32) * scale + bias
        logits = jnp.where(valid[:, :, None, :], logits, -jnp.inf)
        p = jax.nn.softmax(logits, axis=-1)
        outs.append(jnp.einsum('bqhk,bqkd->bqhd', p.astype(v.dtype), vg))
    return jnp.concatenate(outs, axis=1)


def causal_dwconv(h, w, b):
    y = lax.conv_general_dilated(
        h, w[:, None, :].astype(h.dtype), window_strides=(1,),
        padding=((CONV_WIDTH - 1, 0),), dimension_numbers=('NWC', 'WIO', 'NWC'),
        feature_group_count=h.shape[-1])
    return y + b


def setup_inputs(seed: int = 0) -> dict:
    key = jax.random.key(seed)
    ks = jax.random.split(key, 20)
    f32 = jnp.float32
    nrm = lambda k, shape, s: jax.random.normal(k, shape, f32) * s
    gain = lambda k, shape: 1.0 + 0.02 * jax.random.normal(k, shape, f32)
    return {
        "x": nrm(ks[0], (BATCH, SEQ, D_MODEL), 1.0),
        "attn_norm": gain(ks[1], (DEPTH, D_MODEL)),
        "w_in": nrm(ks[2], (DEPTH, D_MODEL, D_IN), D_MODEL ** -0.5),
        "a_q_norm": gain(ks[3], (DEPTH, A_QK_DIM)),
        "a_k_norm": gain(ks[4], (DEPTH, A_QK_DIM)),
        "lambda_qk": nrm(ks[5], (DEPTH, 4, A_QK_DIM), 0.1),
        "a_out_norm": gain(ks[6], (DEPTH, A_V_DIM)),
        "b_q_norm": gain(ks[7], (DEPTH, B_HEAD_DIM)),
        "b_k_norm": gain(ks[8], (DEPTH, B_HEAD_DIM)),
        "rel_bias": nrm(ks[9], (N_BUCKETS, N_HEADS_TOTAL), 0.5),
        "w_out": nrm(ks[10], (DEPTH, D_MIX, D_MODEL), D_MIX ** -0.5),
        "ffn_norm": gain(ks[11], (DEPTH, D_MODEL)),
        "w_gate_up": nrm(ks[12], (DEPTH, D_MODEL, 2 * D_FF), D_MODEL ** -0.5),
        "conv_w": nrm(ks[13], (DEPTH, CONV_WIDTH, D_FF), CONV_WIDTH ** -0.5),
        "conv_b": nrm(ks[14], (DEPTH, D_FF), 0.02),
        "w_down": nrm(ks[15], (DEPTH, D_FF, D_MODEL), D_FF ** -0.5),
    }


def reference(x, attn_norm, w_in, a_q_norm, a_k_norm, lambda_qk, a_out_norm,
              b_q_norm, b_k_norm, rel_bias, w_out, ffn_norm, w_gate_up, conv_w,
              conv_b, w_down):
    bsz, s_len, _ = x.shape
    offsets = np.cumsum(IN_SPLITS)[:-1].tolist()
    for l in range(DEPTH):
        h = rms_norm(x, attn_norm[l])
        proj = h @ w_in[l]
        qa, ka, va, qb, kb, vb, qi, ki, wi = jnp.split(proj, offsets, axis=-1)
        qa = rms_norm(qa.reshape(bsz, s_len, A_HEADS, 2, A_QK_DIM), a_q_norm[l])
        ka = rms_norm(ka.reshape(bsz, s_len, A_HEADS, 2, A_QK_DIM), a_k_norm[l])
        va = va.reshape(bsz, s_len, A_HEADS, A_V_DIM)
        lam_init = 0.8 - 0.6 * math.exp(-0.3 * l)
        lq = lambda_qk[l].astype(jnp.float32)
        lam = jnp.exp(jnp.sum(lq[0] * lq[1])) - jnp.exp(jnp.sum(lq[2] * lq[3])) + lam_init
        oa = diff_attention(qa, ka, va, lam, rel_bias)
        oa = rms_norm(oa, a_out_norm[l]) * (1.0 - lam_init)
        qb = rms_norm(qb.reshape(bsz, s_len, B_HEADS, B_HEAD_DIM), b_q_norm[l])
        kb = rms_norm(kb, b_k_norm[l])
        ob = dsa_attention(qb, kb, vb,
                           qi.reshape(bsz, s_len, IDX_HEADS, IDX_DIM), ki, wi, rel_bias)
        mix = jnp.concatenate([oa.reshape(bsz, s_len, -1), ob.reshape(bsz, s_len, -1)], axis=-1)
        x = x + mix @ w_out[l]
        h = rms_norm(x, ffn_norm[l])
        g, u = jnp.split(h @ w_gate_up[l], 2, axis=-1)
        g = causal_dwconv(g, conv_w[l], conv_b[l])
        x = x + (jax.nn.silu(g) * u) @ w_down[l]
    return x
```

```python
import math
import numpy as np
import concourse.bass as bass
import concourse.mybir as mybir
from concourse.bass_utils import run_bass_kernel_spmd

F32 = mybir.dt.float32
BF16 = mybir.dt.bfloat16
AF = mybir.ActivationFunctionType
ALU = mybir.AluOpType
AX = mybir.AxisListType

EPS = 1e-6
NEG = -30000.0
NINF = -1.0e30


class Cfg:
    def __init__(self, D=4096, S=2048, HA=16, HB=16, HI=16, DFF=11008, DEPTH=2, BATCH=4, TOPK_MAX=256, TG=1024):
        self.D, self.S, self.HA, self.HB, self.HI, self.DFF, self.DEPTH, self.BATCH = D, S, HA, HB, HI, DFF, DEPTH, BATCH
        self.T = S
        self.TG = TG
        self.NPASS = S // TG
        self.NTBG = TG // 512
        self.KD = D // 128
        self.NTB = S // 512
        self.NT = S // 128
        self.AQ = HA * 128; self.AK = HA * 128; self.AV = HA * 128
        self.BQ = HB * 128; self.BK = 128; self.BV = 128
        self.IQ = HI * 64; self.IK = 64; self.IW = HI
        self.DIN = self.AQ + self.AK + self.AV + self.BQ + self.BK + self.BV + self.IQ + self.IK + self.IW
        self.DMIX = HA * 128 + HB * 128
        self.KM = self.DMIX // 128
        self.NFC = DFF // 128
        self.NH = HA + HB
        self.TOPK = min(TOPK_MAX, S // 4)
        self.KMAX = max(self.KD, self.KM)
        assert DFF % 128 == 0 and D % 128 == 0 and S % 512 == 0 and HI % 2 == 0
        assert self.IK + self.IW <= 128


FULL = Cfg()


class Res:
    __slots__ = ("name", "w", "r", "dsem", "dval", "multi")

    def __init__(self, name, multi=False):
        self.name = name
        self.w = None
        self.r = {}
        self.dsem = None
        self.dval = 0
        self.multi = multi


class KB:
    def __init__(self, nc):
        self.nc = nc
        self.engs = {"pe": nc.tensor, "act": nc.scalar, "dve": nc.vector, "pool": nc.gpsimd, "sp": nc.sync}
        self.esem = {}
        self.ecnt = {}
        for n in ("pe", "act", "dve", "pool"):
            self.esem[n] = nc.semaphore("es_" + n).__enter__()
            self.ecnt[n] = 0
        self.waited = {n: {} for n in self.engs}
        self.nsem = 4
        self.dres = []

    def sb(self, name, shape, dt):
        t = self.nc.sbuf_tensor(name, list(shape), dt).__enter__()
        return t

    def _wait(self, eng, ev, is_dma):
        sem, val, owner = ev
        if owner == eng and not is_dma:
            return
        key = id(sem)
        if self.waited[eng].get(key, 0) >= val:
            return
        self.engs[eng].wait_ge(sem, val)
        self.waited[eng][key] = val

    def _deps(self, eng, reads, writes, is_dma):
        for r in reads:
            if r.w is not None:
                self._wait(eng, r.w, is_dma)
        for w in writes:
            if w.w is not None and not w.multi:
                self._wait(eng, w.w, is_dma)
            for ev in list(w.r.values()):
                self._wait(eng, ev, is_dma)

    def op(self, eng, fn, reads=(), writes=(), strict=False):
        self._deps(eng, reads, writes, strict or eng in ("dve", "act"))
        ins = fn()
        self.ecnt[eng] += 1
        sem = self.esem[eng]
        ins.then_inc(sem, 1)
        ev = (sem, self.ecnt[eng], eng)
        for r in reads:
            r.r[id(sem)] = ev
        for w in writes:
            w.w = ev
            if not w.multi:
                w.r = {}

    def dma(self, q, out, in_, reads=(), writes=()):
        self._deps(q, reads, writes, True)
        w0 = writes[0]
        if w0.dsem is None:
            w0.dsem = self.nc.semaphore("ds_" + w0.name).__enter__()
            self.nsem += 1
            self.dres.append(w0)
        ins = self.engs[q].dma_start(out=out, in_=in_)
        ins.then_inc(w0.dsem, 16)
        w0.dval += 16
        ev = (w0.dsem, w0.dval, None)
        for r in reads:
            r.r[id(w0.dsem)] = ev
        for w in writes:
            w.w = ev
            if not w.multi:
                w.r = {}

    def final_wait(self, eng, res):
        if res.w is not None:
            self._wait(eng, res.w, True)


class Slots:
    def __init__(self, name, views):
        self.t = list(views)
        self.r = [Res(f"{name}{i}") for i in range(len(views))]
        self.i = -1
        self.n = len(views)

    def next(self):
        self.i = (self.i + 1) % self.n
        return self.t[self.i], self.r[self.i]


class Arena:
    def __init__(self, kb, name, nbytes):
        self.n2 = (nbytes + 1) // 2
        self.t = kb.sb(name, [128, self.n2], BF16)
        self.off = 0

    def reset(self):
        self.off = 0

    def alloc(self, shape, dt, parts=128):
        esz = 4 if dt == F32 else 2
        n = 1
        for s in shape[1:]:
            n *= s
        nb = n * esz
        nb = (nb + 63) // 64 * 64
        o2 = self.off // 2
        assert o2 + nb // 2 <= self.n2, ("arena overflow", self.off, nb, self.n2 * 2)
        v = self.t[0:shape[0], o2:o2 + (n * esz) // 2]
        if dt == F32:
            v = v.bitcast(F32)
        if len(shape) == 3:
            v = v.rearrange("p (a b) -> p a b", b=shape[2])
        self.off += nb
        return v


def build_program(cfg, debug=False):
    c = cfg
    nc = bass.Bass("TRN2", target_bir_lowering=False)
    kb = KB(nc)
    T, KD, KM, NTB, NT, NFC = c.T, c.KD, c.KM, c.NTB, c.NT, c.NFC
    TG, NPASS, NTBG = c.TG, c.NPASS, c.NTBG
    L = c.DEPTH

    def din(name, shape, dt=F32):
        return nc.dram_tensor(name, list(shape), dt, kind="ExternalInput").ap()

    def dint(name, shape, dt):
        kind = "ExternalOutput" if debug else "Internal"
        return nc.dram_tensor(name, list(shape), dt, kind=kind).ap()

    xT = din("xT", [KD, 128, T])
    w_in = din("w_in", [L, c.D, c.DIN])
    w_out = din("w_out", [L, c.DMIX, c.D])
    w_gu = din("w_gu", [L, c.D, 2 * c.DFF])
    w_dn = din("w_dn", [L, c.DFF, c.D])
    gA = din("gA", [L, 128, KD])
    gF = din("gF", [L, 128, KD])
    gsm = din("gsm", [L, 128, 8])
    lqb = din("lqb", [L, 128, 256])
    cwv = din("cwv", [L, 128, NFC, 3])
    cbv = din("cbv", [L, 128, NFC])
    relb15 = din("relb15", [128, c.NH])
    btab = din("btab", [c.NH, 128, 5, 512])
    identd = din("identd", [128, 128])
    maskd = din("maskd", [128, 5, 512])
    yT = nc.dram_tensor("yT", [KD, 128, T], F32, kind="ExternalOutput").ap()

    xa = dint("xa", [KD, 128, T], F32)
    xb = dint("xb", [KD, 128, T], F32)
    qaT = dint("qaT", [c.HA, 128, T], BF16)
    kaT = dint("kaT", [c.HA, 128, T], BF16)
    vaT = dint("vaT", [c.HA, 128, T], BF16)
    qbT = dint("qbT", [c.HB, 128, T], BF16)
    kbT = dint("kbT", [128, T], BF16)
    vbT = dint("vbT", [128, T], BF16)
    qiT = dint("qiT", [c.HI // 2, 128, T], BF16)
    kiT = dint("kiT", [64, T], BF16)
    wiT = dint("wiT", [c.IW, T], F32)
    mixT = dint("mixT", [KM, 128, T], BF16)
    actT = dint("actT", [NFC, 128, T], BF16)
    R = {n: Res(n, multi=True) for n in ("xT", "xa", "xb", "yT", "qaT", "kaT", "vaT", "qbT", "kbT", "vbT",
                                          "qiT", "kiT", "wiT", "mixT", "actT")}

    pb = [nc.psum_tensor(f"pb{i}", [128, 512], F32).__enter__() for i in range(8)]
    pr = [Res(f"pb{i}") for i in range(8)]
    pb7b = pb[7][:].bitcast(BF16)

    hbytes = max(c.KMAX * TG * 2, NT * T * 2)
    big = Arena(kb, "big", hbytes)
    hT = big.alloc([128, c.KMAX, TG], BF16); big.reset()
    mq_all = big.alloc([128, NT, T], BF16); big.reset()
    hres = [Res(f"hT{k}") for k in range(c.KMAX)]
    r_mq = [Res(f"mq{i}") for i in range(NT)]
    ident_f = kb.sb("ident_f", [128, 128], F32); r_identf = Res("ident_f")
    ident_b = kb.sb("ident_b", [128, 128], BF16); r_identb = Res("ident_b")
    ones_f = kb.sb("ones_f", [128, 128], F32); r_onesf = Res("ones_f")
    ones_b = kb.sb("ones_b", [128, 128], BF16); r_onesb = Res("ones_b")
    bd64_f = kb.sb("bd64_f", [128, 128], F32); r_bd64 = Res("bd64_f")
    mask_f = kb.sb("mask_f", [128, 5, 512], F32); r_maskf = Res("mask_f")
    relb_sb = kb.sb("relb_sb", [128, c.NH], F32); r_relb = Res("relb")
    thr_c = kb.sb("thr_c", [128, 1], F32); r_thrc = Res("thr_c")
    epsb = kb.sb("epsb", [128, 4], F32); r_epsb = Res("epsb")
    ssq_part = kb.sb("ssq_part", [128, T], F32); r_ssq = Res("ssq_part")
    rstd = kb.sb("rstd", [128, T], F32); r_rstd = Res("rstd")
    gA_sb = kb.sb("gA_sb", [128, KD], F32); r_gA = Res("gA")
    gF_sb = kb.sb("gF_sb", [128, KD], F32); r_gF = Res("gF")
    gsm_sb = kb.sb("gsm_sb", [128, 8], F32); r_gsm = Res("gsm")
    gs2 = kb.sb("gs2", [128, 8], F32); r_gs2 = Res("gs2")
    lq_sb = kb.sb("lq_sb", [128, 256], F32); r_lq = Res("lq")
    lwork = kb.sb("lwork", [128, 128], F32); r_lwork = Res("lwork")
    lsm = kb.sb("lsm", [128, 8], F32); r_lsm = Res("lsm")
    cw_sb = kb.sb("cw_sb", [128, NFC, 3], F32); r_cw = Res("cw")
    cb_sb = kb.sb("cb_sb", [128, NFC], F32); r_cb = Res("cb")
    gcar = kb.sb("gcar", [128, NFC, 2], F32); r_gcar = Res("gcar")
    f32a = Slots("f32a", [kb.sb(f"f32a{i}", [128, 512], F32) for i in range(4)])
    f32b = Slots("f32b", [kb.sb(f"f32b{i}", [128, 512], F32) for i in range(4)])
    ob16 = Slots("ob16", [kb.sb(f"ob16{i}", [128, 512], BF16) for i in range(4)])
    m8s = Slots("m8s", [kb.sb(f"m8s{i}", [128, 8], F32) for i in range(2)])
    m8cs = Slots("m8cs", [kb.sb(f"m8cs{i}", [128, 8], F32) for i in range(2)])

    GF = 8
    CC = min(KD, max(1, 8 // NTB))
    g_bytes = 4 * c.KMAX * 256 + 3 * GF * CC * 256 + 4 * T * 2 + 3 * 2112 + 3 * 2048 + 1024
    a_bytes = 8 * T * 2 + 2 * 5120 + 4 * 1024 + 2 * T * 2 + 1024 + 2 * c.HI * 256 + 2 * T * 4 + 2048
    ar = Arena(kb, "arena", max(g_bytes, a_bytes))
    wsl = Slots("wsl", [ar.alloc([128, c.KMAX, 128], BF16) for _ in range(4)])
    wds = Slots("wds", [ar.alloc([128, GF, CC * 128], BF16) for _ in range(3)])
    asl = Slots("asl", [ar.alloc([128, T], BF16) for _ in range(4)])
    gsl = Slots("gsl", [ar.alloc([128, 514], F32) for _ in range(3)])
    xs5 = Slots("xs5", [ar.alloc([128, 512], F32) for _ in range(3)])
    ar.reset()
    qhs = Slots("qhs", [ar.alloc([128, T], BF16) for _ in range(2)])
    khs = Slots("khs", [ar.alloc([128, T], BF16) for _ in range(2)])
    vTs = Slots("vTs", [ar.alloc([128, T], BF16) for _ in range(2)])
    vhs = Slots("vhs", [ar.alloc([128, NT, 128], BF16) for _ in range(2)])
    bms = Slots("bms", [ar.alloc([128, 5, 512], BF16) for _ in range(2)])
    ebs = Slots("ebs", [ar.alloc([128, 512], BF16) for _ in range(4)])
    kb_sb = ar.alloc([128, T], BF16); r_kbsb = Res("kb_sb")
    ki_sb = ar.alloc([64, T], BF16); r_kisb = Res("ki_sb")
    wi_tok = ar.alloc([128, NT, 16], F32); r_witok = Res("wi_tok")
    qis = Slots("qis", [ar.alloc([64, c.HI, 128], BF16) for _ in range(2)])
    sc_t = ar.alloc([128, T], F32); rsc = Res("sc")
    work = ar.alloc([128, T], F32); r_work = Res("work")

    def barrier():
        for e in ("pe", "act", "dve", "pool", "sp"):
            for n in ("pe", "act", "dve", "pool"):
                if kb.ecnt[n] > 0:
                    kb._wait(e, (kb.esem[n], kb.ecnt[n], None), True)
            for r in kb.dres:
                kb._wait(e, (r.dsem, r.dval, None), True)

    kb.dma("sp", ident_f[:], identd, writes=[r_identf])
    kb.dma("sp", mask_f[:], maskd, writes=[r_maskf])
    kb.dma("sp", relb_sb[:], relb15, writes=[r_relb])
    kb.op("dve", lambda: nc.vector.tensor_copy(out=ident_b[:], in_=ident_f[:]), reads=[r_identf], writes=[r_identb])
    kb.op("dve", lambda: nc.vector.memset(ones_f[:], 1.0), writes=[r_onesf])
    kb.op("dve", lambda: nc.vector.memset(ones_b[:], 1.0), writes=[r_onesb])
    kb.op("dve", lambda: nc.vector.memset(bd64_f[:], 0.0), writes=[r_bd64])
    kb.op("dve", lambda: nc.vector.memset(bd64_f[0:64, 0:64], 1.0), writes=[r_bd64])
    kb.op("dve", lambda: nc.vector.memset(bd64_f[64:128, 64:128], 1.0), writes=[r_bd64])
    kb.op("dve", lambda: nc.vector.memset(thr_c[:], -1.0e29), writes=[r_thrc])
    kb.op("dve", lambda: nc.vector.memset(epsb[:, 1:2], float(64 * EPS)), writes=[r_epsb])
    kb.op("dve", lambda: nc.vector.memset(epsb[:, 2:3], float(128 * EPS)), writes=[r_epsb])
    kb.op("dve", lambda: nc.vector.memset(epsb[:, 3:4], float(c.D * EPS)), writes=[r_epsb])

    dbg_n = [0]

    def dbg_dump(name, ap, res, shape, dt=F32):
        if not debug:
            return
        d = nc.dram_tensor("dbg_" + name, list(shape), dt, kind="ExternalOutput").ap()
        kb.dma("sp", d, ap, reads=[res], writes=[Res("dbg_" + name)])

    bank_rot = [0]

    def next_bank(n=6):
        b = bank_rot[0] % n
        bank_rot[0] += 1
        return b


    def rsqrt_op(out_ap, out_res, in_ap, in_res, G):
        kb.op("act", lambda: nc.scalar.activation(out=out_ap, in_=in_ap, func=AF.Sqrt, bias=epsb[:, {0: 0, 64: 1, 128: 2}.get(G, 0):{0: 0, 64: 1, 128: 2}.get(G, 0) + 1] if G in (64, 128) else epsb[:, 3:4], scale=1.0),
              reads=[in_res, r_epsb], writes=[out_res])
        kb.op("dve", lambda: nc.vector.reciprocal(out=out_ap, in_=out_ap), reads=[out_res], writes=[out_res])

    def ssq_zero():
        kb.op("dve", lambda: nc.vector.memset(ssq_part[:], 0.0), writes=[r_ssq])

    def ssq_accum(src_ap, src_res, lo, n):
        sq, rsq = f32b.next()
        kb.op("act", lambda: nc.scalar.activation(out=sq[:, :n], in_=src_ap, func=AF.Square),
              reads=[src_res], writes=[rsq])
        kb.op("dve", lambda: nc.vector.tensor_tensor(out=ssq_part[:, lo:lo + n], in0=ssq_part[:, lo:lo + n],
                                                     in1=sq[:, :n], op=ALU.add),
              reads=[rsq, r_ssq], writes=[r_ssq])

    def finish_norm(G):
        for tb in range(NTB):
            b = 6
            kb.op("pe", lambda: nc.tensor.matmul(pb[b][:, :], lhsT=ones_f[:], rhs=ssq_part[:, tb * 512:(tb + 1) * 512],
                                                 start=True, stop=True),
                  reads=[r_onesf, r_ssq], writes=[pr[b]])
            rsqrt_op(rstd[:, tb * 512:(tb + 1) * 512], r_rstd, pb[b][:, :], pr[b], G)

    def phase_S(src, rsrc):
        ssq_zero()
        for k in range(KD):
            for tb in range(NTB):
                xk, rxk = xs5.next()
                kb.dma("sp", xk[:], src[k][:, tb * 512:(tb + 1) * 512], reads=[rsrc], writes=[rxk])
                ssq_accum(xk[:], rxk, tb * 512, 512)

    def phase_N(ps, src, rsrc, gcol, rg):
        for k in range(KD):
            for tb in range(NTBG):
                t0 = ps * TG + tb * 512
                xk, rxk = xs5.next()
                kb.dma("sp", xk[:], src[k][:, t0:t0 + 512], reads=[rsrc], writes=[rxk])
                kb.op("dve", lambda: nc.vector.scalar_tensor_tensor(out=hT[:, k, tb * 512:(tb + 1) * 512], in0=xk[:],
                                                                    scalar=gcol[:, k:k + 1], in1=rstd[:, t0:t0 + 512],
                                                                    op0=ALU.mult, op1=ALU.mult),
                      reads=[rxk, rg, r_rstd], writes=[hres[k]])

    def load_w(wdram, c0, M, nk):
        wt, rw = wsl.next()
        kb.dma("pool", wt[:, :nk, :M], wdram[:, c0:c0 + M].rearrange("(k p) c -> p k c", p=128), writes=[rw])
        return wt, rw

    def gemm_unit(bank, wt, rw, M, nk, tb):
        def fn():
            ins = None
            for k in range(nk):
                ins = nc.tensor.matmul(pb[bank][:M, :], lhsT=wt[:, k, :M], rhs=hT[:, k, tb * 512:(tb + 1) * 512],
                                       start=(k == 0), stop=(k == nk - 1))
            return ins
        kb.op("pe", fn, reads=[rw] + hres[:nk], writes=[pr[bank]])

    def stream_groups(wdram, groups, nk, unit_fn, inflight):
        loaded = []
        n = len(groups)

        def ld(i):
            loaded.append([load_w(wdram, c0, M, nk) + (M,) for (c0, M) in groups[i][1]])
        for i in range(min(inflight, n)):
            ld(i)
        for i in range(n):
            unit_fn(groups[i][0], loaded[i])
            if i + inflight < n:
                ld(i + inflight)

    def phase_P(l, ps):
        groups = []
        o = 0
        for h in range(c.HA): groups.append((("aq", h), [(o + h * 128, 128)]))
        o += c.AQ
        for h in range(c.HA): groups.append((("ak", h), [(o + h * 128, 128)]))
        o += c.AK
        for h in range(c.HA): groups.append((("av", h), [(o + h * 128, 128)]))
        o += c.AV
        for h in range(c.HB): groups.append((("bq", h), [(o + h * 128, 128)]))
        o += c.BQ
        groups.append((("bk", 0), [(o, 128)])); o += 128
        groups.append((("bv", 0), [(o, 128)])); o += 128
        for j in range(c.HI // 2): groups.append((("iq", j), [(o + j * 128, 128)]))
        o += c.IQ
        groups.append((("ikw", 0), [(o, c.IK + c.IW)]))

        def unit(tag, ws):
            kind, idx = tag
            wt, rw, M = ws[0]
            for tb in range(NTBG):
                b = next_bank(6)
                gemm_unit(b, wt, rw, M, KD, tb)
                t0 = ps * TG + tb * 512
                cols = slice(t0, t0 + 512)
                if kind in ("av", "bv", "iq"):
                    ob, rob = ob16.next()
                    kb.op("act", lambda: nc.scalar.copy(out=ob[:], in_=pb[b][:, :]), reads=[pr[b]], writes=[rob])
                    if kind == "av": dst, rd = vaT[idx][:, cols], R["vaT"]
                    elif kind == "bv": dst, rd = vbT[:, cols], R["vbT"]
                    else: dst, rd = qiT[idx][:, cols], R["qiT"]
                    kb.dma("sp", dst, ob[:], reads=[rob], writes=[rd])
                elif kind == "ikw":
                    ob, rob = ob16.next()
                    kb.op("act", lambda: nc.scalar.copy(out=ob[0:64, :], in_=pb[b][0:64, :]), reads=[pr[b]], writes=[rob])
                    kb.dma("sp", kiT[:, cols], ob[0:64, :], reads=[rob], writes=[R["kiT"]])
                    of, rof = f32a.next()
                    kb.op("act", lambda: nc.scalar.copy(out=of[64:64 + c.IW, :], in_=pb[b][64:64 + c.IW, :]),
                          reads=[pr[b]], writes=[rof])
                    kb.dma("sp", wiT[0:c.IW, cols], of[64:64 + c.IW, :], reads=[rof], writes=[R["wiT"]])
                else:
                    G = 64 if kind in ("aq", "ak") else 128
                    red, rred = (bd64_f, r_bd64) if G == 64 else (ones_f, r_onesf)
                    gi = {"aq": 0, "ak": 1, "bq": 2, "bk": 3}[kind]
                    sq, rsq = f32a.next()
                    kb.op("act", lambda: nc.scalar.activation(out=sq[:], in_=pb[b][:, :], func=AF.Square),
                          reads=[pr[b]], writes=[rsq])
                    sbk = 6
                    kb.op("pe", lambda: nc.tensor.matmul(pb[sbk][:, :], lhsT=red[:], rhs=sq[:], start=True, stop=True),
                          reads=[rred, rsq], writes=[pr[sbk]])
                    rs, rrs = f32b.next()
                    rsqrt_op(rs[:], rrs, pb[sbk][:, :], pr[sbk], G)
                    ob, rob = ob16.next()
                    kb.op("dve", lambda: nc.vector.scalar_tensor_tensor(out=ob[:], in0=pb[b][:, :], scalar=gs2[:, gi:gi + 1],
                                                                        in1=rs[:], op0=ALU.mult, op1=ALU.mult),
                          reads=[pr[b], rrs, r_gs2], writes=[rob])
                    if kind == "aq": dst, rd = qaT[idx][:, cols], R["qaT"]
                    elif kind == "ak": dst, rd = kaT[idx][:, cols], R["kaT"]
                    elif kind == "bq": dst, rd = qbT[idx][:, cols], R["qbT"]
                    else: dst, rd = kbT[:, cols], R["kbT"]
                    kb.dma("sp", dst, ob[:], reads=[rob], writes=[rd])

        stream_groups(w_in[l], groups, KD, unit, 3)

    def load_layer_params(l):
        lam_init = 0.8 - 0.6 * math.exp(-0.3 * l)
        kb.dma("sp", gA_sb[:], gA[l], writes=[r_gA])
        kb.dma("sp", gF_sb[:], gF[l], writes=[r_gF])
        kb.dma("sp", gsm_sb[:], gsm[l], writes=[r_gsm])
        kb.dma("sp", lq_sb[:], lqb[l], writes=[r_lq])
        kb.dma("sp", cw_sb[:], cwv[l], writes=[r_cw])
        kb.dma("sp", cb_sb[:], cbv[l], writes=[r_cb])
        sD = float(math.sqrt(c.D))
        kb.op("dve", lambda: nc.vector.tensor_scalar_mul(out=gA_sb[:], in0=gA_sb[:], scalar1=sD), reads=[r_gA], writes=[r_gA])
        kb.op("dve", lambda: nc.vector.tensor_scalar_mul(out=gF_sb[:], in0=gF_sb[:], scalar1=sD), reads=[r_gF], writes=[r_gF])
        facs = [1.0, 8.0, 1.0, math.sqrt(128.0), math.sqrt(128.0) * (1.0 - lam_init)]
        for i, f in enumerate(facs):
            kb.op("dve", lambda: nc.vector.tensor_scalar_mul(out=gs2[:, i:i + 1], in0=gsm_sb[:, i:i + 1], scalar1=float(f)),
                  reads=[r_gsm], writes=[r_gs2])
        kb.op("dve", lambda: nc.vector.tensor_tensor(out=lwork[:, 0:64], in0=lq_sb[:, 0:64], in1=lq_sb[:, 64:128], op=ALU.mult),
              reads=[r_lq], writes=[r_lwork])
        kb.op("dve", lambda: nc.vector.tensor_tensor(out=lwork[:, 64:128], in0=lq_sb[:, 128:192], in1=lq_sb[:, 192:256], op=ALU.mult),
              reads=[r_lq], writes=[r_lwork])
        kb.op("dve", lambda: nc.vector.reduce_sum(out=lsm[:, 0:1], in_=lwork[:, 0:64], axis=AX.X), reads=[r_lwork], writes=[r_lsm])
        kb.op("dve", lambda: nc.vector.reduce_sum(out=lsm[:, 1:2], in_=lwork[:, 64:128], axis=AX.X), reads=[r_lwork], writes=[r_lsm])
        kb.op("act", lambda: nc.scalar.activation(out=lsm[:, 2:4], in_=lsm[:, 0:2], func=AF.Exp), reads=[r_lsm], writes=[r_lsm])
        kb.op("dve", lambda: nc.vector.tensor_tensor(out=lsm[:, 4:5], in0=lsm[:, 3:4], in1=lsm[:, 2:3], op=ALU.subtract),
              reads=[r_lsm], writes=[r_lsm])
        kb.op("dve", lambda: nc.vector.tensor_scalar_add(out=lsm[:, 5:6], in0=lsm[:, 4:5], scalar1=float(-lam_init)),
              reads=[r_lsm], writes=[r_lsm])
    nlam = lsm[:, 5:6]

    def load_bias_tiles(hglob):
        bm, rbm = bms.next()
        kb.dma("pool", bm[:], btab[hglob], writes=[rbm])
        kb.op("dve", lambda: nc.vector.tensor_tensor(out=bm[:], in0=bm[:], in1=mask_f[:], op=ALU.add),
              reads=[rbm, r_maskf], writes=[rbm])
        return bm, rbm

    def transpose_v(src_dram, rsrc):
        vt, rvt = vTs.next()
        kb.dma("sp", vt[:], src_dram, reads=[rsrc], writes=[rvt])
        vh, rvh = vhs.next()
        for g in range(0, NT, 8):
            ng = min(8, NT - g)

            def fn():
                ins = None
                for i in range(ng):
                    ins = nc.tensor.transpose(out=pb7b[:, i * 128:(i + 1) * 128],
                                              in_=vt[:, (g + i) * 128:(g + i + 1) * 128], identity=ident_b[:])
                return ins
            kb.op("pe", fn, reads=[rvt, r_identb], writes=[pr[7]])
            kb.op("act", lambda: nc.scalar.copy(out=vh[:, g:g + ng, :],
                                                in_=pb7b[:, 0:ng * 128].rearrange("p (a b) -> p a b", b=128)),
                  reads=[pr[7]], writes=[rvh])
        return vh, rvh

    sbank = [0]

    def next_sbank():
        sbank[0] ^= 1
        return sbank[0]

    def attn_block(hglob, qb, q_ap, rq, k_ap, rk, vh, rvh, bm, rbm, obank, dbank, use_mq):
        q0 = qb * 512
        nkt = 4 * qb + 4
        for kt in range(nkt):
            cs = max(0, kt * 128 - q0)
            near = kt >= 4 * qb - 1
            sbk = next_sbank()
            cls = kt - 4 * qb + 1

            def fn_s():
                ins = nc.tensor.matmul(pb[sbk][:, cs:512], lhsT=k_ap[:, kt * 128:(kt + 1) * 128],
                                       rhs=q_ap[:, q0 + cs:q0 + 512], start=True, stop=(not near and not use_mq))
                if use_mq:
                    for qi in range(cs // 128, 4):
                        last = (qi == 3) and not near
                        ins = nc.tensor.matmul(pb[sbk][:, qi * 128:(qi + 1) * 128],
                                               lhsT=mq_all[:, 4 * qb + qi, kt * 128:(kt + 1) * 128],
                                               rhs=ident_b[:], start=False, stop=last)
                if near:
                    ins = nc.tensor.matmul(pb[sbk][:, cs:512], lhsT=ident_b[:], rhs=bm[:, cls, cs:512],
                                           start=False, stop=True)
                return ins
            rd = [rq, rk, r_identb]
            if near: rd.append(rbm)
            if use_mq: rd += [r_mq[4 * qb + qi] for qi in range(cs // 128, 4)]
            kb.op("pe", fn_s, reads=rd, writes=[pr[sbk]])
            eb, reb = ebs.next()
            if near:
                kb.op("act", lambda: nc.scalar.activation(out=eb[:, cs:512], in_=pb[sbk][:, cs:512], func=AF.Exp),
                      reads=[pr[sbk]], writes=[reb])
            else:
                kb.op("act", lambda: nc.scalar.activation(out=eb[:, cs:512], in_=pb[sbk][:, cs:512], func=AF.Exp,
                                                          bias=relb_sb[:, hglob:hglob + 1]),
                      reads=[pr[sbk], r_relb], writes=[reb])

            def fn_pv():
                nc.tensor.matmul(pb[obank][:, cs:512], lhsT=vh[:, kt, :], rhs=eb[:, cs:512],
                                 start=(kt == 0), stop=(kt == nkt - 1))
                return nc.tensor.matmul(pb[dbank][:, cs:512], lhsT=ones_b[:], rhs=eb[:, cs:512],
                                        start=(kt == 0), stop=(kt == nkt - 1))
            kb.op("pe", fn_pv, reads=[rvh, reb, r_onesb], writes=[pr[obank], pr[dbank]])

    def phase_A_diff():
        for h in range(c.HA):
            qh, rqh = qhs.next(); kh, rkh = khs.next()
            kb.dma("sp", qh[:], qaT[h], reads=[R["qaT"]], writes=[rqh])
            kb.dma("sp", kh[:], kaT[h], reads=[R["kaT"]], writes=[rkh])
            bm, rbm = load_bias_tiles(h)
            vh, rvh = transpose_v(vaT[h], R["vaT"])
            for qb in range(NTB):
                attn_block(h, qb, qh[0:64, :], rqh, kh[0:64, :], rkh, vh, rvh, bm, rbm, 2, 4, False)
                attn_block(h, qb, qh[64:128, :], rqh, kh[64:128, :], rkh, vh, rvh, bm, rbm, 3, 5, False)
                r1, rr1 = f32a.next(); r2, rr2 = f32a.next()
                kb.op("dve", lambda: nc.vector.reciprocal(out=r1[:], in_=pb[4][:, :]), reads=[pr[4]], writes=[rr1])
                kb.op("dve", lambda: nc.vector.reciprocal(out=r2[:], in_=pb[5][:, :]), reads=[pr[5]], writes=[rr2])
                kb.op("dve", lambda: nc.vector.tensor_tensor(out=r1[:], in0=pb[2][:, :], in1=r1[:], op=ALU.mult),
                      reads=[pr[2], rr1], writes=[rr1])
                kb.op("dve", lambda: nc.vector.tensor_tensor(out=r2[:], in0=pb[3][:, :], in1=r2[:], op=ALU.mult),
                      reads=[pr[3], rr2], writes=[rr2])
                if h == 0 and qb == 0:
                    dbg_dump("t1", r1[:], rr1, [128, 512]); dbg_dump("t2", r2[:], rr2, [128, 512]); dbg_dump("lsm", lsm[:], r_lsm, [128, 8])
                oa, roa = f32b.next()
                kb.op("dve", lambda: nc.vector.scalar_tensor_tensor(out=oa[:], in0=r2[:], scalar=nlam, in1=r1[:],
                                                                    op0=ALU.mult, op1=ALU.add),
                      reads=[rr1, rr2, r_lsm], writes=[roa])
                if h == 0 and qb == 0:
                    dbg_dump("oa", oa[:], roa, [128, 512])
                sq, rsq = f32b.next()
                kb.op("act", lambda: nc.scalar.activation(out=sq[:], in_=oa[:], func=AF.Square), reads=[roa], writes=[rsq])
                kb.op("pe", lambda: nc.tensor.matmul(pb[6][:, :], lhsT=ones_f[:], rhs=sq[:], start=True, stop=True),
                      reads=[r_onesf, rsq], writes=[pr[6]])
                rs, rrs = f32a.next()
                rsqrt_op(rs[:], rrs, pb[6][:, :], pr[6], 128)
                ob, rob = ob16.next()
                kb.op("dve", lambda: nc.vector.scalar_tensor_tensor(out=ob[:], in0=oa[:], scalar=gs2[:, 4:5], in1=rs[:],
                                                                    op0=ALU.mult, op1=ALU.mult),
                      reads=[roa, rrs, r_gs2], writes=[rob])
                kb.dma("sp", mixT[h][:, qb * 512:(qb + 1) * 512], ob[:], reads=[rob], writes=[R["mixT"]])

    def phase_A_index():
        kb.dma("sp", ki_sb[:], kiT, reads=[R["kiT"]], writes=[r_kisb])
        for tb in range(NTB):
            wt_, rwt_ = f32a.next()
            kb.dma("sp", wt_[0:c.IW, :], wiT[0:c.IW, tb * 512:(tb + 1) * 512], reads=[R["wiT"]], writes=[rwt_])
            for t4 in range(4):
                t = tb * 4 + t4
                kb.op("pe", lambda: nc.tensor.transpose(out=pb[6][:, 0:c.IW], in_=wt_[0:c.IW, t4 * 128:(t4 + 1) * 128],
                                                        identity=ident_f[0:c.IW, 0:c.IW]),
                      reads=[rwt_, r_identf], writes=[pr[6]])
                kb.op("act", lambda: nc.scalar.copy(out=wi_tok[:, t, 0:c.IW], in_=pb[6][:, 0:c.IW]),
                      reads=[pr[6]], writes=[r_witok])
        qiv = qiT.rearrange("b (two p) t -> p (b two) t", two=2)
        sc = sc_t
        for qt in range(NT):
            Lk = (qt + 1) * 128
            qi_t, rqi = qis.next()
            kb.dma("sp", qi_t[:], qiv[:, :, qt * 128:(qt + 1) * 128], reads=[R["qiT"]], writes=[rqi])
            for kbk in range((Lk + 511) // 512):
                n = min(512, Lk - kbk * 512)
                for hi in range(c.HI):
                    b = next_bank(6)
                    kb.op("pe", lambda: nc.tensor.matmul(pb[b][:, :n], lhsT=qi_t[:, hi, :], rhs=ki_sb[:, kbk * 512:kbk * 512 + n],
                                                         start=True, stop=True),
                          reads=[rqi, r_kisb], writes=[pr[b]])
                    rl, rrl = f32b.next()
                    kb.op("act", lambda: nc.scalar.activation(out=rl[:, :n], in_=pb[b][:, :n], func=AF.Relu),
                          reads=[pr[b]], writes=[rrl])
                    dst = sc[:, kbk * 512:kbk * 512 + n]
                    if hi == 0:
                        kb.op("dve", lambda: nc.vector.tensor_scalar_mul(out=dst, in0=rl[:, :n], scalar1=wi_tok[:, qt, 0:1]),
                              reads=[rrl, r_witok], writes=[rsc])
                    else:
                        kb.op("dve", lambda: nc.vector.scalar_tensor_tensor(out=dst, in0=rl[:, :n], scalar=wi_tok[:, qt, hi:hi + 1],
                                                                            in1=dst, op0=ALU.mult, op1=ALU.add),
                              reads=[rrl, r_witok, rsc], writes=[rsc])
            kb.op("dve", lambda: nc.vector.memset(sc[0:64, Lk - 64:Lk], NINF), reads=[rsc], writes=[rsc])
            if qt == 2:
                dbg_dump("sc2", sc[:, :Lk], rsc, [128, Lk])
            if Lk > c.TOPK:
                m8 = None
                nr = c.TOPK // 8
                for r in range(nr):
                    src = sc if r == 0 else work
                    rsrc = rsc if r == 0 else r_work
                    m8, rm8 = m8s.next()
                    kb.op("dve", lambda: nc.vector.max(out=m8[:], in_=src[:, :Lk]), reads=[rsrc], writes=[rm8], strict=True)
                    m8c, rm8c = m8cs.next()
                    kb.op("act", lambda: nc.scalar.copy(out=m8c[:], in_=m8[:]), reads=[rm8], writes=[rm8c])
                    if r < nr - 1:
                        kb.op("dve", lambda: nc.vector.match_replace(out=work[:, :Lk], in_to_replace=m8c[:], in_values=src[:, :Lk],
                                                                     imm_value=NINF),
                              reads=[rsrc, rm8c], writes=[r_work], strict=True)
                thr_ap, rthr = m8c[:, 7:8], rm8c
            else:
                thr_ap, rthr = thr_c[:, 0:1], r_thrc
            kb.op("dve", lambda: nc.vector.tensor_scalar(out=mq_all[:, qt, :Lk], in0=sc[:, :Lk], scalar1=thr_ap, scalar2=NEG,
                                                         op0=ALU.is_lt, op1=ALU.mult),
                  reads=[rsc, rthr], writes=[r_mq[qt]], strict=True)
            if qt == 2:
                dbg_dump("m8", m8c[:], rm8c, [128, 8])
                dbg_dump("mq2", mq_all[:, qt, :Lk], r_mq[qt], [128, Lk], BF16)

    def phase_A_dsa():
        kb.dma("sp", kb_sb[:], kbT, reads=[R["kbT"]], writes=[r_kbsb])
        vh, rvh = transpose_v(vbT, R["vbT"])
        for h in range(c.HB):
            qh, rqh = qhs.next()
            kb.dma("sp", qh[:], qbT[h], reads=[R["qbT"]], writes=[rqh])
            bm, rbm = load_bias_tiles(c.HA + h)
            for qb in range(NTB):
                ob_, db_ = (2, 4) if (qb % 2 == 0) else (3, 5)
                attn_block(c.HA + h, qb, qh, rqh, kb_sb, r_kbsb, vh, rvh, bm, rbm, ob_, db_, True)
                r1, rr1 = f32a.next()
                kb.op("dve", lambda: nc.vector.reciprocal(out=r1[:], in_=pb[db_][:, :]), reads=[pr[db_]], writes=[rr1])
                ob, rob = ob16.next()
                kb.op("dve", lambda: nc.vector.tensor_tensor(out=ob[:], in0=pb[ob_][:, :], in1=r1[:], op=ALU.mult),
                      reads=[pr[ob_], rr1], writes=[rob])
                kb.dma("sp", mixT[c.HA + h][:, qb * 512:(qb + 1) * 512], ob[:], reads=[rob], writes=[R["mixT"]])

    def residual_epilogue(b, src, rsrc, dst, rdst, cb, t0, accum):
        cols = slice(t0, t0 + 512)
        xin, rxin = xs5.next()
        kb.dma("sp", xin[:], src[cb][:, cols], reads=[rsrc], writes=[rxin])
        xn, rxn = f32a.next()
        kb.op("dve", lambda: nc.vector.tensor_tensor(out=xn[:], in0=pb[b][:, :], in1=xin[:], op=ALU.add),
              reads=[pr[b], rxin], writes=[rxn])
        kb.dma("sp", dst[cb][:, cols], xn[:], reads=[rxn], writes=[rdst])
        if accum:
            ssq_accum(xn[:], rxn, t0, 512)

    def phase_O(l, ps, src, rsrc):
        for k in range(KM):
            kb.dma("sp", hT[:, k, :], mixT[k][:, ps * TG:(ps + 1) * TG], reads=[R["mixT"]], writes=[hres[k]])
        groups = [(cb, [(cb * 128, 128)]) for cb in range(KD)]

        def unit(cb, ws):
            wt, rw, M = ws[0]
            for tb in range(NTBG):
                b = next_bank(6)
                gemm_unit(b, wt, rw, M, KM, tb)
                residual_epilogue(b, src, rsrc, xa, R["xa"], cb, ps * TG + tb * 512, True)
        stream_groups(w_out[l], groups, KM, unit, 3)

    def phase_F1(l, ps):
        groups = [(fc, [(fc * 128, 128), (c.DFF + fc * 128, 128)]) for fc in range(NFC)]

        def unit(fc, ws):
            (wg, rwg, _), (wu, rwu, _) = ws
            prev = None
            for tb in range(NTBG):
                bg = next_bank(6)
                gemm_unit(bg, wg, rwg, 128, KD, tb)
                bu = next_bank(6)
                gemm_unit(bu, wu, rwu, 128, KD, tb)
                gs, rgs = gsl.next()
                kb.op("act", lambda: nc.scalar.copy(out=gs[:, 2:514], in_=pb[bg][:, :]), reads=[pr[bg]], writes=[rgs])
                if tb == 0:
                    if ps == 0:
                        kb.op("dve", lambda: nc.vector.memset(gs[:, 0:2], 0.0), reads=[rgs], writes=[rgs])
                    else:
                        kb.op("dve", lambda: nc.vector.tensor_copy(out=gs[:, 0:2], in_=gcar[:, fc, :]), reads=[r_gcar, rgs], writes=[rgs])
                else:
                    pg, rpg = prev
                    kb.op("dve", lambda: nc.vector.tensor_copy(out=gs[:, 0:2], in_=pg[:, 512:514]), reads=[rpg, rgs], writes=[rgs])
                if tb == NTBG - 1 and ps < NPASS - 1:
                    kb.op("dve", lambda: nc.vector.tensor_copy(out=gcar[:, fc, :], in_=gs[:, 512:514]), reads=[rgs], writes=[r_gcar])
                prev = (gs, rgs)
                a, ra = f32a.next()
                kb.op("dve", lambda: nc.vector.tensor_scalar(out=a[:], in0=gs[:, 2:514], scalar1=cw_sb[:, fc, 2:3],
                                                             scalar2=cb_sb[:, fc:fc + 1], op0=ALU.mult, op1=ALU.add),
                      reads=[rgs, r_cw, r_cb], writes=[ra])
                kb.op("dve", lambda: nc.vector.scalar_tensor_tensor(out=a[:], in0=gs[:, 1:513], scalar=cw_sb[:, fc, 1:2], in1=a[:],
                                                                    op0=ALU.mult, op1=ALU.add),
                      reads=[rgs, r_cw, ra], writes=[ra])
                kb.op("dve", lambda: nc.vector.scalar_tensor_tensor(out=a[:], in0=gs[:, 0:512], scalar=cw_sb[:, fc, 0:1], in1=a[:],
                                                                    op0=ALU.mult, op1=ALU.add),
                      reads=[rgs, r_cw, ra], writes=[ra])
                s, rs_ = f32b.next()
                kb.op("act", lambda: nc.scalar.activation(out=s[:], in_=a[:], func=AF.Silu), reads=[ra], writes=[rs_])
                ob, rob = ob16.next()
                kb.op("dve", lambda: nc.vector.tensor_tensor(out=ob[:], in0=pb[bu][:, :], in1=s[:], op=ALU.mult),
                      reads=[pr[bu], rs_], writes=[rob])
                t0 = ps * TG + tb * 512
                kb.dma("sp", actT[fc][:, t0:t0 + 512], ob[:], reads=[rob], writes=[R["actT"]])
        stream_groups(w_gu[l], groups, KD, unit, 2)

    def phase_F2(l, dst, rdst, accum):
        if accum:
            ssq_zero()
        wl = w_dn[l]
        npass = KD // CC
        for cp in range(npass):
            c0 = cp * CC * 128
            wd = None
            for fc in range(NFC):
                if fc % GF == 0:
                    ng = min(GF, NFC - fc)
                    wd, rwd = wds.next()
                    kb.dma("pool", wd[:, :ng, :], wl[fc * 128:(fc + ng) * 128, c0:c0 + CC * 128].rearrange("(f p) c -> p f c", p=128),
                           writes=[rwd])
                at, rat = asl.next()
                kb.dma("sp", at[:], actT[fc], reads=[R["actT"]], writes=[rat])

                def fn():
                    ins = None
                    for cc in range(CC):
                        for tb in range(NTB):
                            ins = nc.tensor.matmul(pb[cc * NTB + tb][:, :], lhsT=wd[:, fc % GF, cc * 128:(cc + 1) * 128],
                                                   rhs=at[:, tb * 512:(tb + 1) * 512], start=(fc == 0), stop=(fc == NFC - 1))
                    return ins
                kb.op("pe", fn, reads=[rwd, rat], writes=[pr[i] for i in range(CC * NTB)])
            for cc in range(CC):
                for tb in range(NTB):
                    residual_epilogue(cc * NTB + tb, xa, R["xa"], dst, rdst, cp * CC + cc, tb * 512, accum)

    cur, rcur = xT, R["xT"]
    for l in range(L):
        load_layer_params(l)
        if l == 0:
            phase_S(cur, rcur)
        finish_norm(c.D)
        for ps in range(NPASS):
            phase_N(ps, cur, rcur, gA_sb, r_gA)
            phase_P(l, ps)
        barrier()
        phase_A_diff()
        phase_A_index()
        phase_A_dsa()
        barrier()
        ssq_zero()
        for ps in range(NPASS):
            phase_O(l, ps, cur, rcur)
        finish_norm(c.D)
        for ps in range(NPASS):
            phase_N(ps, xa, R["xa"], gF_sb, r_gF)
            phase_F1(l, ps)
        last = (l == L - 1)
        nxt, rnxt = (yT, R["yT"]) if last else (xb, R["xb"])
        phase_F2(l, nxt, rnxt, not last)
        cur, rcur = nxt, rnxt
    kb.final_wait("sp", R["yT"])
    return nc, kb


def _rel_bucket_np(rel):
    import jax
    import jax.numpy as jnp
    cpu = jax.devices("cpu")[0]
    with jax.default_device(cpu):
        rel = jnp.asarray(rel, dtype=jnp.int32)
        nb = 32 // 2
        max_exact = nb // 2
        bucket = jnp.where(rel > 0, nb, 0)
        n = jnp.abs(rel)
        nf = jnp.maximum(n, 1).astype(jnp.float32)
        large = max_exact + (jnp.log(nf / max_exact) / math.log(128 / max_exact) * (nb - max_exact)).astype(jnp.int32)
        large = jnp.minimum(large, nb - 1)
        out = bucket + jnp.where(n < max_exact, n, large)
        return np.asarray(out)


def prep_shared(cfg, inputs):
    c = cfg
    L = c.DEPTH
    f = lambda a: np.ascontiguousarray(np.asarray(a, dtype=np.float32))
    col = lambda v, k: np.ascontiguousarray(v.reshape(k, 128).T)
    sh = {}
    sh["w_in"] = f(inputs["w_in"]); sh["w_out"] = f(inputs["w_out"])
    sh["w_gu"] = f(inputs["w_gate_up"]); sh["w_dn"] = f(inputs["w_down"])
    an = f(inputs["attn_norm"]); fn = f(inputs["ffn_norm"])
    sh["gA"] = np.stack([col(an[l], c.KD) for l in range(L)])
    sh["gF"] = np.stack([col(fn[l], c.KD) for l in range(L)])
    gsm = np.zeros((L, 128, 8), np.float32)
    aq = f(inputs["a_q_norm"]); ak = f(inputs["a_k_norm"]); ao = f(inputs["a_out_norm"])
    bq = f(inputs["b_q_norm"]); bk = f(inputs["b_k_norm"])
    for l in range(L):
        gsm[l, :, 0] = np.concatenate([aq[l], aq[l]])
        gsm[l, :, 1] = np.concatenate([ak[l], ak[l]])
        gsm[l, :, 2] = bq[l]; gsm[l, :, 3] = bk[l]; gsm[l, :, 4] = ao[l]
    sh["gsm"] = gsm
    lq = f(inputs["lambda_qk"]).reshape(L, 1, 256)
    sh["lqb"] = np.ascontiguousarray(np.broadcast_to(lq, (L, 128, 256)))
    cw = f(inputs["conv_w"]); cb = f(inputs["conv_b"])
    sh["cwv"] = np.ascontiguousarray(cw.reshape(L, 3, c.NFC, 128).transpose(0, 3, 2, 1))
    sh["cbv"] = np.ascontiguousarray(cb.reshape(L, c.NFC, 128).transpose(0, 2, 1))
    rb = f(inputs["rel_bias"])
    sh["relb15"] = np.ascontiguousarray(np.broadcast_to(rb[15][None, :], (128, c.NH)))
    p = np.arange(128)[:, None, None]
    dl = (np.arange(5) * 128 - 128)[None, :, None]
    j = np.arange(512)[None, None, :]
    d = dl + p - j
    bidx = _rel_bucket_np(d)
    sh["btab"] = np.ascontiguousarray(rb[bidx].transpose(3, 0, 1, 2))
    vis = (dl // 64 + p // 64) <= (j // 64)
    sh["maskd"] = np.where(vis, 0.0, NEG).astype(np.float32)
    sh["identd"] = np.eye(128, dtype=np.float32)
    return sh


def run(cfg, inputs, debug=False):
    c = cfg
    sh = prep_shared(c, inputs)
    x = np.asarray(inputs["x"], dtype=np.float32)
    B = x.shape[0]
    in_maps = []
    for b in range(B):
        m = dict(sh)
        m["xT"] = np.ascontiguousarray(x[b].T.reshape(c.KD, 128, c.T))
        in_maps.append(m)
    nc, kb = build_program(c, debug=debug)
    res = run_bass_kernel_spmd(nc, in_maps, core_ids=list(range(B)))
    out = np.empty_like(x)
    for b in range(B):
        out[b] = res.results[b]["yT"].reshape(c.D, c.T).T
    if debug:
        return out, res
    return out


def kernel(**inputs):
    return run(FULL, inputs)
```

```python
import math
import numpy as np
import concourse.bass as bass
import concourse.mybir as mybir
from concourse.bass_utils import run_bass_kernel_spmd

F32 = mybir.dt.float32
BF16 = mybir.dt.bfloat16
AF = mybir.ActivationFunctionType
ALU = mybir.AluOpType
AX = mybir.AxisListType

EPS = 1e-6
NEG = -30000.0
NINF = -1.0e30


class Cfg:
    def __init__(self, D=4096, S=2048, HA=16, HB=16, HI=16, DFF=11008, DEPTH=2, BATCH=4, TOPK_MAX=256, TG=1024):
        self.D, self.S, self.HA, self.HB, self.HI, self.DFF, self.DEPTH, self.BATCH = D, S, HA, HB, HI, DFF, DEPTH, BATCH
        self.T = S
        self.TG = TG
        self.NPASS = S // TG
        self.NTBG = TG // 512
        self.KD = D // 128
        self.NTB = S // 512
        self.NT = S // 128
        self.AQ = HA * 128; self.AK = HA * 128; self.AV = HA * 128
        self.BQ = HB * 128; self.BK = 128; self.BV = 128
        self.IQ = HI * 64; self.IK = 64; self.IW = HI
        self.DIN = self.AQ + self.AK + self.AV + self.BQ + self.BK + self.BV + self.IQ + self.IK + self.IW
        self.DMIX = HA * 128 + HB * 128
        self.KM = self.DMIX // 128
        self.NFC = DFF // 128
        self.NH = HA + HB
        self.TOPK = min(TOPK_MAX, S // 4)
        self.KMAX = max(self.KD, self.KM)
        assert DFF % 128 == 0 and D % 128 == 0 and S % 512 == 0 and HI % 2 == 0
        assert self.IK + self.IW <= 128


FULL = Cfg()


class Res:
    __slots__ = ("name", "w", "r", "dsem", "dval", "multi")

    def __init__(self, name, multi=False):
        self.name = name
        self.w = None
        self.r = {}
        self.dsem = None
        self.dval = 0
        self.multi = multi


class KB:
    def __init__(self, nc):
        self.nc = nc
        self.engs = {"pe": nc.tensor, "act": nc.scalar, "dve": nc.vector, "pool": nc.gpsimd, "sp": nc.sync}
        self.esem = {}
        self.ecnt = {}
        for n in ("pe", "act", "dve", "pool"):
            self.esem[n] = nc.semaphore("es_" + n).__enter__()
            self.ecnt[n] = 0
        self.waited = {n: {} for n in self.engs}
        self.nsem = 4
        self.dres = []

    def sb(self, name, shape, dt):
        t = self.nc.sbuf_tensor(name, list(shape), dt).__enter__()
        return t

    def _wait(self, eng, ev, is_dma):
        sem, val, owner = ev
        if owner == eng and not is_dma:
            return
        key = id(sem)
        if self.waited[eng].get(key, 0) >= val:
            return
        self.engs[eng].wait_ge(sem, val)
        self.waited[eng][key] = val

    def _deps(self, eng, reads, writes, is_dma):
        for r in reads:
            if r.w is not None:
                self._wait(eng, r.w, is_dma)
        for w in writes:
            if w.w is not None and not w.multi:
                self._wait(eng, w.w, is_dma)
            for ev in list(w.r.values()):
                self._wait(eng, ev, is_dma)

    def op(self, eng, fn, reads=(), writes=(), strict=False):
        self._deps(eng, reads, writes, strict or eng in ("dve", "act"))
        ins = fn()
        self.ecnt[eng] += 1
        sem = self.esem[eng]
        ins.then_inc(sem, 1)
        ev = (sem, self.ecnt[eng], eng)
        for r in reads:
            r.r[id(sem)] = ev
        for w in writes:
            w.w = ev
            if not w.multi:
                w.r = {}

    def dma(self, q, out, in_, reads=(), writes=()):
        self._deps(q, reads, writes, True)
        w0 = writes[0]
        if w0.dsem is None:
            w0.dsem = self.nc.semaphore("ds_" + w0.name).__enter__()
            self.nsem += 1
            self.dres.append(w0)
        ins = self.engs[q].dma_start(out=out, in_=in_)
        ins.then_inc(w0.dsem, 16)
        w0.dval += 16
        ev = (w0.dsem, w0.dval, None)
        for r in reads:
            r.r[id(w0.dsem)] = ev
        for w in writes:
            w.w = ev
            if not w.multi:
                w.r = {}

    def final_wait(self, eng, res):
        if res.w is not None:
            self._wait(eng, res.w, True)


class Slots:
    def __init__(self, name, views):
        self.t = list(views)
        self.r = [Res(f"{name}{i}") for i in range(len(views))]
        self.i = -1
        self.n = len(views)

    def next(self):
        self.i = (self.i + 1) % self.n
        return self.t[self.i], self.r[self.i]


class Arena:
    def __init__(self, kb, name, nbytes):
        self.n2 = (nbytes + 1) // 2
        self.t = kb.sb(name, [128, self.n2], BF16)
        self.off = 0

    def reset(self):
        self.off = 0

    def alloc(self, shape, dt, parts=128):
        esz = 4 if dt == F32 else 2
        n = 1
        for s in shape[1:]:
            n *= s
        nb = n * esz
        nb = (nb + 63) // 64 * 64
        o2 = self.off // 2
        assert o2 + nb // 2 <= self.n2, ("arena overflow", self.off, nb, self.n2 * 2)
        v = self.t[0:shape[0], o2:o2 + (n * esz) // 2]
        if dt == F32:
            v = v.bitcast(F32)
        if len(shape) == 3:
            v = v.rearrange("p (a b) -> p a b", b=shape[2])
        self.off += nb
        return v


def build_program(cfg, debug=False):
    c = cfg
    nc = bass.Bass("TRN2", target_bir_lowering=False)
    kb = KB(nc)
    T, KD, KM, NTB, NT, NFC = c.T, c.KD, c.KM, c.NTB, c.NT, c.NFC
    TG, NPASS, NTBG = c.TG, c.NPASS, c.NTBG
    L = c.DEPTH

    def din(name, shape, dt=F32):
        return nc.dram_tensor(name, list(shape), dt, kind="ExternalInput").ap()

    def dint(name, shape, dt):
        kind = "ExternalOutput" if debug else "Internal"
        return nc.dram_tensor(name, list(shape), dt, kind=kind).ap()

    xT = din("xT", [KD, 128, T])
    w_in = din("w_in", [L, c.D, c.DIN])
    w_out = din("w_out", [L, c.DMIX, c.D])
    w_gu = din("w_gu", [L, c.D, 2 * c.DFF])
    w_dn = din("w_dn", [L, c.DFF, c.D])
    gA = din("gA", [L, 128, KD])
    gF = din("gF", [L, 128, KD])
    gsm = din("gsm", [L, 128, 8])
    lqb = din("lqb", [L, 128, 256])
    cwv = din("cwv", [L, 128, NFC, 3])
    cbv = din("cbv", [L, 128, NFC])
    relb15 = din("relb15", [128, c.NH])
    btab = din("btab", [c.NH, 128, 5, 512])
    identd = din("identd", [128, 128])
    maskd = din("maskd", [128, 5, 512])
    yT = nc.dram_tensor("yT", [KD, 128, T], F32, kind="ExternalOutput").ap()

    xa = dint("xa", [KD, 128, T], F32)
    xb = dint("xb", [KD, 128, T], F32)
    qaT = dint("qaT", [c.HA, 128, T], BF16)
    kaT = dint("kaT", [c.HA, 128, T], BF16)
    vaT = dint("vaT", [c.HA, 128, T], BF16)
    qbT = dint("qbT", [c.HB, 128, T], BF16)
    kbT = dint("kbT", [128, T], BF16)
    vbT = dint("vbT", [128, T], BF16)
    qiT = dint("qiT", [c.HI // 2, 128, T], BF16)
    kiT = dint("kiT", [64, T], BF16)
    wiT = dint("wiT", [c.IW, T], F32)
    mixT = dint("mixT", [KM, 128, T], BF16)
    actT = dint("actT", [NFC, 128, T], BF16)
    R = {n: Res(n, multi=True) for n in ("xT", "xa", "xb", "yT", "qaT", "kaT", "vaT", "qbT", "kbT", "vbT",
                                          "qiT", "kiT", "wiT", "mixT", "actT")}

    pb = [nc.psum_tensor(f"pb{i}", [128, 512], F32).__enter__() for i in range(8)]
    pr = [Res(f"pb{i}") for i in range(8)]
    pb7b = pb[7][:].bitcast(BF16)

    hbytes = max(c.KMAX * TG * 2, NT * T * 2)
    big = Arena(kb, "big", hbytes)
    hT = big.alloc([128, c.KMAX, TG], BF16); big.reset()
    mq_all = big.alloc([128, NT, T], BF16); big.reset()
    hres = [Res(f"hT{k}") for k in range(c.KMAX)]
    r_mq = [Res(f"mq{i}") for i in range(NT)]
    ident_f = kb.sb("ident_f", [128, 128], F32); r_identf = Res("ident_f")
    ident_b = kb.sb("ident_b", [128, 128], BF16); r_identb = Res("ident_b")
    ones_f = kb.sb("ones_f", [128, 128], F32); r_onesf = Res("ones_f")
    ones_b = kb.sb("ones_b", [128, 128], BF16); r_onesb = Res("ones_b")
    bd64_f = kb.sb("bd64_f", [128, 128], F32); r_bd64 = Res("bd64_f")
    mask_f = kb.sb("mask_f", [128, 5, 512], F32); r_maskf = Res("mask_f")
    relb_sb = kb.sb("relb_sb", [128, c.NH], F32); r_relb = Res("relb")
    thr_c = kb.sb("thr_c", [128, 1], F32); r_thrc = Res("thr_c")
    epsb = kb.sb("epsb", [128, 4], F32); r_epsb = Res("epsb")
    ssq_part = kb.sb("ssq_part", [128, T], F32); r_ssq = Res("ssq_part")
    rstd = kb.sb("rstd", [128, T], F32); r_rstd = Res("rstd")
    gA_sb = kb.sb("gA_sb", [128, KD], F32); r_gA = Res("gA")
    gF_sb = kb.sb("gF_sb", [128, KD], F32); r_gF = Res("gF")
    gsm_sb = kb.sb("gsm_sb", [128, 8], F32); r_gsm = Res("gsm")
    gs2 = kb.sb("gs2", [128, 8], F32); r_gs2 = Res("gs2")
    lq_sb = kb.sb("lq_sb", [128, 256], F32); r_lq = Res("lq")
    lwork = kb.sb("lwork", [128, 128], F32); r_lwork = Res("lwork")
    lsm = kb.sb("lsm", [128, 8], F32); r_lsm = Res("lsm")
    cw_sb = kb.sb("cw_sb", [128, NFC, 3], F32); r_cw = Res("cw")
    cb_sb = kb.sb("cb_sb", [128, NFC], F32); r_cb = Res("cb")
    gcar = kb.sb("gcar", [128, NFC, 2], F32); r_gcar = Res("gcar")
    f32a = Slots("f32a", [kb.sb(f"f32a{i}", [128, 512], F32) for i in range(4)])
    f32b = Slots("f32b", [kb.sb(f"f32b{i}", [128, 512], F32) for i in range(4)])
    ob16 = Slots("ob16", [kb.sb(f"ob16{i}", [128, 512], BF16) for i in range(4)])
    m8s = Slots("m8s", [kb.sb(f"m8s{i}", [128, 8], F32) for i in range(2)])
    m8cs = Slots("m8cs", [kb.sb(f"m8cs{i}", [128, 8], F32) for i in range(2)])

    GF = 8
    NTBF = min(2, NTB)
    CCF = min(KD, 8 // NTBF)
    TF = NTBF * 512
    AG = 2
    g_bytes = 4 * c.KMAX * 256 + 3 * GF * CCF * 256 + 4 * AG * TF * 2 + 3 * 2112 + 3 * 2048 + 1024
    a_bytes = 8 * T * 2 + 2 * 5120 + 5 * 1024 + 2 * T * 2 + 1024 + 2 * c.HI * 256 + 2 * T * 4 + 2048
    ar = Arena(kb, "arena", max(g_bytes, a_bytes))
    wsl = Slots("wsl", [ar.alloc([128, c.KMAX, 128], BF16) for _ in range(4)])
    wds = Slots("wds", [ar.alloc([128, GF, CCF * 128], BF16) for _ in range(3)])
    asl = Slots("asl", [ar.alloc([128, AG, TF], BF16) for _ in range(4)])
    gsl = Slots("gsl", [ar.alloc([128, 514], F32) for _ in range(3)])
    xs5 = Slots("xs5", [ar.alloc([128, 512], F32) for _ in range(3)])
    ar.reset()
    qhs = Slots("qhs", [ar.alloc([128, T], BF16) for _ in range(2)])
    khs = Slots("khs", [ar.alloc([128, T], BF16) for _ in range(2)])
    vTs = Slots("vTs", [ar.alloc([128, T], BF16) for _ in range(2)])
    vhs = Slots("vhs", [ar.alloc([128, NT, 128], BF16) for _ in range(2)])
    bms = Slots("bms", [ar.alloc([128, 5, 512], BF16) for _ in range(2)])
    ebs = Slots("ebs", [ar.alloc([128, 512], BF16) for _ in range(5)])
    kb_sb = ar.alloc([128, T], BF16); r_kbsb = Res("kb_sb")
    ki_sb = ar.alloc([64, T], BF16); r_kisb = Res("ki_sb")
    wi_tok = ar.alloc([128, NT, 16], F32); r_witok = Res("wi_tok")
    qis = Slots("qis", [ar.alloc([64, c.HI, 128], BF16) for _ in range(2)])
    sc_t = ar.alloc([128, T], F32); rsc = Res("sc")
    work = ar.alloc([128, T], F32); r_work = Res("work")

    def barrier():
        for e in ("pe", "act", "dve", "pool", "sp"):
            for n in ("pe", "act", "dve", "pool"):
                if kb.ecnt[n] > 0:
                    kb._wait(e, (kb.esem[n], kb.ecnt[n], None), True)
            for r in kb.dres:
                kb._wait(e, (r.dsem, r.dval, None), True)

    kb.dma("sp", ident_f[:], identd, writes=[r_identf])
    kb.dma("sp", mask_f[:], maskd, writes=[r_maskf])
    kb.dma("sp", relb_sb[:], relb15, writes=[r_relb])
    kb.op("dve", lambda: nc.vector.tensor_copy(out=ident_b[:], in_=ident_f[:]), reads=[r_identf], writes=[r_identb])
    kb.op("dve", lambda: nc.vector.memset(ones_f[:], 1.0), writes=[r_onesf])
    kb.op("dve", lambda: nc.vector.memset(ones_b[:], 1.0), writes=[r_onesb])
    kb.op("dve", lambda: nc.vector.memset(bd64_f[:], 0.0), writes=[r_bd64])
    kb.op("dve", lambda: nc.vector.memset(bd64_f[0:64, 0:64], 1.0), writes=[r_bd64])
    kb.op("dve", lambda: nc.vector.memset(bd64_f[64:128, 64:128], 1.0), writes=[r_bd64])
    kb.op("dve", lambda: nc.vector.memset(thr_c[:], -1.0e29), writes=[r_thrc])
    kb.op("dve", lambda: nc.vector.memset(epsb[:, 1:2], float(64 * EPS)), writes=[r_epsb])
    kb.op("dve", lambda: nc.vector.memset(epsb[:, 2:3], float(128 * EPS)), writes=[r_epsb])
    kb.op("dve", lambda: nc.vector.memset(epsb[:, 3:4], float(c.D * EPS)), writes=[r_epsb])

    dbg_n = [0]
    dbg_seen = set()

    def dbg_dump(name, ap, res, shape, dt=F32):
        if not debug or name in dbg_seen:
            return
        dbg_seen.add(name)
        d = nc.dram_tensor("dbg_" + name, list(shape), dt, kind="ExternalOutput").ap()
        kb.dma("sp", d, ap, reads=[res], writes=[Res("dbg_" + name)])

    bank_rot = [0]

    def next_bank(n=6):
        b = bank_rot[0] % n
        bank_rot[0] += 1
        return b


    def rsqrt_op(out_ap, out_res, in_ap, in_res, G):
        kb.op("act", lambda: nc.scalar.activation(out=out_ap, in_=in_ap, func=AF.Sqrt, bias=epsb[:, {0: 0, 64: 1, 128: 2}.get(G, 0):{0: 0, 64: 1, 128: 2}.get(G, 0) + 1] if G in (64, 128) else epsb[:, 3:4], scale=1.0),
              reads=[in_res, r_epsb], writes=[out_res])
        kb.op("dve", lambda: nc.vector.reciprocal(out=out_ap, in_=out_ap), reads=[out_res], writes=[out_res])

    def ssq_zero():
        kb.op("dve", lambda: nc.vector.memset(ssq_part[:], 0.0), writes=[r_ssq])

    def ssq_accum(src_ap, src_res, lo, n):
        sq, rsq = f32b.next()
        kb.op("act", lambda: nc.scalar.activation(out=sq[:, :n], in_=src_ap, func=AF.Square),
              reads=[src_res], writes=[rsq])
        kb.op("dve", lambda: nc.vector.tensor_tensor(out=ssq_part[:, lo:lo + n], in0=ssq_part[:, lo:lo + n],
                                                     in1=sq[:, :n], op=ALU.add),
              reads=[rsq, r_ssq], writes=[r_ssq])

    def finish_norm(G):
        for tb in range(NTB):
            b = 6
            kb.op("pe", lambda: nc.tensor.matmul(pb[b][:, :], lhsT=ones_f[:], rhs=ssq_part[:, tb * 512:(tb + 1) * 512],
                                                 start=True, stop=True),
                  reads=[r_onesf, r_ssq], writes=[pr[b]])
            rsqrt_op(rstd[:, tb * 512:(tb + 1) * 512], r_rstd, pb[b][:, :], pr[b], G)

    def phase_S(src, rsrc):
        ssq_zero()
        for k in range(KD):
            for tb in range(NTB):
                xk, rxk = xs5.next()
                kb.dma("sp", xk[:], src[k][:, tb * 512:(tb + 1) * 512], reads=[rsrc], writes=[rxk])
                ssq_accum(xk[:], rxk, tb * 512, 512)

    def phase_N(ps, src, rsrc, gcol, rg):
        for k in range(KD):
            for tb in range(NTBG):
                t0 = ps * TG + tb * 512
                xk, rxk = xs5.next()
                kb.dma("sp", xk[:], src[k][:, t0:t0 + 512], reads=[rsrc], writes=[rxk])
                kb.op("dve", lambda: nc.vector.scalar_tensor_tensor(out=hT[:, k, tb * 512:(tb + 1) * 512], in0=xk[:],
                                                                    scalar=gcol[:, k:k + 1], in1=rstd[:, t0:t0 + 512],
                                                                    op0=ALU.mult, op1=ALU.mult),
                      reads=[rxk, rg, r_rstd], writes=[hres[k]])

    def load_w(wdram, c0, M, nk):
        wt, rw = wsl.next()
        kb.dma("pool", wt[:, :nk, :M], wdram[:, c0:c0 + M].rearrange("(k p) c -> p k c", p=128), writes=[rw])
        return wt, rw

    def gemm_unit(bank, wt, rw, M, nk, tb):
        def fn():
            ins = None
            for k in range(nk):
                ins = nc.tensor.matmul(pb[bank][:M, :], lhsT=wt[:, k, :M], rhs=hT[:, k, tb * 512:(tb + 1) * 512],
                                       start=(k == 0), stop=(k == nk - 1))
            return ins
        kb.op("pe", fn, reads=[rw] + hres[:nk], writes=[pr[bank]])

    def stream_groups(wdram, groups, nk, unit_fn, inflight):
        loaded = []
        n = len(groups)

        def ld(i):
            loaded.append([load_w(wdram, c0, M, nk) + (M,) for (c0, M) in groups[i][1]])
        for i in range(min(inflight, n)):
            ld(i)
        for i in range(n):
            unit_fn(groups[i][0], loaded[i])
            if i + inflight < n:
                ld(i + inflight)

    def phase_P(l, ps):
        groups = []
        o = 0
        for h in range(c.HA): groups.append((("aq", h), [(o + h * 128, 128)]))
        o += c.AQ
        for h in range(c.HA): groups.append((("ak", h), [(o + h * 128, 128)]))
        o += c.AK
        for h in range(c.HA): groups.append((("av", h), [(o + h * 128, 128)]))
        o += c.AV
        for h in range(c.HB): groups.append((("bq", h), [(o + h * 128, 128)]))
        o += c.BQ
        groups.append((("bk", 0), [(o, 128)])); o += 128
        groups.append((("bv", 0), [(o, 128)])); o += 128
        for j in range(c.HI // 2): groups.append((("iq", j), [(o + j * 128, 128)]))
        o += c.IQ
        groups.append((("ikw", 0), [(o, c.IK + c.IW)]))

        def unit(tag, ws):
            kind, idx = tag
            wt, rw, M = ws[0]
            for tb in range(NTBG):
                b = next_bank(6)
                gemm_unit(b, wt, rw, M, KD, tb)
                t0 = ps * TG + tb * 512
                cols = slice(t0, t0 + 512)
                if kind in ("av", "bv", "iq"):
                    ob, rob = ob16.next()
                    kb.op("act", lambda: nc.scalar.copy(out=ob[:], in_=pb[b][:, :]), reads=[pr[b]], writes=[rob])
                    if kind == "av": dst, rd = vaT[idx][:, cols], R["vaT"]
                    elif kind == "bv": dst, rd = vbT[:, cols], R["vbT"]
                    else: dst, rd = qiT[idx][:, cols], R["qiT"]
                    kb.dma("sp", dst, ob[:], reads=[rob], writes=[rd])
                elif kind == "ikw":
                    ob, rob = ob16.next()
                    kb.op("act", lambda: nc.scalar.copy(out=ob[0:64, :], in_=pb[b][0:64, :]), reads=[pr[b]], writes=[rob])
                    kb.dma("sp", kiT[:, cols], ob[0:64, :], reads=[rob], writes=[R["kiT"]])
                    of, rof = f32a.next()
                    kb.op("act", lambda: nc.scalar.copy(out=of[64:64 + c.IW, :], in_=pb[b][64:64 + c.IW, :]),
                          reads=[pr[b]], writes=[rof])
                    kb.dma("sp", wiT[0:c.IW, cols], of[64:64 + c.IW, :], reads=[rof], writes=[R["wiT"]])
                else:
                    G = 64 if kind in ("aq", "ak") else 128
                    red, rred = (bd64_f, r_bd64) if G == 64 else (ones_f, r_onesf)
                    gi = {"aq": 0, "ak": 1, "bq": 2, "bk": 3}[kind]
                    sq, rsq = f32a.next()
                    kb.op("act", lambda: nc.scalar.activation(out=sq[:], in_=pb[b][:, :], func=AF.Square),
                          reads=[pr[b]], writes=[rsq])
                    sbk = 6
                    kb.op("pe", lambda: nc.tensor.matmul(pb[sbk][:, :], lhsT=red[:], rhs=sq[:], start=True, stop=True),
                          reads=[rred, rsq], writes=[pr[sbk]])
                    rs, rrs = f32b.next()
                    rsqrt_op(rs[:], rrs, pb[sbk][:, :], pr[sbk], G)
                    ob, rob = ob16.next()
                    kb.op("dve", lambda: nc.vector.scalar_tensor_tensor(out=ob[:], in0=pb[b][:, :], scalar=gs2[:, gi:gi + 1],
                                                                        in1=rs[:], op0=ALU.mult, op1=ALU.mult),
                          reads=[pr[b], rrs, r_gs2], writes=[rob])
                    if kind == "aq": dst, rd = qaT[idx][:, cols], R["qaT"]
                    elif kind == "ak": dst, rd = kaT[idx][:, cols], R["kaT"]
                    elif kind == "bq": dst, rd = qbT[idx][:, cols], R["qbT"]
                    else: dst, rd = kbT[:, cols], R["kbT"]
                    kb.dma("sp", dst, ob[:], reads=[rob], writes=[rd])

        stream_groups(w_in[l], groups, KD, unit, 3)

    def load_layer_params(l):
        lam_init = 0.8 - 0.6 * math.exp(-0.3 * l)
        kb.dma("sp", gA_sb[:], gA[l], writes=[r_gA])
        kb.dma("sp", gF_sb[:], gF[l], writes=[r_gF])
        kb.dma("sp", gsm_sb[:], gsm[l], writes=[r_gsm])
        kb.dma("sp", lq_sb[:], lqb[l], writes=[r_lq])
        kb.dma("sp", cw_sb[:], cwv[l], writes=[r_cw])
        kb.dma("sp", cb_sb[:], cbv[l], writes=[r_cb])
        sD = float(math.sqrt(c.D))
        kb.op("dve", lambda: nc.vector.tensor_scalar_mul(out=gA_sb[:], in0=gA_sb[:], scalar1=sD), reads=[r_gA], writes=[r_gA])
        kb.op("dve", lambda: nc.vector.tensor_scalar_mul(out=gF_sb[:], in0=gF_sb[:], scalar1=sD), reads=[r_gF], writes=[r_gF])
        facs = [1.0, 8.0, 1.0, math.sqrt(128.0), math.sqrt(128.0) * (1.0 - lam_init)]
        for i, f in enumerate(facs):
            kb.op("dve", lambda: nc.vector.tensor_scalar_mul(out=gs2[:, i:i + 1], in0=gsm_sb[:, i:i + 1], scalar1=float(f)),
                  reads=[r_gsm], writes=[r_gs2])
        kb.op("dve", lambda: nc.vector.tensor_tensor(out=lwork[:, 0:64], in0=lq_sb[:, 0:64], in1=lq_sb[:, 64:128], op=ALU.mult),
              reads=[r_lq], writes=[r_lwork])
        kb.op("dve", lambda: nc.vector.tensor_tensor(out=lwork[:, 64:128], in0=lq_sb[:, 128:192], in1=lq_sb[:, 192:256], op=ALU.mult),
              reads=[r_lq], writes=[r_lwork])
        kb.op("dve", lambda: nc.vector.reduce_sum(out=lsm[:, 0:1], in_=lwork[:, 0:64], axis=AX.X), reads=[r_lwork], writes=[r_lsm])
        kb.op("dve", lambda: nc.vector.reduce_sum(out=lsm[:, 1:2], in_=lwork[:, 64:128], axis=AX.X), reads=[r_lwork], writes=[r_lsm])
        kb.op("act", lambda: nc.scalar.activation(out=lsm[:, 2:4], in_=lsm[:, 0:2], func=AF.Exp), reads=[r_lsm], writes=[r_lsm])
        kb.op("dve", lambda: nc.vector.tensor_tensor(out=lsm[:, 4:5], in0=lsm[:, 3:4], in1=lsm[:, 2:3], op=ALU.subtract),
              reads=[r_lsm], writes=[r_lsm])
        kb.op("dve", lambda: nc.vector.tensor_scalar_add(out=lsm[:, 5:6], in0=lsm[:, 4:5], scalar1=float(-lam_init)),
              reads=[r_lsm], writes=[r_lsm])
    nlam = lsm[:, 5:6]

    def load_bias_tiles(hglob):
        bm, rbm = bms.next()
        kb.dma("pool", bm[:], btab[hglob], writes=[rbm])
        kb.op("dve", lambda: nc.vector.tensor_tensor(out=bm[:], in0=bm[:], in1=mask_f[:], op=ALU.add),
              reads=[rbm, r_maskf], writes=[rbm])
        return bm, rbm

    def transpose_v(src_dram, rsrc):
        vt, rvt = vTs.next()
        kb.dma("sp", vt[:], src_dram, reads=[rsrc], writes=[rvt])
        vh, rvh = vhs.next()
        for g in range(0, NT, 8):
            ng = min(8, NT - g)

            def fn():
                ins = None
                for i in range(ng):
                    ins = nc.tensor.transpose(out=pb7b[:, i * 128:(i + 1) * 128],
                                              in_=vt[:, (g + i) * 128:(g + i + 1) * 128], identity=ident_b[:])
                return ins
            kb.op("pe", fn, reads=[rvt, r_identb], writes=[pr[7]])
            kb.op("act", lambda: nc.scalar.copy(out=vh[:, g:g + ng, :],
                                                in_=pb7b[:, 0:ng * 128].rearrange("p (a b) -> p a b", b=128)),
                  reads=[pr[7]], writes=[rvh])
        return vh, rvh

    sbank = [0]

    def next_sbank():
        sbank[0] ^= 1
        return sbank[0]

    SB = [0, 1, 7]
    LA = 2
    ticker = [None]

    def tick():
        g = ticker[0]
        if g is not None:
            try:
                next(g)
            except StopIteration:
                ticker[0] = None

    def attn_block(hglob, qb, q_ap, rq, k_ap, rk, vh, rvh, bm, rbm, obank, dbank, use_mq):
        q0 = qb * 512
        nkt = 4 * qb + 4

        def emit_s(kt):
            cs = max(0, kt * 128 - q0)
            near = kt >= 4 * qb - 1
            sbank[0] = (sbank[0] + 1) % len(SB)
            sbk = SB[sbank[0]]
            cls = kt - 4 * qb + 1

            def fn_s():
                ins = nc.tensor.matmul(pb[sbk][:, cs:512], lhsT=k_ap[:, kt * 128:(kt + 1) * 128],
                                       rhs=q_ap[:, q0 + cs:q0 + 512], start=True, stop=(not near and not use_mq))
                if use_mq:
                    for qi in range(cs // 128, 4):
                        last = (qi == 3) and not near
                        ins = nc.tensor.matmul(pb[sbk][:, qi * 128:(qi + 1) * 128],
                                               lhsT=mq_all[:, 4 * qb + qi, kt * 128:(kt + 1) * 128],
                                               rhs=ident_b[:], start=False, stop=last)
                if near:
                    ins = nc.tensor.matmul(pb[sbk][:, cs:512], lhsT=ident_b[:], rhs=bm[:, cls, cs:512],
                                           start=False, stop=True)
                return ins
            rd = [rq, rk, r_identb]
            if near: rd.append(rbm)
            if use_mq: rd += [r_mq[4 * qb + qi] for qi in range(cs // 128, 4)]
            kb.op("pe", fn_s, reads=rd, writes=[pr[sbk]])
            eb, reb = ebs.next()
            if near:
                kb.op("act", lambda: nc.scalar.activation(out=eb[:, cs:512], in_=pb[sbk][:, cs:512], func=AF.Exp),
                      reads=[pr[sbk]], writes=[reb])
            else:
                kb.op("act", lambda: nc.scalar.activation(out=eb[:, cs:512], in_=pb[sbk][:, cs:512], func=AF.Exp,
                                                          bias=relb_sb[:, hglob:hglob + 1]),
                      reads=[pr[sbk], r_relb], writes=[reb])
            return (kt, cs, eb, reb)

        def emit_pv(st):
            kt, cs, eb, reb = st

            def fn_pv():
                nc.tensor.matmul(pb[obank][:, cs:512], lhsT=vh[:, kt, :], rhs=eb[:, cs:512],
                                 start=(kt == 0), stop=(kt == nkt - 1))
                return nc.tensor.matmul(pb[dbank][:, cs:512], lhsT=ones_b[:], rhs=eb[:, cs:512],
                                        start=(kt == 0), stop=(kt == nkt - 1))
            kb.op("pe", fn_pv, reads=[rvh, reb, r_onesb], writes=[pr[obank], pr[dbank]])

        pend = []
        for kt in range(nkt + LA):
            if kt < nkt:
                pend.append(emit_s(kt))
            if kt >= LA:
                emit_pv(pend.pop(0))
                tick()

    def phase_A_diff():
        for h in range(c.HA):
            qh, rqh = qhs.next(); kh, rkh = khs.next()
            kb.dma("sp", qh[:], qaT[h], reads=[R["qaT"]], writes=[rqh])
            kb.dma("sp", kh[:], kaT[h], reads=[R["kaT"]], writes=[rkh])
            bm, rbm = load_bias_tiles(h)
            vh, rvh = transpose_v(vaT[h], R["vaT"])
            for qb in range(NTB):
                attn_block(h, qb, qh[0:64, :], rqh, kh[0:64, :], rkh, vh, rvh, bm, rbm, 2, 4, False)
                attn_block(h, qb, qh[64:128, :], rqh, kh[64:128, :], rkh, vh, rvh, bm, rbm, 3, 5, False)
                r1, rr1 = f32a.next(); r2, rr2 = f32a.next()
                kb.op("dve", lambda: nc.vector.reciprocal(out=r1[:], in_=pb[4][:, :]), reads=[pr[4]], writes=[rr1])
                kb.op("dve", lambda: nc.vector.reciprocal(out=r2[:], in_=pb[5][:, :]), reads=[pr[5]], writes=[rr2])
                kb.op("dve", lambda: nc.vector.tensor_tensor(out=r1[:], in0=pb[2][:, :], in1=r1[:], op=ALU.mult),
                      reads=[pr[2], rr1], writes=[rr1])
                kb.op("dve", lambda: nc.vector.tensor_tensor(out=r2[:], in0=pb[3][:, :], in1=r2[:], op=ALU.mult),
                      reads=[pr[3], rr2], writes=[rr2])
                if h == 0 and qb == 0:
                    dbg_dump("t1", r1[:], rr1, [128, 512]); dbg_dump("t2", r2[:], rr2, [128, 512]); dbg_dump("lsm", lsm[:], r_lsm, [128, 8])
                oa, roa = f32b.next()
                kb.op("dve", lambda: nc.vector.scalar_tensor_tensor(out=oa[:], in0=r2[:], scalar=nlam, in1=r1[:],
                                                                    op0=ALU.mult, op1=ALU.add),
                      reads=[rr1, rr2, r_lsm], writes=[roa])
                if h == 0 and qb == 0:
                    dbg_dump("oa", oa[:], roa, [128, 512])
                sq, rsq = f32b.next()
                kb.op("act", lambda: nc.scalar.activation(out=sq[:], in_=oa[:], func=AF.Square), reads=[roa], writes=[rsq])
                kb.op("pe", lambda: nc.tensor.matmul(pb[6][:, :], lhsT=ones_f[:], rhs=sq[:], start=True, stop=True),
                      reads=[r_onesf, rsq], writes=[pr[6]])
                rs, rrs = f32a.next()
                rsqrt_op(rs[:], rrs, pb[6][:, :], pr[6], 128)
                ob, rob = ob16.next()
                kb.op("dve", lambda: nc.vector.scalar_tensor_tensor(out=ob[:], in0=oa[:], scalar=gs2[:, 4:5], in1=rs[:],
                                                                    op0=ALU.mult, op1=ALU.mult),
                      reads=[roa, rrs, r_gs2], writes=[rob])
                kb.dma("sp", mixT[h][:, qb * 512:(qb + 1) * 512], ob[:], reads=[rob], writes=[R["mixT"]])

    def phase_A_index():
        kb.dma("sp", ki_sb[:], kiT, reads=[R["kiT"]], writes=[r_kisb])
        for tb in range(NTB):
            wt_, rwt_ = f32a.next()
            kb.dma("sp", wt_[0:c.IW, :], wiT[0:c.IW, tb * 512:(tb + 1) * 512], reads=[R["wiT"]], writes=[rwt_])
            for t4 in range(4):
                t = tb * 4 + t4
                kb.op("pe", lambda: nc.tensor.transpose(out=pb[6][:, 0:c.IW], in_=wt_[0:c.IW, t4 * 128:(t4 + 1) * 128],
                                                        identity=ident_f[0:c.IW, 0:c.IW]),
                      reads=[rwt_, r_identf], writes=[pr[6]])
                kb.op("act", lambda: nc.scalar.copy(out=wi_tok[:, t, 0:c.IW], in_=pb[6][:, 0:c.IW]),
                      reads=[pr[6]], writes=[r_witok])
        qiv = qiT.rearrange("b (two p) t -> p (b two) t", two=2)
        sc = sc_t
        for qt in range(NT):
            Lk = (qt + 1) * 128
            qi_t, rqi = qis.next()
            kb.dma("sp", qi_t[:], qiv[:, :, qt * 128:(qt + 1) * 128], reads=[R["qiT"]], writes=[rqi])
            for kbk in range((Lk + 511) // 512):
                n = min(512, Lk - kbk * 512)
                for hi in range(c.HI):
                    b = 6
                    kb.op("pe", lambda: nc.tensor.matmul(pb[b][:, :n], lhsT=qi_t[:, hi, :], rhs=ki_sb[:, kbk * 512:kbk * 512 + n],
                                                         start=True, stop=True),
                          reads=[rqi, r_kisb], writes=[pr[b]])
                    rl, rrl = f32b.next()
                    kb.op("act", lambda: nc.scalar.activation(out=rl[:, :n], in_=pb[b][:, :n], func=AF.Relu),
                          reads=[pr[b]], writes=[rrl])
                    dst = sc[:, kbk * 512:kbk * 512 + n]
                    if hi == 0:
                        kb.op("dve", lambda: nc.vector.tensor_scalar_mul(out=dst, in0=rl[:, :n], scalar1=wi_tok[:, qt, 0:1]),
                              reads=[rrl, r_witok], writes=[rsc])
                    else:
                        kb.op("dve", lambda: nc.vector.scalar_tensor_tensor(out=dst, in0=rl[:, :n], scalar=wi_tok[:, qt, hi:hi + 1],
                                                                            in1=dst, op0=ALU.mult, op1=ALU.add),
                              reads=[rrl, r_witok, rsc], writes=[rsc])
                    yield
            kb.op("dve", lambda: nc.vector.memset(sc[0:64, Lk - 64:Lk], NINF), reads=[rsc], writes=[rsc])
            if qt == 2:
                dbg_dump("sc2", sc[:, :Lk], rsc, [128, Lk])
            if Lk > c.TOPK:
                m8 = None
                nr = c.TOPK // 8
                for r in range(nr):
                    src = sc if r == 0 else work
                    rsrc = rsc if r == 0 else r_work
                    m8, rm8 = m8s.next()
                    kb.op("dve", lambda: nc.vector.max(out=m8[:], in_=src[:, :Lk]), reads=[rsrc], writes=[rm8], strict=True)
                    m8c, rm8c = m8cs.next()
                    kb.op("act", lambda: nc.scalar.copy(out=m8c[:], in_=m8[:]), reads=[rm8], writes=[rm8c])
                    if r < nr - 1:
                        kb.op("dve", lambda: nc.vector.match_replace(out=work[:, :Lk], in_to_replace=m8c[:], in_values=src[:, :Lk],
                                                                     imm_value=NINF),
                              reads=[rsrc, rm8c], writes=[r_work], strict=True)
                    yield
                thr_ap, rthr = m8c[:, 7:8], rm8c
            else:
                thr_ap, rthr = thr_c[:, 0:1], r_thrc
            kb.op("dve", lambda: nc.vector.tensor_scalar(out=mq_all[:, qt, :Lk], in0=sc[:, :Lk], scalar1=thr_ap, scalar2=NEG,
                                                         op0=ALU.is_lt, op1=ALU.mult),
                  reads=[rsc, rthr], writes=[r_mq[qt]], strict=True)
            if qt == 2:
                dbg_dump("m8", m8c[:], rm8c, [128, 8])
                dbg_dump("mq2", mq_all[:, qt, :Lk], r_mq[qt], [128, Lk], BF16)

    def phase_A_dsa():
        kb.dma("sp", kb_sb[:], kbT, reads=[R["kbT"]], writes=[r_kbsb])
        vh, rvh = transpose_v(vbT, R["vbT"])
        for h in range(c.HB):
            qh, rqh = qhs.next()
            kb.dma("sp", qh[:], qbT[h], reads=[R["qbT"]], writes=[rqh])
            bm, rbm = load_bias_tiles(c.HA + h)
            for qb in range(NTB):
                ob_, db_ = (2, 4) if (qb % 2 == 0) else (3, 5)
                attn_block(c.HA + h, qb, qh, rqh, kb_sb, r_kbsb, vh, rvh, bm, rbm, ob_, db_, True)
                r1, rr1 = f32a.next()
                kb.op("dve", lambda: nc.vector.reciprocal(out=r1[:], in_=pb[db_][:, :]), reads=[pr[db_]], writes=[rr1])
                ob, rob = ob16.next()
                kb.op("dve", lambda: nc.vector.tensor_tensor(out=ob[:], in0=pb[ob_][:, :], in1=r1[:], op=ALU.mult),
                      reads=[pr[ob_], rr1], writes=[rob])
                kb.dma("sp", mixT[c.HA + h][:, qb * 512:(qb + 1) * 512], ob[:], reads=[rob], writes=[R["mixT"]])

    def residual_epilogue(b, src, rsrc, dst, rdst, cb, t0, accum):
        cols = slice(t0, t0 + 512)
        xin, rxin = xs5.next()
        kb.dma("sp", xin[:], src[cb][:, cols], reads=[rsrc], writes=[rxin])
        xn, rxn = f32a.next()
        kb.op("dve", lambda: nc.vector.tensor_tensor(out=xn[:], in0=pb[b][:, :], in1=xin[:], op=ALU.add),
              reads=[pr[b], rxin], writes=[rxn])
        kb.dma("sp", dst[cb][:, cols], xn[:], reads=[rxn], writes=[rdst])
        if accum:
            ssq_accum(xn[:], rxn, t0, 512)

    def phase_O(l, ps, src, rsrc):
        for k in range(KM):
            kb.dma("sp", hT[:, k, :], mixT[k][:, ps * TG:(ps + 1) * TG], reads=[R["mixT"]], writes=[hres[k]])
        groups = [(cb, [(cb * 128, 128)]) for cb in range(KD)]

        def unit(cb, ws):
            wt, rw, M = ws[0]
            for tb in range(NTBG):
                b = next_bank(6)
                gemm_unit(b, wt, rw, M, KM, tb)
                residual_epilogue(b, src, rsrc, xa, R["xa"], cb, ps * TG + tb * 512, True)
        stream_groups(w_out[l], groups, KM, unit, 3)

    def phase_F1(l, ps):
        groups = [(fc, [(fc * 128, 128), (c.DFF + fc * 128, 128)]) for fc in range(NFC)]

        def unit(fc, ws):
            (wg, rwg, _), (wu, rwu, _) = ws
            prev = None
            for tb in range(NTBG):
                bg = next_bank(6)
                gemm_unit(bg, wg, rwg, 128, KD, tb)
                bu = next_bank(6)
                gemm_unit(bu, wu, rwu, 128, KD, tb)
                gs, rgs = gsl.next()
                kb.op("act", lambda: nc.scalar.copy(out=gs[:, 2:514], in_=pb[bg][:, :]), reads=[pr[bg]], writes=[rgs])
                if tb == 0:
                    if ps == 0:
                        kb.op("dve", lambda: nc.vector.memset(gs[:, 0:2], 0.0), reads=[rgs], writes=[rgs])
                    else:
                        kb.op("dve", lambda: nc.vector.tensor_copy(out=gs[:, 0:2], in_=gcar[:, fc, :]), reads=[r_gcar, rgs], writes=[rgs])
                else:
                    pg, rpg = prev
                    kb.op("dve", lambda: nc.vector.tensor_copy(out=gs[:, 0:2], in_=pg[:, 512:514]), reads=[rpg, rgs], writes=[rgs])
                if tb == NTBG - 1 and ps < NPASS - 1:
                    kb.op("dve", lambda: nc.vector.tensor_copy(out=gcar[:, fc, :], in_=gs[:, 512:514]), reads=[rgs], writes=[r_gcar])
                prev = (gs, rgs)
                a, ra = f32a.next()
                kb.op("dve", lambda: nc.vector.tensor_scalar(out=a[:], in0=gs[:, 2:514], scalar1=cw_sb[:, fc, 2:3],
                                                             scalar2=cb_sb[:, fc:fc + 1], op0=ALU.mult, op1=ALU.add),
                      reads=[rgs, r_cw, r_cb], writes=[ra])
                kb.op("dve", lambda: nc.vector.scalar_tensor_tensor(out=a[:], in0=gs[:, 1:513], scalar=cw_sb[:, fc, 1:2], in1=a[:],
                                                                    op0=ALU.mult, op1=ALU.add),
                      reads=[rgs, r_cw, ra], writes=[ra])
                kb.op("dve", lambda: nc.vector.scalar_tensor_tensor(out=a[:], in0=gs[:, 0:512], scalar=cw_sb[:, fc, 0:1], in1=a[:],
                                                                    op0=ALU.mult, op1=ALU.add),
                      reads=[rgs, r_cw, ra], writes=[ra])
                s, rs_ = f32b.next()
                kb.op("act", lambda: nc.scalar.activation(out=s[:], in_=a[:], func=AF.Silu), reads=[ra], writes=[rs_])
                ob, rob = ob16.next()
                kb.op("dve", lambda: nc.vector.tensor_tensor(out=ob[:], in0=pb[bu][:, :], in1=s[:], op=ALU.mult),
                      reads=[pr[bu], rs_], writes=[rob])
                t0 = ps * TG + tb * 512
                kb.dma("sp", actT[fc][:, t0:t0 + 512], ob[:], reads=[rob], writes=[R["actT"]])
        stream_groups(w_gu[l], groups, KD, unit, 2)

    def phase_F2(l, dst, rdst, accum):
        if accum:
            ssq_zero()
        wl = w_dn[l]
        for tg in range(NTB // NTBF):
            for cg in range(KD // CCF):
                c0 = cg * CCF * 128
                wd = None
                for fc0 in range(0, NFC, AG):
                    if fc0 % GF == 0:
                        ng = min(GF, NFC - fc0)
                        wd, rwd = wds.next()
                        kb.dma("pool", wd[:, :ng, :], wl[fc0 * 128:(fc0 + ng) * 128, c0:c0 + CCF * 128].rearrange("(f p) c -> p f c", p=128),
                               writes=[rwd])
                    na = min(AG, NFC - fc0)
                    at, rat = asl.next()
                    kb.dma("sp", at[:, :na, :], actT[fc0:fc0 + na, :, tg * TF:(tg + 1) * TF].rearrange("f p t -> p f t"),
                           reads=[R["actT"]], writes=[rat])
                    for a in range(na):
                        fc = fc0 + a

                        def fn():
                            ins = None
                            for cc in range(CCF):
                                for tb in range(NTBF):
                                    ins = nc.tensor.matmul(pb[cc * NTBF + tb][:, :], lhsT=wd[:, fc % GF, cc * 128:(cc + 1) * 128],
                                                           rhs=at[:, a, tb * 512:(tb + 1) * 512], start=(fc == 0), stop=(fc == NFC - 1))
                            return ins
                        kb.op("pe", fn, reads=[rwd, rat], writes=[pr[i] for i in range(CCF * NTBF)])
                for cc in range(CCF):
                    for tb in range(NTBF):
                        residual_epilogue(cc * NTBF + tb, xa, R["xa"], dst, rdst, cg * CCF + cc, tg * TF + tb * 512, accum)

    cur, rcur = xT, R["xT"]
    for l in range(L):
        load_layer_params(l)
        if l == 0:
            phase_S(cur, rcur)
        finish_norm(c.D)
        for ps in range(NPASS):
            phase_N(ps, cur, rcur, gA_sb, r_gA)
            phase_P(l, ps)
        barrier()
        ticker[0] = phase_A_index()
        phase_A_diff()
        if ticker[0] is not None:
            for _ in ticker[0]:
                pass
            ticker[0] = None
        phase_A_dsa()
        barrier()
        ssq_zero()
        for ps in range(NPASS):
            phase_O(l, ps, cur, rcur)
        finish_norm(c.D)
        for ps in range(NPASS):
            phase_N(ps, xa, R["xa"], gF_sb, r_gF)
            phase_F1(l, ps)
        last = (l == L - 1)
        nxt, rnxt = (yT, R["yT"]) if last else (xb, R["xb"])
        phase_F2(l, nxt, rnxt, not last)
        cur, rcur = nxt, rnxt
    kb.final_wait("sp", R["yT"])
    return nc, kb


def _rel_bucket_np(rel):
    import jax
    import jax.numpy as jnp
    cpu = jax.devices("cpu")[0]
    with jax.default_device(cpu):
        rel = jnp.asarray(rel, dtype=jnp.int32)
        nb = 32 // 2
        max_exact = nb // 2
        bucket = jnp.where(rel > 0, nb, 0)
        n = jnp.abs(rel)
        nf = jnp.maximum(n, 1).astype(jnp.float32)
        large = max_exact + (jnp.log(nf / max_exact) / math.log(128 / max_exact) * (nb - max_exact)).astype(jnp.int32)
        large = jnp.minimum(large, nb - 1)
        out = bucket + jnp.where(n < max_exact, n, large)
        return np.asarray(out)


def prep_shared(cfg, inputs):
    c = cfg
    L = c.DEPTH
    f = lambda a: np.ascontiguousarray(np.asarray(a, dtype=np.float32))
    col = lambda v, k: np.ascontiguousarray(v.reshape(k, 128).T)
    sh = {}
    sh["w_in"] = f(inputs["w_in"]); sh["w_out"] = f(inputs["w_out"])
    sh["w_gu"] = f(inputs["w_gate_up"]); sh["w_dn"] = f(inputs["w_down"])
    an = f(inputs["attn_norm"]); fn = f(inputs["ffn_norm"])
    sh["gA"] = np.stack([col(an[l], c.KD) for l in range(L)])
    sh["gF"] = np.stack([col(fn[l], c.KD) for l in range(L)])
    gsm = np.zeros((L, 128, 8), np.float32)
    aq = f(inputs["a_q_norm"]); ak = f(inputs["a_k_norm"]); ao = f(inputs["a_out_norm"])
    bq = f(inputs["b_q_norm"]); bk = f(inputs["b_k_norm"])
    for l in range(L):
        gsm[l, :, 0] = np.concatenate([aq[l], aq[l]])
        gsm[l, :, 1] = np.concatenate([ak[l], ak[l]])
        gsm[l, :, 2] = bq[l]; gsm[l, :, 3] = bk[l]; gsm[l, :, 4] = ao[l]
    sh["gsm"] = gsm
    lq = f(inputs["lambda_qk"]).reshape(L, 1, 256)
    sh["lqb"] = np.ascontiguousarray(np.broadcast_to(lq, (L, 128, 256)))
    cw = f(inputs["conv_w"]); cb = f(inputs["conv_b"])
    sh["cwv"] = np.ascontiguousarray(cw.reshape(L, 3, c.NFC, 128).transpose(0, 3, 2, 1))
    sh["cbv"] = np.ascontiguousarray(cb.reshape(L, c.NFC, 128).transpose(0, 2, 1))
    rb = f(inputs["rel_bias"])
    sh["relb15"] = np.ascontiguousarray(np.broadcast_to(rb[15][None, :], (128, c.NH)))
    p = np.arange(128)[:, None, None]
    dl = (np.arange(5) * 128 - 128)[None, :, None]
    j = np.arange(512)[None, None, :]
    d = dl + p - j
    bidx = _rel_bucket_np(d)
    sh["btab"] = np.ascontiguousarray(rb[bidx].transpose(3, 0, 1, 2))
    vis = (dl // 64 + p // 64) <= (j // 64)
    sh["maskd"] = np.where(vis, 0.0, NEG).astype(np.float32)
    sh["identd"] = np.eye(128, dtype=np.float32)
    return sh


def run(cfg, inputs, debug=False):
    c = cfg
    sh = prep_shared(c, inputs)
    x = np.asarray(inputs["x"], dtype=np.float32)
    B = x.shape[0]
    in_maps = []
    for b in range(B):
        m = dict(sh)
        m["xT"] = np.ascontiguousarray(x[b].T.reshape(c.KD, 128, c.T))
        in_maps.append(m)
    nc, kb = build_program(c, debug=debug)
    res = run_bass_kernel_spmd(nc, in_maps, core_ids=list(range(B)))
    out = np.empty_like(x)
    for b in range(B):
        out[b] = res.results[b]["yT"].reshape(c.D, c.T).T
    if debug:
        return out, res
    return out


def kernel(**inputs):
    return run(FULL, inputs)
```

```python
import math
import numpy as np
import concourse.bass as bass
import concourse.mybir as mybir
from concourse.bass_utils import run_bass_kernel_spmd

F32 = mybir.dt.float32
BF16 = mybir.dt.bfloat16
AF = mybir.ActivationFunctionType
ALU = mybir.AluOpType
AX = mybir.AxisListType

EPS = 1e-6
NEG = -30000.0
NINF = -1.0e30


class Cfg:
    def __init__(self, D=4096, S=2048, HA=16, HB=16, HI=16, DFF=11008, DEPTH=2, BATCH=4, TOPK_MAX=256, TG=1024):
        self.D, self.S, self.HA, self.HB, self.HI, self.DFF, self.DEPTH, self.BATCH = D, S, HA, HB, HI, DFF, DEPTH, BATCH
        self.T = S
        self.TG = TG
        self.NPASS = S // TG
        self.NTBG = TG // 512
        self.KD = D // 128
        self.NTB = S // 512
        self.NT = S // 128
        self.AQ = HA * 128; self.AK = HA * 128; self.AV = HA * 128
        self.BQ = HB * 128; self.BK = 128; self.BV = 128
        self.IQ = HI * 64; self.IK = 64; self.IW = HI
        self.DIN = self.AQ + self.AK + self.AV + self.BQ + self.BK + self.BV + self.IQ + self.IK + self.IW
        self.DMIX = HA * 128 + HB * 128
        self.KM = self.DMIX // 128
        self.NFC = DFF // 128
        self.NH = HA + HB
        self.TOPK = min(TOPK_MAX, S // 4)
        self.KMAX = max(self.KD, self.KM)
        assert DFF % 128 == 0 and D % 128 == 0 and S % 512 == 0 and HI % 2 == 0
        assert self.IK + self.IW <= 128


FULL = Cfg()


class Res:
    __slots__ = ("name", "w", "r", "dsem", "dval", "multi")

    def __init__(self, name, multi=False):
        self.name = name
        self.w = None
        self.r = {}
        self.dsem = None
        self.dval = 0
        self.multi = multi


class KB:
    def __init__(self, nc):
        self.nc = nc
        self.engs = {"pe": nc.tensor, "act": nc.scalar, "dve": nc.vector, "pool": nc.gpsimd, "sp": nc.sync}
        self.esem = {}
        self.ecnt = {}
        for n in ("pe", "act", "dve", "pool"):
            self.esem[n] = nc.semaphore("es_" + n).__enter__()
            self.ecnt[n] = 0
        self.waited = {n: {} for n in self.engs}
        self.nsem = 4
        self.dres = []

    def sb(self, name, shape, dt):
        t = self.nc.sbuf_tensor(name, list(shape), dt).__enter__()
        return t

    def _wait(self, eng, ev, is_dma):
        sem, val, owner = ev
        if owner == eng and not is_dma:
            return
        key = id(sem)
        if self.waited[eng].get(key, 0) >= val:
            return
        self.engs[eng].wait_ge(sem, val)
        self.waited[eng][key] = val

    def _deps(self, eng, reads, writes, is_dma):
        for r in reads:
            if r.w is not None:
                self._wait(eng, r.w, is_dma)
        for w in writes:
            if w.w is not None and not w.multi:
                self._wait(eng, w.w, is_dma)
            for ev in list(w.r.values()):
                self._wait(eng, ev, is_dma)

    def op(self, eng, fn, reads=(), writes=(), strict=False):
        self._deps(eng, reads, writes, strict or eng in ("dve", "act"))
        ins = fn()
        self.ecnt[eng] += 1
        sem = self.esem[eng]
        ins.then_inc(sem, 1)
        ev = (sem, self.ecnt[eng], eng)
        for r in reads:
            r.r[id(sem)] = ev
        for w in writes:
            w.w = ev
            if not w.multi:
                w.r = {}

    def dma(self, q, out, in_, reads=(), writes=()):
        self._deps(q, reads, writes, True)
        w0 = writes[0]
        if w0.dsem is None:
            w0.dsem = self.nc.semaphore("ds_" + w0.name).__enter__()
            self.nsem += 1
            self.dres.append(w0)
        ins = self.engs[q].dma_start(out=out, in_=in_)
        ins.then_inc(w0.dsem, 16)
        w0.dval += 16
        ev = (w0.dsem, w0.dval, None)
        for r in reads:
            r.r[id(w0.dsem)] = ev
        for w in writes:
            w.w = ev
            if not w.multi:
                w.r = {}

    def final_wait(self, eng, res):
        if res.w is not None:
            self._wait(eng, res.w, True)


class Slots:
    def __init__(self, name, views):
        self.t = list(views)
        self.r = [Res(f"{name}{i}") for i in range(len(views))]
        self.i = -1
        self.n = len(views)

    def next(self):
        self.i = (self.i + 1) % self.n
        return self.t[self.i], self.r[self.i]


class Arena:
    def __init__(self, kb, name, nbytes):
        self.n2 = (nbytes + 1) // 2
        self.t = kb.sb(name, [128, self.n2], BF16)
        self.off = 0

    def reset(self):
        self.off = 0

    def alloc(self, shape, dt, parts=128):
        esz = 4 if dt == F32 else 2
        n = 1
        for s in shape[1:]:
            n *= s
        nb = n * esz
        nb = (nb + 63) // 64 * 64
        o2 = self.off // 2
        assert o2 + nb // 2 <= self.n2, ("arena overflow", self.off, nb, self.n2 * 2)
        v = self.t[0:shape[0], o2:o2 + (n * esz) // 2]
        if dt == F32:
            v = v.bitcast(F32)
        if len(shape) == 3:
            v = v.rearrange("p (a b) -> p a b", b=shape[2])
        self.off += nb
        return v


def build_program(cfg, debug=False):
    c = cfg
    nc = bass.Bass("TRN2", target_bir_lowering=False)
    kb = KB(nc)
    T, KD, KM, NTB, NT, NFC = c.T, c.KD, c.KM, c.NTB, c.NT, c.NFC
    TG, NPASS, NTBG = c.TG, c.NPASS, c.NTBG
    L = c.DEPTH

    def din(name, shape, dt=F32):
        return nc.dram_tensor(name, list(shape), dt, kind="ExternalInput").ap()

    def dint(name, shape, dt):
        kind = "ExternalOutput" if debug else "Internal"
        return nc.dram_tensor(name, list(shape), dt, kind=kind).ap()

    xT = din("xT", [KD, 128, T])
    w_in = din("w_in", [L, c.D, c.DIN])
    w_out = din("w_out", [L, c.DMIX, c.D])
    w_gu = din("w_gu", [L, c.D, 2 * c.DFF])
    w_dn = din("w_dn", [L, c.DFF, c.D])
    gA = din("gA", [L, 128, KD])
    gF = din("gF", [L, 128, KD])
    gsm = din("gsm", [L, 128, 8])
    lqb = din("lqb", [L, 128, 256])
    cwv = din("cwv", [L, 128, NFC, 3])
    cbv = din("cbv", [L, 128, NFC])
    relb15 = din("relb15", [128, c.NH])
    btab = din("btab", [c.NH, 128, 5, 512])
    identd = din("identd", [128, 128])
    maskd = din("maskd", [128, 5, 512])
    yT = nc.dram_tensor("yT", [KD, 128, T], F32, kind="ExternalOutput").ap()

    xa = dint("xa", [KD, 128, T], F32)
    xb = dint("xb", [KD, 128, T], F32)
    qaT = dint("qaT", [c.HA, 128, T], BF16)
    kaT = dint("kaT", [c.HA, 128, T], BF16)
    vaT = dint("vaT", [c.HA, 128, T], BF16)
    qbT = dint("qbT", [c.HB, 128, T], BF16)
    kbT = dint("kbT", [128, T], BF16)
    vbT = dint("vbT", [128, T], BF16)
    qiT = dint("qiT", [c.HI // 2, 128, T], BF16)
    kiT = dint("kiT", [64, T], BF16)
    wiT = dint("wiT", [c.IW, T], F32)
    mixT = dint("mixT", [KM, 128, T], BF16)
    actT = dint("actT", [NFC, 128, T], BF16)
    R = {n: Res(n, multi=True) for n in ("xT", "xa", "xb", "yT", "qaT", "kaT", "vaT", "qbT", "kbT", "vbT",
                                          "qiT", "kiT", "wiT", "mixT", "actT")}

    pb = [nc.psum_tensor(f"pb{i}", [128, 512], F32).__enter__() for i in range(8)]
    pr = [Res(f"pb{i}") for i in range(8)]
    pb7b = pb[7][:].bitcast(BF16)

    hbytes = max(c.KMAX * TG * 2, NT * T * 2)
    big = Arena(kb, "big", hbytes)
    hT = big.alloc([128, c.KMAX, TG], BF16); big.reset()
    mq_all = big.alloc([128, NT, T], BF16); big.reset()
    hres = [Res(f"hT{k}") for k in range(c.KMAX)]
    r_mq = [Res(f"mq{i}") for i in range(NT)]
    ident_f = kb.sb("ident_f", [128, 128], F32); r_identf = Res("ident_f")
    ident_b = kb.sb("ident_b", [128, 128], BF16); r_identb = Res("ident_b")
    ones_f = kb.sb("ones_f", [128, 128], F32); r_onesf = Res("ones_f")
    ones_b = kb.sb("ones_b", [128, 128], BF16); r_onesb = Res("ones_b")
    bd64_f = kb.sb("bd64_f", [128, 128], F32); r_bd64 = Res("bd64_f")
    mask_f = kb.sb("mask_f", [128, 5, 512], F32); r_maskf = Res("mask_f")
    relb_sb = kb.sb("relb_sb", [128, c.NH], F32); r_relb = Res("relb")
    thr_c = kb.sb("thr_c", [128, 1], F32); r_thrc = Res("thr_c")
    epsb = kb.sb("epsb", [128, 4], F32); r_epsb = Res("epsb")
    r_ssq = Res("ssq_part"); r_rstd = Res("rstd")
    gA_sb = kb.sb("gA_sb", [128, KD], F32); r_gA = Res("gA")
    gF_sb = kb.sb("gF_sb", [128, KD], F32); r_gF = Res("gF")
    gsm_sb = kb.sb("gsm_sb", [128, 8], F32); r_gsm = Res("gsm")
    gs2 = kb.sb("gs2", [128, 8], F32); r_gs2 = Res("gs2")
    lq_sb = kb.sb("lq_sb", [128, 256], F32); r_lq = Res("lq")
    lwork = kb.sb("lwork", [128, 128], F32); r_lwork = Res("lwork")
    lsm = kb.sb("lsm", [128, 8], F32); r_lsm = Res("lsm")
    cw_sb = kb.sb("cw_sb", [128, NFC, 3], F32); r_cw = Res("cw")
    cb_sb = kb.sb("cb_sb", [128, NFC], F32); r_cb = Res("cb")
    gcar = kb.sb("gcar", [128, NFC, 2], F32); r_gcar = Res("gcar")
    f32a = Slots("f32a", [kb.sb(f"f32a{i}", [128, 512], F32) for i in range(4)])
    f32b = Slots("f32b", [kb.sb(f"f32b{i}", [128, 512], F32) for i in range(4)])
    ob16 = Slots("ob16", [kb.sb(f"ob16{i}", [128, 512], BF16) for i in range(4)])
    m8s_ch = [Slots(f"m8s{j}", [kb.sb(f"m8s{j}_{i}", [128, 8], F32) for i in range(2)]) for j in range(2)]
    m8cs_ch = [Slots(f"m8cs{j}", [kb.sb(f"m8cs{j}_{i}", [128, 8], F32) for i in range(2)]) for j in range(2)]

    GF = 8
    NTBF = min(2, NTB)
    CCF = min(KD, 8 // NTBF)
    TF = NTBF * 512
    AG = 2
    g_bytes = 4 * c.KMAX * 256 + 3 * GF * CCF * 256 + 4 * AG * TF * 2 + 3 * 2112 + 3 * 2048 + 1024 + 2 * T * 4
    a_bytes = 8 * T * 2 + 2 * 5120 + 5 * 1024 + 2 * T * 2 + 1024 + 2 * c.HI * 256 + 4 * T * 4 + 2048
    ar = Arena(kb, "arena", max(g_bytes, a_bytes))
    wsl = Slots("wsl", [ar.alloc([128, c.KMAX, 128], BF16) for _ in range(4)])
    wds = Slots("wds", [ar.alloc([128, GF, CCF * 128], BF16) for _ in range(3)])
    asl = Slots("asl", [ar.alloc([128, AG, TF], BF16) for _ in range(4)])
    gsl = Slots("gsl", [ar.alloc([128, 514], F32) for _ in range(3)])
    xs5 = Slots("xs5", [ar.alloc([128, 512], F32) for _ in range(3)])
    ssq_part = ar.alloc([128, T], F32)
    rstd = ar.alloc([128, T], F32)
    ar.reset()
    qhs = Slots("qhs", [ar.alloc([128, T], BF16) for _ in range(2)])
    khs = Slots("khs", [ar.alloc([128, T], BF16) for _ in range(2)])
    vTs = Slots("vTs", [ar.alloc([128, T], BF16) for _ in range(2)])
    vhs = Slots("vhs", [ar.alloc([128, NT, 128], BF16) for _ in range(2)])
    bms = Slots("bms", [ar.alloc([128, 5, 512], BF16) for _ in range(2)])
    ebs = Slots("ebs", [ar.alloc([128, 512], BF16) for _ in range(5)])
    kb_sb = ar.alloc([128, T], BF16); r_kbsb = Res("kb_sb")
    ki_sb = ar.alloc([64, T], BF16); r_kisb = Res("ki_sb")
    wi_tok = ar.alloc([128, NT, 16], F32); r_witok = Res("wi_tok")
    qis = Slots("qis", [ar.alloc([64, c.HI, 128], BF16) for _ in range(2)])
    sc_ch = [(ar.alloc([128, T], F32), Res(f"sc{i}")) for i in range(2)]
    work_ch = [(ar.alloc([128, T], F32), Res(f"work{i}")) for i in range(2)]

    def barrier():
        for e in ("pe", "act", "dve", "pool", "sp"):
            for n in ("pe", "act", "dve", "pool"):
                if kb.ecnt[n] > 0:
                    kb._wait(e, (kb.esem[n], kb.ecnt[n], None), True)
            for r in kb.dres:
                kb._wait(e, (r.dsem, r.dval, None), True)

    kb.dma("sp", ident_f[:], identd, writes=[r_identf])
    kb.dma("sp", mask_f[:], maskd, writes=[r_maskf])
    kb.dma("sp", relb_sb[:], relb15, writes=[r_relb])
    kb.op("dve", lambda: nc.vector.tensor_copy(out=ident_b[:], in_=ident_f[:]), reads=[r_identf], writes=[r_identb])
    kb.op("dve", lambda: nc.vector.memset(ones_f[:], 1.0), writes=[r_onesf])
    kb.op("dve", lambda: nc.vector.memset(ones_b[:], 1.0), writes=[r_onesb])
    kb.op("dve", lambda: nc.vector.memset(bd64_f[:], 0.0), writes=[r_bd64])
    kb.op("dve", lambda: nc.vector.memset(bd64_f[0:64, 0:64], 1.0), writes=[r_bd64])
    kb.op("dve", lambda: nc.vector.memset(bd64_f[64:128, 64:128], 1.0), writes=[r_bd64])
    kb.op("dve", lambda: nc.vector.memset(thr_c[:], -1.0e29), writes=[r_thrc])
    kb.op("dve", lambda: nc.vector.memset(epsb[:, 1:2], float(64 * EPS)), writes=[r_epsb])
    kb.op("dve", lambda: nc.vector.memset(epsb[:, 2:3], float(128 * EPS)), writes=[r_epsb])
    kb.op("dve", lambda: nc.vector.memset(epsb[:, 3:4], float(c.D * EPS)), writes=[r_epsb])

    dbg_n = [0]
    dbg_seen = set()

    def dbg_dump(name, ap, res, shape, dt=F32):
        if not debug or name in dbg_seen:
            return
        dbg_seen.add(name)
        d = nc.dram_tensor("dbg_" + name, list(shape), dt, kind="ExternalOutput").ap()
        kb.dma("sp", d, ap, reads=[res], writes=[Res("dbg_" + name)])

    bank_rot = [0]

    def next_bank(n=6):
        b = bank_rot[0] % n
        bank_rot[0] += 1
        return b


    def rsqrt_op(out_ap, out_res, in_ap, in_res, G):
        kb.op("act", lambda: nc.scalar.activation(out=out_ap, in_=in_ap, func=AF.Sqrt, bias=epsb[:, {0: 0, 64: 1, 128: 2}.get(G, 0):{0: 0, 64: 1, 128: 2}.get(G, 0) + 1] if G in (64, 128) else epsb[:, 3:4], scale=1.0),
              reads=[in_res, r_epsb], writes=[out_res])
        kb.op("dve", lambda: nc.vector.reciprocal(out=out_ap, in_=out_ap), reads=[out_res], writes=[out_res])

    def ssq_zero():
        kb.op("dve", lambda: nc.vector.memset(ssq_part[:], 0.0), writes=[r_ssq])

    def ssq_accum(src_ap, src_res, lo, n):
        sq, rsq = f32b.next()
        kb.op("act", lambda: nc.scalar.activation(out=sq[:, :n], in_=src_ap, func=AF.Square),
              reads=[src_res], writes=[rsq])
        kb.op("dve", lambda: nc.vector.tensor_tensor(out=ssq_part[:, lo:lo + n], in0=ssq_part[:, lo:lo + n],
                                                     in1=sq[:, :n], op=ALU.add),
              reads=[rsq, r_ssq], writes=[r_ssq])

    def finish_norm(G):
        for tb in range(NTB):
            b = 6
            kb.op("pe", lambda: nc.tensor.matmul(pb[b][:, :], lhsT=ones_f[:], rhs=ssq_part[:, tb * 512:(tb + 1) * 512],
                                                 start=True, stop=True),
                  reads=[r_onesf, r_ssq], writes=[pr[b]])
            rsqrt_op(rstd[:, tb * 512:(tb + 1) * 512], r_rstd, pb[b][:, :], pr[b], G)

    def phase_S(src, rsrc):
        ssq_zero()
        for k in range(KD):
            for tb in range(NTB):
                xk, rxk = xs5.next()
                kb.dma("sp", xk[:], src[k][:, tb * 512:(tb + 1) * 512], reads=[rsrc], writes=[rxk])
                ssq_accum(xk[:], rxk, tb * 512, 512)

    def phase_N(ps, src, rsrc, gcol, rg):
        for k in range(KD):
            for tb in range(NTBG):
                t0 = ps * TG + tb * 512
                xk, rxk = xs5.next()
                kb.dma("sp", xk[:], src[k][:, t0:t0 + 512], reads=[rsrc], writes=[rxk])
                kb.op("dve", lambda: nc.vector.scalar_tensor_tensor(out=hT[:, k, tb * 512:(tb + 1) * 512], in0=xk[:],
                                                                    scalar=gcol[:, k:k + 1], in1=rstd[:, t0:t0 + 512],
                                                                    op0=ALU.mult, op1=ALU.mult),
                      reads=[rxk, rg, r_rstd], writes=[hres[k]])

    def load_w(wdram, c0, M, nk):
        wt, rw = wsl.next()
        kb.dma("pool", wt[:, :nk, :M], wdram[:, c0:c0 + M].rearrange("(k p) c -> p k c", p=128), writes=[rw])
        return wt, rw

    def gemm_unit(bank, wt, rw, M, nk, tb):
        def fn():
            ins = None
            for k in range(nk):
                ins = nc.tensor.matmul(pb[bank][:M, :], lhsT=wt[:, k, :M], rhs=hT[:, k, tb * 512:(tb + 1) * 512],
                                       start=(k == 0), stop=(k == nk - 1))
            return ins
        kb.op("pe", fn, reads=[rw] + hres[:nk], writes=[pr[bank]])

    def stream_groups(wdram, groups, nk, unit_fn, inflight):
        loaded = []
        n = len(groups)

        def ld(i):
            loaded.append([load_w(wdram, c0, M, nk) + (M,) for (c0, M) in groups[i][1]])
        for i in range(min(inflight, n)):
            ld(i)
        for i in range(n):
            unit_fn(groups[i][0], loaded[i])
            if i + inflight < n:
                ld(i + inflight)

    def phase_P(l, ps):
        groups = []
        o = 0
        for h in range(c.HA): groups.append((("aq", h), [(o + h * 128, 128)]))
        o += c.AQ
        for h in range(c.HA): groups.append((("ak", h), [(o + h * 128, 128)]))
        o += c.AK
        for h in range(c.HA): groups.append((("av", h), [(o + h * 128, 128)]))
        o += c.AV
        for h in range(c.HB): groups.append((("bq", h), [(o + h * 128, 128)]))
        o += c.BQ
        groups.append((("bk", 0), [(o, 128)])); o += 128
        groups.append((("bv", 0), [(o, 128)])); o += 128
        for j in range(c.HI // 2): groups.append((("iq", j), [(o + j * 128, 128)]))
        o += c.IQ
        groups.append((("ikw", 0), [(o, c.IK + c.IW)]))

        def unit(tag, ws):
            kind, idx = tag
            wt, rw, M = ws[0]
            for tb in range(NTBG):
                b = next_bank(6)
                gemm_unit(b, wt, rw, M, KD, tb)
                t0 = ps * TG + tb * 512
                cols = slice(t0, t0 + 512)
                if kind in ("av", "bv", "iq"):
                    ob, rob = ob16.next()
                    kb.op("act", lambda: nc.scalar.copy(out=ob[:], in_=pb[b][:, :]), reads=[pr[b]], writes=[rob])
                    if kind == "av": dst, rd = vaT[idx][:, cols], R["vaT"]
                    elif kind == "bv": dst, rd = vbT[:, cols], R["vbT"]
                    else: dst, rd = qiT[idx][:, cols], R["qiT"]
                    kb.dma("sp", dst, ob[:], reads=[rob], writes=[rd])
                elif kind == "ikw":
                    ob, rob = ob16.next()
                    kb.op("act", lambda: nc.scalar.copy(out=ob[0:64, :], in_=pb[b][0:64, :]), reads=[pr[b]], writes=[rob])
                    kb.dma("sp", kiT[:, cols], ob[0:64, :], reads=[rob], writes=[R["kiT"]])
                    of, rof = f32a.next()
                    kb.op("act", lambda: nc.scalar.copy(out=of[64:64 + c.IW, :], in_=pb[b][64:64 + c.IW, :]),
                          reads=[pr[b]], writes=[rof])
                    kb.dma("sp", wiT[0:c.IW, cols], of[64:64 + c.IW, :], reads=[rof], writes=[R["wiT"]])
                else:
                    G = 64 if kind in ("aq", "ak") else 128
                    red, rred = (bd64_f, r_bd64) if G == 64 else (ones_f, r_onesf)
                    gi = {"aq": 0, "ak": 1, "bq": 2, "bk": 3}[kind]
                    sq, rsq = f32a.next()
                    kb.op("act", lambda: nc.scalar.activation(out=sq[:], in_=pb[b][:, :], func=AF.Square),
                          reads=[pr[b]], writes=[rsq])
                    sbk = 6
                    kb.op("pe", lambda: nc.tensor.matmul(pb[sbk][:, :], lhsT=red[:], rhs=sq[:], start=True, stop=True),
                          reads=[rred, rsq], writes=[pr[sbk]])
                    rs, rrs = f32b.next()
                    rsqrt_op(rs[:], rrs, pb[sbk][:, :], pr[sbk], G)
                    ob, rob = ob16.next()
                    kb.op("dve", lambda: nc.vector.scalar_tensor_tensor(out=ob[:], in0=pb[b][:, :], scalar=gs2[:, gi:gi + 1],
                                                                        in1=rs[:], op0=ALU.mult, op1=ALU.mult),
                          reads=[pr[b], rrs, r_gs2], writes=[rob])
                    if kind == "aq": dst, rd = qaT[idx][:, cols], R["qaT"]
                    elif kind == "ak": dst, rd = kaT[idx][:, cols], R["kaT"]
                    elif kind == "bq": dst, rd = qbT[idx][:, cols], R["qbT"]
                    else: dst, rd = kbT[:, cols], R["kbT"]
                    kb.dma("sp", dst, ob[:], reads=[rob], writes=[rd])

        stream_groups(w_in[l], groups, KD, unit, 3)

    def load_layer_params(l):
        lam_init = 0.8 - 0.6 * math.exp(-0.3 * l)
        kb.dma("sp", gA_sb[:], gA[l], writes=[r_gA])
        kb.dma("sp", gF_sb[:], gF[l], writes=[r_gF])
        kb.dma("sp", gsm_sb[:], gsm[l], writes=[r_gsm])
        kb.dma("sp", lq_sb[:], lqb[l], writes=[r_lq])
        kb.dma("sp", cw_sb[:], cwv[l], writes=[r_cw])
        kb.dma("sp", cb_sb[:], cbv[l], writes=[r_cb])
        sD = float(math.sqrt(c.D))
        kb.op("dve", lambda: nc.vector.tensor_scalar_mul(out=gA_sb[:], in0=gA_sb[:], scalar1=sD), reads=[r_gA], writes=[r_gA])
        kb.op("dve", lambda: nc.vector.tensor_scalar_mul(out=gF_sb[:], in0=gF_sb[:], scalar1=sD), reads=[r_gF], writes=[r_gF])
        facs = [1.0, 8.0, 1.0, math.sqrt(128.0), math.sqrt(128.0) * (1.0 - lam_init)]
        for i, f in enumerate(facs):
            kb.op("dve", lambda: nc.vector.tensor_scalar_mul(out=gs2[:, i:i + 1], in0=gsm_sb[:, i:i + 1], scalar1=float(f)),
                  reads=[r_gsm], writes=[r_gs2])
        kb.op("dve", lambda: nc.vector.tensor_tensor(out=lwork[:, 0:64], in0=lq_sb[:, 0:64], in1=lq_sb[:, 64:128], op=ALU.mult),
              reads=[r_lq], writes=[r_lwork])
        kb.op("dve", lambda: nc.vector.tensor_tensor(out=lwork[:, 64:128], in0=lq_sb[:, 128:192], in1=lq_sb[:, 192:256], op=ALU.mult),
              reads=[r_lq], writes=[r_lwork])
        kb.op("dve", lambda: nc.vector.reduce_sum(out=lsm[:, 0:1], in_=lwork[:, 0:64], axis=AX.X), reads=[r_lwork], writes=[r_lsm])
        kb.op("dve", lambda: nc.vector.reduce_sum(out=lsm[:, 1:2], in_=lwork[:, 64:128], axis=AX.X), reads=[r_lwork], writes=[r_lsm])
        kb.op("act", lambda: nc.scalar.activation(out=lsm[:, 2:4], in_=lsm[:, 0:2], func=AF.Exp), reads=[r_lsm], writes=[r_lsm])
        kb.op("dve", lambda: nc.vector.tensor_tensor(out=lsm[:, 4:5], in0=lsm[:, 3:4], in1=lsm[:, 2:3], op=ALU.subtract),
              reads=[r_lsm], writes=[r_lsm])
        kb.op("dve", lambda: nc.vector.tensor_scalar_add(out=lsm[:, 5:6], in0=lsm[:, 4:5], scalar1=float(-lam_init)),
              reads=[r_lsm], writes=[r_lsm])
    nlam = lsm[:, 5:6]

    def load_bias_tiles(hglob):
        bm, rbm = bms.next()
        kb.dma("pool", bm[:], btab[hglob], writes=[rbm])
        kb.op("dve", lambda: nc.vector.tensor_tensor(out=bm[:], in0=bm[:], in1=mask_f[:], op=ALU.add),
              reads=[rbm, r_maskf], writes=[rbm])
        return bm, rbm

    def transpose_v(src_dram, rsrc):
        vt, rvt = vTs.next()
        kb.dma("sp", vt[:], src_dram, reads=[rsrc], writes=[rvt])
        vh, rvh = vhs.next()
        for g in range(0, NT, 8):
            ng = min(8, NT - g)

            def fn():
                ins = None
                for i in range(ng):
                    ins = nc.tensor.transpose(out=pb7b[:, i * 128:(i + 1) * 128],
                                              in_=vt[:, (g + i) * 128:(g + i + 1) * 128], identity=ident_b[:])
                return ins
            kb.op("pe", fn, reads=[rvt, r_identb], writes=[pr[7]])
            kb.op("act", lambda: nc.scalar.copy(out=vh[:, g:g + ng, :],
                                                in_=pb7b[:, 0:ng * 128].rearrange("p (a b) -> p a b", b=128)),
                  reads=[pr[7]], writes=[rvh])
        return vh, rvh

    sbank = [0]

    def next_sbank():
        sbank[0] ^= 1
        return sbank[0]

    SB = [0, 1, 7]
    LA = 2
    ticker = [None]

    def tick():
        g = ticker[0]
        if g is not None:
            try:
                next(g)
            except StopIteration:
                ticker[0] = None

    def attn_block(hglob, qb, q_ap, rq, k_ap, rk, vh, rvh, bm, rbm, obank, dbank, use_mq):
        q0 = qb * 512
        nkt = 4 * qb + 4

        def emit_s(kt):
            cs = max(0, kt * 128 - q0)
            near = kt >= 4 * qb - 1
            sbank[0] = (sbank[0] + 1) % len(SB)
            sbk = SB[sbank[0]]
            cls = kt - 4 * qb + 1

            def fn_s():
                ins = nc.tensor.matmul(pb[sbk][:, cs:512], lhsT=k_ap[:, kt * 128:(kt + 1) * 128],
                                       rhs=q_ap[:, q0 + cs:q0 + 512], start=True, stop=(not near and not use_mq))
                if use_mq:
                    for qi in range(cs // 128, 4):
                        last = (qi == 3) and not near
                        ins = nc.tensor.matmul(pb[sbk][:, qi * 128:(qi + 1) * 128],
                                               lhsT=mq_all[:, 4 * qb + qi, kt * 128:(kt + 1) * 128],
                                               rhs=ident_b[:], start=False, stop=last)
                if near:
                    ins = nc.tensor.matmul(pb[sbk][:, cs:512], lhsT=ident_b[:], rhs=bm[:, cls, cs:512],
                                           start=False, stop=True)
                return ins
            rd = [rq, rk, r_identb]
            if near: rd.append(rbm)
            if use_mq: rd += [r_mq[4 * qb + qi] for qi in range(cs // 128, 4)]
            kb.op("pe", fn_s, reads=rd, writes=[pr[sbk]])
            eb, reb = ebs.next()
            if near:
                kb.op("act", lambda: nc.scalar.activation(out=eb[:, cs:512], in_=pb[sbk][:, cs:512], func=AF.Exp),
                      reads=[pr[sbk]], writes=[reb])
            else:
                kb.op("act", lambda: nc.scalar.activation(out=eb[:, cs:512], in_=pb[sbk][:, cs:512], func=AF.Exp,
                                                          bias=relb_sb[:, hglob:hglob + 1]),
                      reads=[pr[sbk], r_relb], writes=[reb])
            return (kt, cs, eb, reb)

        def emit_pv(st):
            kt, cs, eb, reb = st

            def fn_pv():
                nc.tensor.matmul(pb[obank][:, cs:512], lhsT=vh[:, kt, :], rhs=eb[:, cs:512],
                                 start=(kt == 0), stop=(kt == nkt - 1))
                return nc.tensor.matmul(pb[dbank][:, cs:512], lhsT=ones_b[:], rhs=eb[:, cs:512],
                                        start=(kt == 0), stop=(kt == nkt - 1))
            kb.op("pe", fn_pv, reads=[rvh, reb, r_onesb], writes=[pr[obank], pr[dbank]])

        pend = []
        for kt in range(nkt + LA):
            if kt < nkt:
                pend.append(emit_s(kt))
            if kt >= LA:
                emit_pv(pend.pop(0))
                tick()

    def phase_A_diff():
        for h in range(c.HA):
            qh, rqh = qhs.next(); kh, rkh = khs.next()
            kb.dma("sp", qh[:], qaT[h], reads=[R["qaT"]], writes=[rqh])
            kb.dma("sp", kh[:], kaT[h], reads=[R["kaT"]], writes=[rkh])
            bm, rbm = load_bias_tiles(h)
            vh, rvh = transpose_v(vaT[h], R["vaT"])
            for qb in range(NTB):
                attn_block(h, qb, qh[0:64, :], rqh, kh[0:64, :], rkh, vh, rvh, bm, rbm, 2, 4, False)
                attn_block(h, qb, qh[64:128, :], rqh, kh[64:128, :], rkh, vh, rvh, bm, rbm, 3, 5, False)
                r1, rr1 = f32a.next(); r2, rr2 = f32a.next()
                kb.op("dve", lambda: nc.vector.reciprocal(out=r1[:], in_=pb[4][:, :]), reads=[pr[4]], writes=[rr1])
                kb.op("dve", lambda: nc.vector.reciprocal(out=r2[:], in_=pb[5][:, :]), reads=[pr[5]], writes=[rr2])
                kb.op("dve", lambda: nc.vector.tensor_tensor(out=r1[:], in0=pb[2][:, :], in1=r1[:], op=ALU.mult),
                      reads=[pr[2], rr1], writes=[rr1])
                kb.op("dve", lambda: nc.vector.tensor_tensor(out=r2[:], in0=pb[3][:, :], in1=r2[:], op=ALU.mult),
                      reads=[pr[3], rr2], writes=[rr2])
                if h == 0 and qb == 0:
                    dbg_dump("t1", r1[:], rr1, [128, 512]); dbg_dump("t2", r2[:], rr2, [128, 512]); dbg_dump("lsm", lsm[:], r_lsm, [128, 8])
                oa, roa = f32b.next()
                kb.op("dve", lambda: nc.vector.scalar_tensor_tensor(out=oa[:], in0=r2[:], scalar=nlam, in1=r1[:],
                                                                    op0=ALU.mult, op1=ALU.add),
                      reads=[rr1, rr2, r_lsm], writes=[roa])
                if h == 0 and qb == 0:
                    dbg_dump("oa", oa[:], roa, [128, 512])
                sq, rsq = f32b.next()
                kb.op("act", lambda: nc.scalar.activation(out=sq[:], in_=oa[:], func=AF.Square), reads=[roa], writes=[rsq])
                kb.op("pe", lambda: nc.tensor.matmul(pb[6][:, :], lhsT=ones_f[:], rhs=sq[:], start=True, stop=True),
                      reads=[r_onesf, rsq], writes=[pr[6]])
                rs, rrs = f32a.next()
                rsqrt_op(rs[:], rrs, pb[6][:, :], pr[6], 128)
                ob, rob = ob16.next()
                kb.op("dve", lambda: nc.vector.scalar_tensor_tensor(out=ob[:], in0=oa[:], scalar=gs2[:, 4:5], in1=rs[:],
                                                                    op0=ALU.mult, op1=ALU.mult),
                      reads=[roa, rrs, r_gs2], writes=[rob])
                kb.dma("sp", mixT[h][:, qb * 512:(qb + 1) * 512], ob[:], reads=[rob], writes=[R["mixT"]])

    def index_tile(qt, ch):
        sc, rsc = sc_ch[ch]
        work, r_work = work_ch[ch]
        m8s_, m8cs_ = m8s_ch[ch], m8cs_ch[ch]
        qiv = qiT.rearrange("b (two p) t -> p (b two) t", two=2)
        Lk = (qt + 1) * 128
        qi_t, rqi = qis.t[ch], qis.r[ch]
        kb.dma("sp", qi_t[:], qiv[:, :, qt * 128:(qt + 1) * 128], reads=[R["qiT"]], writes=[rqi])
        for kbk in range((Lk + 511) // 512):
            n = min(512, Lk - kbk * 512)
            for hi in range(c.HI):
                b = 6
                kb.op("pe", lambda: nc.tensor.matmul(pb[b][:, :n], lhsT=qi_t[:, hi, :], rhs=ki_sb[:, kbk * 512:kbk * 512 + n],
                                                     start=True, stop=True),
                      reads=[rqi, r_kisb], writes=[pr[b]])
                rl, rrl = f32b.next()
                kb.op("act", lambda: nc.scalar.activation(out=rl[:, :n], in_=pb[b][:, :n], func=AF.Relu),
                      reads=[pr[b]], writes=[rrl])
                dst = sc[:, kbk * 512:kbk * 512 + n]
                if hi == 0:
                    kb.op("dve", lambda: nc.vector.tensor_scalar_mul(out=dst, in0=rl[:, :n], scalar1=wi_tok[:, qt, 0:1]),
                          reads=[rrl, r_witok], writes=[rsc])
                else:
                    kb.op("dve", lambda: nc.vector.scalar_tensor_tensor(out=dst, in0=rl[:, :n], scalar=wi_tok[:, qt, hi:hi + 1],
                                                                        in1=dst, op0=ALU.mult, op1=ALU.add),
                          reads=[rrl, r_witok, rsc], writes=[rsc])
                yield
        kb.op("dve", lambda: nc.vector.memset(sc[0:64, Lk - 64:Lk], NINF), reads=[rsc], writes=[rsc])
        if qt == 2:
            dbg_dump("sc2", sc[:, :Lk], rsc, [128, Lk])
        if Lk > c.TOPK:
            m8c = None
            nr = c.TOPK // 8
            for r in range(nr):
                src = sc if r == 0 else work
                rsrc = rsc if r == 0 else r_work
                m8, rm8 = m8s_.next()
                kb.op("dve", lambda: nc.vector.max(out=m8[:], in_=src[:, :Lk]), reads=[rsrc], writes=[rm8], strict=True)
                m8c, rm8c = m8cs_.next()
                kb.op("act", lambda: nc.scalar.copy(out=m8c[:], in_=m8[:]), reads=[rm8], writes=[rm8c])
                yield
                if r < nr - 1:
                    kb.op("dve", lambda: nc.vector.match_replace(out=work[:, :Lk], in_to_replace=m8c[:], in_values=src[:, :Lk],
                                                                 imm_value=NINF),
                          reads=[rsrc, rm8c], writes=[r_work], strict=True)
            thr_ap, rthr = m8c[:, 7:8], rm8c
        else:
            thr_ap, rthr = thr_c[:, 0:1], r_thrc
        kb.op("dve", lambda: nc.vector.tensor_scalar(out=mq_all[:, qt, :Lk], in0=sc[:, :Lk], scalar1=thr_ap, scalar2=NEG,
                                                     op0=ALU.is_lt, op1=ALU.mult),
              reads=[rsc, rthr], writes=[r_mq[qt]], strict=True)
        if qt == 2:
            dbg_dump("m8", m8c[:], rm8c, [128, 8])
            dbg_dump("mq2", mq_all[:, qt, :Lk], r_mq[qt], [128, Lk], BF16)
        yield

    def index_chain(ch):
        for qt in range(ch, NT, 2):
            yield from index_tile(qt, ch)

    def phase_A_index():
        kb.dma("sp", ki_sb[:], kiT, reads=[R["kiT"]], writes=[r_kisb])
        for tb in range(NTB):
            wt_, rwt_ = f32a.next()
            kb.dma("sp", wt_[0:c.IW, :], wiT[0:c.IW, tb * 512:(tb + 1) * 512], reads=[R["wiT"]], writes=[rwt_])
            for t4 in range(4):
                t = tb * 4 + t4
                kb.op("pe", lambda: nc.tensor.transpose(out=pb[6][:, 0:c.IW], in_=wt_[0:c.IW, t4 * 128:(t4 + 1) * 128],
                                                        identity=ident_f[0:c.IW, 0:c.IW]),
                      reads=[rwt_, r_identf], writes=[pr[6]])
                kb.op("act", lambda: nc.scalar.copy(out=wi_tok[:, t, 0:c.IW], in_=pb[6][:, 0:c.IW]),
                      reads=[pr[6]], writes=[r_witok])
        yield
        gens = [index_chain(0), index_chain(1)]
        while gens:
            for g in list(gens):
                try:
                    next(g)
                except StopIteration:
                    gens.remove(g)
            yield

    def phase_A_dsa():
        kb.dma("sp", kb_sb[:], kbT, reads=[R["kbT"]], writes=[r_kbsb])
        vh, rvh = transpose_v(vbT, R["vbT"])
        for h in range(c.HB):
            qh, rqh = qhs.next()
            kb.dma("sp", qh[:], qbT[h], reads=[R["qbT"]], writes=[rqh])
            bm, rbm = load_bias_tiles(c.HA + h)
            for qb in range(NTB):
                ob_, db_ = (2, 4) if (qb % 2 == 0) else (3, 5)
                attn_block(c.HA + h, qb, qh, rqh, kb_sb, r_kbsb, vh, rvh, bm, rbm, ob_, db_, True)
                r1, rr1 = f32a.next()
                kb.op("dve", lambda: nc.vector.reciprocal(out=r1[:], in_=pb[db_][:, :]), reads=[pr[db_]], writes=[rr1])
                ob, rob = ob16.next()
                kb.op("dve", lambda: nc.vector.tensor_tensor(out=ob[:], in0=pb[ob_][:, :], in1=r1[:], op=ALU.mult),
                      reads=[pr[ob_], rr1], writes=[rob])
                kb.dma("sp", mixT[c.HA + h][:, qb * 512:(qb + 1) * 512], ob[:], reads=[rob], writes=[R["mixT"]])

    def residual_epilogue(b, src, rsrc, dst, rdst, cb, t0, accum):
        cols = slice(t0, t0 + 512)
        xin, rxin = xs5.next()
        kb.dma("sp", xin[:], src[cb][:, cols], reads=[rsrc], writes=[rxin])
        xn, rxn = f32a.next()
        kb.op("dve", lambda: nc.vector.tensor_tensor(out=xn[:], in0=pb[b][:, :], in1=xin[:], op=ALU.add),
              reads=[pr[b], rxin], writes=[rxn])
        kb.dma("sp", dst[cb][:, cols], xn[:], reads=[rxn], writes=[rdst])
        if accum:
            ssq_accum(xn[:], rxn, t0, 512)

    def phase_O(l, ps, src, rsrc):
        for k in range(KM):
            kb.dma("sp", hT[:, k, :], mixT[k][:, ps * TG:(ps + 1) * TG], reads=[R["mixT"]], writes=[hres[k]])
        groups = [(cb, [(cb * 128, 128)]) for cb in range(KD)]

        def unit(cb, ws):
            wt, rw, M = ws[0]
            for tb in range(NTBG):
                b = next_bank(6)
                gemm_unit(b, wt, rw, M, KM, tb)
                residual_epilogue(b, src, rsrc, xa, R["xa"], cb, ps * TG + tb * 512, True)
        stream_groups(w_out[l], groups, KM, unit, 3)

    def phase_F1(l, ps):
        groups = [(fc, [(fc * 128, 128), (c.DFF + fc * 128, 128)]) for fc in range(NFC)]

        def unit(fc, ws):
            (wg, rwg, _), (wu, rwu, _) = ws
            prev = None
            for tb in range(NTBG):
                bg = next_bank(6)
                gemm_unit(bg, wg, rwg, 128, KD, tb)
                bu = next_bank(6)
                gemm_unit(bu, wu, rwu, 128, KD, tb)
                gs, rgs = gsl.next()
                kb.op("act", lambda: nc.scalar.copy(out=gs[:, 2:514], in_=pb[bg][:, :]), reads=[pr[bg]], writes=[rgs])
                if tb == 0:
                    if ps == 0:
                        kb.op("dve", lambda: nc.vector.memset(gs[:, 0:2], 0.0), reads=[rgs], writes=[rgs])
                    else:
                        kb.op("dve", lambda: nc.vector.tensor_copy(out=gs[:, 0:2], in_=gcar[:, fc, :]), reads=[r_gcar, rgs], writes=[rgs])
                else:
                    pg, rpg = prev
                    kb.op("dve", lambda: nc.vector.tensor_copy(out=gs[:, 0:2], in_=pg[:, 512:514]), reads=[rpg, rgs], writes=[rgs])
                if tb == NTBG - 1 and ps < NPASS - 1:
                    kb.op("dve", lambda: nc.vector.tensor_copy(out=gcar[:, fc, :], in_=gs[:, 512:514]), reads=[rgs], writes=[r_gcar])
                prev = (gs, rgs)
                a, ra = f32a.next()
                kb.op("dve", lambda: nc.vector.tensor_scalar(out=a[:], in0=gs[:, 2:514], scalar1=cw_sb[:, fc, 2:3],
                                                             scalar2=cb_sb[:, fc:fc + 1], op0=ALU.mult, op1=ALU.add),
                      reads=[rgs, r_cw, r_cb], writes=[ra])
                kb.op("dve", lambda: nc.vector.scalar_tensor_tensor(out=a[:], in0=gs[:, 1:513], scalar=cw_sb[:, fc, 1:2], in1=a[:],
                                                                    op0=ALU.mult, op1=ALU.add),
                      reads=[rgs, r_cw, ra], writes=[ra])
                kb.op("dve", lambda: nc.vector.scalar_tensor_tensor(out=a[:], in0=gs[:, 0:512], scalar=cw_sb[:, fc, 0:1], in1=a[:],
                                                                    op0=ALU.mult, op1=ALU.add),
                      reads=[rgs, r_cw, ra], writes=[ra])
                s, rs_ = f32b.next()
                kb.op("act", lambda: nc.scalar.activation(out=s[:], in_=a[:], func=AF.Silu), reads=[ra], writes=[rs_])
                ob, rob = ob16.next()
                kb.op("dve", lambda: nc.vector.tensor_tensor(out=ob[:], in0=pb[bu][:, :], in1=s[:], op=ALU.mult),
                      reads=[pr[bu], rs_], writes=[rob])
                t0 = ps * TG + tb * 512
                kb.dma("sp", actT[fc][:, t0:t0 + 512], ob[:], reads=[rob], writes=[R["actT"]])
        stream_groups(w_gu[l], groups, KD, unit, 2)

    def phase_F2(l, dst, rdst, accum):
        if accum:
            ssq_zero()
        wl = w_dn[l]
        for tg in range(NTB // NTBF):
            for cg in range(KD // CCF):
                c0 = cg * CCF * 128
                wd = None
                for fc0 in range(0, NFC, AG):
                    if fc0 % GF == 0:
                        ng = min(GF, NFC - fc0)
                        wd, rwd = wds.next()
                        kb.dma("pool", wd[:, :ng, :], wl[fc0 * 128:(fc0 + ng) * 128, c0:c0 + CCF * 128].rearrange("(f p) c -> p f c", p=128),
                               writes=[rwd])
                    na = min(AG, NFC - fc0)
                    at, rat = asl.next()
                    kb.dma("sp", at[:, :na, :], actT[fc0:fc0 + na, :, tg * TF:(tg + 1) * TF].rearrange("f p t -> p f t"),
                           reads=[R["actT"]], writes=[rat])
                    for a in range(na):
                        fc = fc0 + a

                        def fn():
                            ins = None
                            for cc in range(CCF):
                                for tb in range(NTBF):
                                    ins = nc.tensor.matmul(pb[cc * NTBF + tb][:, :], lhsT=wd[:, fc % GF, cc * 128:(cc + 1) * 128],
                                                           rhs=at[:, a, tb * 512:(tb + 1) * 512], start=(fc == 0), stop=(fc == NFC - 1))
                            return ins
                        kb.op("pe", fn, reads=[rwd, rat], writes=[pr[i] for i in range(CCF * NTBF)])
                for cc in range(CCF):
                    for tb in range(NTBF):
                        residual_epilogue(cc * NTBF + tb, xa, R["xa"], dst, rdst, cg * CCF + cc, tg * TF + tb * 512, accum)

    cur, rcur = xT, R["xT"]
    for l in range(L):
        load_layer_params(l)
        if l == 0:
            phase_S(cur, rcur)
        finish_norm(c.D)
        for ps in range(NPASS):
            phase_N(ps, cur, rcur, gA_sb, r_gA)
            phase_P(l, ps)
        barrier()
        ticker[0] = phase_A_index()
        phase_A_diff()
        if ticker[0] is not None:
            for _ in ticker[0]:
                pass
            ticker[0] = None
        phase_A_dsa()
        barrier()
        ssq_zero()
        for ps in range(NPASS):
            phase_O(l, ps, cur, rcur)
        finish_norm(c.D)
        for ps in range(NPASS):
            phase_N(ps, xa, R["xa"], gF_sb, r_gF)
            phase_F1(l, ps)
        last = (l == L - 1)
        nxt, rnxt = (yT, R["yT"]) if last else (xb, R["xb"])
        phase_F2(l, nxt, rnxt, not last)
        cur, rcur = nxt, rnxt
    kb.final_wait("sp", R["yT"])
    return nc, kb


def _rel_bucket_np(rel):
    import jax
    import jax.numpy as jnp
    cpu = jax.devices("cpu")[0]
    with jax.default_device(cpu):
        rel = jnp.asarray(rel, dtype=jnp.int32)
        nb = 32 // 2
        max_exact = nb // 2
        bucket = jnp.where(rel > 0, nb, 0)
        n = jnp.abs(rel)
        nf = jnp.maximum(n, 1).astype(jnp.float32)
        large = max_exact + (jnp.log(nf / max_exact) / math.log(128 / max_exact) * (nb - max_exact)).astype(jnp.int32)
        large = jnp.minimum(large, nb - 1)
        out = bucket + jnp.where(n < max_exact, n, large)
        return np.asarray(out)


def prep_shared(cfg, inputs):
    c = cfg
    L = c.DEPTH
    f = lambda a: np.ascontiguousarray(np.asarray(a, dtype=np.float32))
    col = lambda v, k: np.ascontiguousarray(v.reshape(k, 128).T)
    sh = {}
    sh["w_in"] = f(inputs["w_in"]); sh["w_out"] = f(inputs["w_out"])
    sh["w_gu"] = f(inputs["w_gate_up"]); sh["w_dn"] = f(inputs["w_down"])
    an = f(inputs["attn_norm"]); fn = f(inputs["ffn_norm"])
    sh["gA"] = np.stack([col(an[l], c.KD) for l in range(L)])
    sh["gF"] = np.stack([col(fn[l], c.KD) for l in range(L)])
    gsm = np.zeros((L, 128, 8), np.float32)
    aq = f(inputs["a_q_norm"]); ak = f(inputs["a_k_norm"]); ao = f(inputs["a_out_norm"])
    bq = f(inputs["b_q_norm"]); bk = f(inputs["b_k_norm"])
    for l in range(L):
        gsm[l, :, 0] = np.concatenate([aq[l], aq[l]])
        gsm[l, :, 1] = np.concatenate([ak[l], ak[l]])
        gsm[l, :, 2] = bq[l]; gsm[l, :, 3] = bk[l]; gsm[l, :, 4] = ao[l]
    sh["gsm"] = gsm
    lq = f(inputs["lambda_qk"]).reshape(L, 1, 256)
    sh["lqb"] = np.ascontiguousarray(np.broadcast_to(lq, (L, 128, 256)))
    cw = f(inputs["conv_w"]); cb = f(inputs["conv_b"])
    sh["cwv"] = np.ascontiguousarray(cw.reshape(L, 3, c.NFC, 128).transpose(0, 3, 2, 1))
    sh["cbv"] = np.ascontiguousarray(cb.reshape(L, c.NFC, 128).transpose(0, 2, 1))
    rb = f(inputs["rel_bias"])
    sh["relb15"] = np.ascontiguousarray(np.broadcast_to(rb[15][None, :], (128, c.NH)))
    p = np.arange(128)[:, None, None]
    dl = (np.arange(5) * 128 - 128)[None, :, None]
    j = np.arange(512)[None, None, :]
    d = dl + p - j
    bidx = _rel_bucket_np(d)
    sh["btab"] = np.ascontiguousarray(rb[bidx].transpose(3, 0, 1, 2))
    vis = (dl // 64 + p // 64) <= (j // 64)
    sh["maskd"] = np.where(vis, 0.0, NEG).astype(np.float32)
    sh["identd"] = np.eye(128, dtype=np.float32)
    return sh


def run(cfg, inputs, debug=False):
    c = cfg
    sh = prep_shared(c, inputs)
    x = np.asarray(inputs["x"], dtype=np.float32)
    B = x.shape[0]
    in_maps = []
    for b in range(B):
        m = dict(sh)
        m["xT"] = np.ascontiguousarray(x[b].T.reshape(c.KD, 128, c.T))
        in_maps.append(m)
    nc, kb = build_program(c, debug=debug)
    res = run_bass_kernel_spmd(nc, in_maps, core_ids=list(range(B)))
    out = np.empty_like(x)
    for b in range(B):
        out[b] = res.results[b]["yT"].reshape(c.D, c.T).T
    if debug:
        return out, res
    return out


def kernel(**inputs):
    return run(FULL, inputs)
```

```python
import math
import numpy as np
import concourse.bass as bass
import concourse.mybir as mybir
from concourse.bass_utils import run_bass_kernel_spmd

F32 = mybir.dt.float32
BF16 = mybir.dt.bfloat16
AF = mybir.ActivationFunctionType
ALU = mybir.AluOpType
AX = mybir.AxisListType

EPS = 1e-6
NEG = -30000.0
NINF = -1.0e30


class Cfg:
    def __init__(self, D=4096, S=2048, HA=16, HB=16, HI=16, DFF=11008, DEPTH=2, BATCH=4, TOPK_MAX=256, TG=1024):
        self.D, self.S, self.HA, self.HB, self.HI, self.DFF, self.DEPTH, self.BATCH = D, S, HA, HB, HI, DFF, DEPTH, BATCH
        self.T = S
        self.TG = TG
        self.NPASS = S // TG
        self.NTBG = TG // 512
        self.KD = D // 128
        self.NTB = S // 512
        self.NT = S // 128
        self.AQ = HA * 128; self.AK = HA * 128; self.AV = HA * 128
        self.BQ = HB * 128; self.BK = 128; self.BV = 128
        self.IQ = HI * 64; self.IK = 64; self.IW = HI
        self.DIN = self.AQ + self.AK + self.AV + self.BQ + self.BK + self.BV + self.IQ + self.IK + self.IW
        self.DMIX = HA * 128 + HB * 128
        self.KM = self.DMIX // 128
        self.NFC = DFF // 128
        self.NH = HA + HB
        self.TOPK = min(TOPK_MAX, S // 4)
        self.KMAX = max(self.KD, self.KM)
        assert DFF % 128 == 0 and D % 128 == 0 and S % 512 == 0 and HI % 2 == 0
        assert self.IK + self.IW <= 128


FULL = Cfg()


class Res:
    __slots__ = ("name", "w", "r", "dsem", "dval", "multi")

    def __init__(self, name, multi=False):
        self.name = name
        self.w = None
        self.r = {}
        self.dsem = None
        self.dval = 0
        self.multi = multi


class KB:
    def __init__(self, nc):
        self.nc = nc
        self.engs = {"pe": nc.tensor, "act": nc.scalar, "dve": nc.vector, "pool": nc.gpsimd, "sp": nc.sync}
        self.esem = {}
        self.ecnt = {}
        for n in ("pe", "act", "dve", "pool"):
            self.esem[n] = nc.semaphore("es_" + n).__enter__()
            self.ecnt[n] = 0
        self.waited = {n: {} for n in self.engs}
        self.nsem = 4
        self.dres = []

    def sb(self, name, shape, dt):
        t = self.nc.sbuf_tensor(name, list(shape), dt).__enter__()
        return t

    def _wait(self, eng, ev, is_dma):
        sem, val, owner = ev
        if owner == eng and not is_dma:
            return
        key = id(sem)
        if self.waited[eng].get(key, 0) >= val:
            return
        self.engs[eng].wait_ge(sem, val)
        self.waited[eng][key] = val

    def _deps(self, eng, reads, writes, is_dma):
        for r in reads:
            if r.w is not None:
                self._wait(eng, r.w, is_dma)
        for w in writes:
            if w.w is not None and not w.multi:
                self._wait(eng, w.w, is_dma)
            for ev in list(w.r.values()):
                self._wait(eng, ev, is_dma)

    def op(self, eng, fn, reads=(), writes=(), strict=False):
        self._deps(eng, reads, writes, strict or eng in ("dve", "act"))
        ins = fn()
        self.ecnt[eng] += 1
        sem = self.esem[eng]
        ins.then_inc(sem, 1)
        ev = (sem, self.ecnt[eng], eng)
        for r in reads:
            r.r[id(sem)] = ev
        for w in writes:
            w.w = ev
            if not w.multi:
                w.r = {}

    def dma(self, q, out, in_, reads=(), writes=()):
        self._deps(q, reads, writes, True)
        w0 = writes[0]
        if w0.dsem is None:
            w0.dsem = self.nc.semaphore("ds_" + w0.name).__enter__()
            self.nsem += 1
            self.dres.append(w0)
        ins = self.engs[q].dma_start(out=out, in_=in_)
        ins.then_inc(w0.dsem, 16)
        w0.dval += 16
        ev = (w0.dsem, w0.dval, None)
        for r in reads:
            r.r[id(w0.dsem)] = ev
        for w in writes:
            w.w = ev
            if not w.multi:
                w.r = {}

    def final_wait(self, eng, res):
        if res.w is not None:
            self._wait(eng, res.w, True)


class Slots:
    def __init__(self, name, views):
        self.t = list(views)
        self.r = [Res(f"{name}{i}") for i in range(len(views))]
        self.i = -1
        self.n = len(views)

    def next(self):
        self.i = (self.i + 1) % self.n
        return self.t[self.i], self.r[self.i]


class Arena:
    def __init__(self, kb, name, nbytes):
        self.n2 = (nbytes + 1) // 2
        self.t = kb.sb(name, [128, self.n2], BF16)
        self.off = 0

    def reset(self):
        self.off = 0

    def alloc(self, shape, dt, parts=128):
        esz = 4 if dt == F32 else 2
        n = 1
        for s in shape[1:]:
            n *= s
        nb = n * esz
        nb = (nb + 63) // 64 * 64
        o2 = self.off // 2
        assert o2 + nb // 2 <= self.n2, ("arena overflow", self.off, nb, self.n2 * 2)
        v = self.t[0:shape[0], o2:o2 + (n * esz) // 2]
        if dt == F32:
            v = v.bitcast(F32)
        if len(shape) == 3:
            v = v.rearrange("p (a b) -> p a b", b=shape[2])
        self.off += nb
        return v


def build_program(cfg, debug=False):
    c = cfg
    nc = bass.Bass("TRN2", target_bir_lowering=False)
    kb = KB(nc)
    T, KD, KM, NTB, NT, NFC = c.T, c.KD, c.KM, c.NTB, c.NT, c.NFC
    TG, NPASS, NTBG = c.TG, c.NPASS, c.NTBG
    L = c.DEPTH

    def din(name, shape, dt=F32):
        return nc.dram_tensor(name, list(shape), dt, kind="ExternalInput").ap()

    def dint(name, shape, dt):
        kind = "ExternalOutput" if debug else "Internal"
        return nc.dram_tensor(name, list(shape), dt, kind=kind).ap()

    xT = din("xT", [KD, 128, T])
    w_in = din("w_in", [L, c.D, c.DIN])
    w_out = din("w_out", [L, c.DMIX, c.D])
    w_gu = din("w_gu", [L, c.D, 2 * c.DFF])
    w_dn = din("w_dn", [L, c.DFF, c.D])
    gA = din("gA", [L, 128, KD])
    gF = din("gF", [L, 128, KD])
    gsm = din("gsm", [L, 128, 8])
    lqb = din("lqb", [L, 128, 256])
    cwv = din("cwv", [L, 128, NFC, 3])
    cbv = din("cbv", [L, 128, NFC])
    relb15 = din("relb15", [128, c.NH])
    btab = din("btab", [c.NH, 128, 5, 512])
    identd = din("identd", [128, 128])
    maskd = din("maskd", [128, 5, 512])
    yT = nc.dram_tensor("yT", [KD, 128, T], F32, kind="ExternalOutput").ap()

    xa = dint("xa", [KD, 128, T], F32)
    xb = dint("xb", [KD, 128, T], F32)
    qaT = dint("qaT", [c.HA, 128, T], BF16)
    kaT = dint("kaT", [c.HA, 128, T], BF16)
    vaT = dint("vaT", [c.HA, 128, T], BF16)
    qbT = dint("qbT", [c.HB, 128, T], BF16)
    kbT = dint("kbT", [128, T], BF16)
    vbT = dint("vbT", [128, T], BF16)
    qiT = dint("qiT", [c.HI // 2, 128, T], BF16)
    kiT = dint("kiT", [64, T], BF16)
    wiT = dint("wiT", [c.IW, T], F32)
    mixT = dint("mixT", [KM, 128, T], BF16)
    actT = dint("actT", [NFC, 128, T], BF16)
    R = {n: Res(n, multi=True) for n in ("xT", "xa", "xb", "yT", "qaT", "kaT", "vaT", "qbT", "kbT", "vbT",
                                          "qiT", "kiT", "wiT", "mixT", "actT")}

    pb = [nc.psum_tensor(f"pb{i}", [128, 512], F32).__enter__() for i in range(8)]
    pr = [Res(f"pb{i}") for i in range(8)]
    pb7b = pb[7][:].bitcast(BF16)

    hbytes = max(c.KMAX * TG * 2, NT * T * 2)
    big = Arena(kb, "big", hbytes)
    hT = big.alloc([128, c.KMAX, TG], BF16); big.reset()
    mq_all = big.alloc([128, NT, T], BF16); big.reset()
    hres = [Res(f"hT{k}") for k in range(c.KMAX)]
    r_mq = [Res(f"mq{i}") for i in range(NT)]
    ident_f = kb.sb("ident_f", [128, 128], F32); r_identf = Res("ident_f")
    ident_b = kb.sb("ident_b", [128, 128], BF16); r_identb = Res("ident_b")
    ones_f = kb.sb("ones_f", [128, 128], F32); r_onesf = Res("ones_f")
    ones_b = kb.sb("ones_b", [128, 128], BF16); r_onesb = Res("ones_b")
    bd64_f = kb.sb("bd64_f", [128, 128], F32); r_bd64 = Res("bd64_f")
    mask_f = kb.sb("mask_f", [128, 5, 512], F32); r_maskf = Res("mask_f")
    relb_sb = kb.sb("relb_sb", [128, c.NH], F32); r_relb = Res("relb")
    thr_c = kb.sb("thr_c", [128, 1], F32); r_thrc = Res("thr_c")
    epsb = kb.sb("epsb", [128, 4], F32); r_epsb = Res("epsb")
    r_ssq = Res("ssq_part"); r_rstd = Res("rstd")
    gA_sb = kb.sb("gA_sb", [128, KD], F32); r_gA = Res("gA")
    gF_sb = kb.sb("gF_sb", [128, KD], F32); r_gF = Res("gF")
    gsm_sb = kb.sb("gsm_sb", [128, 8], F32); r_gsm = Res("gsm")
    gs2 = kb.sb("gs2", [128, 8], F32); r_gs2 = Res("gs2")
    lq_sb = kb.sb("lq_sb", [128, 256], F32); r_lq = Res("lq")
    lwork = kb.sb("lwork", [128, 128], F32); r_lwork = Res("lwork")
    lsm = kb.sb("lsm", [128, 8], F32); r_lsm = Res("lsm")
    cw_sb = kb.sb("cw_sb", [128, NFC, 3], F32); r_cw = Res("cw")
    cb_sb = kb.sb("cb_sb", [128, NFC], F32); r_cb = Res("cb")
    gcar = kb.sb("gcar", [128, NFC, 2], F32); r_gcar = Res("gcar")
    f32a = Slots("f32a", [kb.sb(f"f32a{i}", [128, 512], F32) for i in range(4)])
    f32b = Slots("f32b", [kb.sb(f"f32b{i}", [128, 512], F32) for i in range(4)])
    ob16 = Slots("ob16", [kb.sb(f"ob16{i}", [128, 512], BF16) for i in range(4)])
    m8s_ch = [Slots(f"m8s{j}", [kb.sb(f"m8s{j}_{i}", [128, 8], F32) for i in range(2)]) for j in range(2)]
    m8cs_ch = [Slots(f"m8cs{j}", [kb.sb(f"m8cs{j}_{i}", [128, 8], F32) for i in range(2)]) for j in range(2)]

    GF = 8
    NTBF = min(2, NTB)
    CCF = min(KD, 8 // NTBF)
    TF = NTBF * 512
    AG = 2
    g_bytes = 4 * c.KMAX * 256 + 3 * GF * CCF * 256 + 4 * AG * TF * 2 + 3 * 2112 + 3 * 2048 + 1024 + 2 * T * 4
    a_bytes = 8 * T * 2 + 2 * 5120 + 5 * 1024 + 2 * T * 2 + 1024 + 2 * c.HI * 256 + 4 * T * 4 + 2048
    ar = Arena(kb, "arena", max(g_bytes, a_bytes))
    wsl = Slots("wsl", [ar.alloc([128, c.KMAX, 128], BF16) for _ in range(4)])
    wds = Slots("wds", [ar.alloc([128, GF, CCF * 128], BF16) for _ in range(3)])
    asl = Slots("asl", [ar.alloc([128, AG, TF], BF16) for _ in range(4)])
    gsl = Slots("gsl", [ar.alloc([128, 514], F32) for _ in range(3)])
    xs5 = Slots("xs5", [ar.alloc([128, 512], F32) for _ in range(3)])
    ssq_part = ar.alloc([128, T], F32)
    rstd = ar.alloc([128, T], F32)
    ar.reset()
    qhs = Slots("qhs", [ar.alloc([128, T], BF16) for _ in range(2)])
    khs = Slots("khs", [ar.alloc([128, T], BF16) for _ in range(2)])
    vTs = Slots("vTs", [ar.alloc([128, T], BF16) for _ in range(2)])
    vhs = Slots("vhs", [ar.alloc([128, NT, 128], BF16) for _ in range(2)])
    bms = Slots("bms", [ar.alloc([128, 5, 512], BF16) for _ in range(2)])
    ebs = Slots("ebs", [ar.alloc([128, 512], BF16) for _ in range(5)])
    kb_sb = ar.alloc([128, T], BF16); r_kbsb = Res("kb_sb")
    ki_sb = ar.alloc([64, T], BF16); r_kisb = Res("ki_sb")
    wi_tok = ar.alloc([128, NT, 16], F32); r_witok = Res("wi_tok")
    qis = Slots("qis", [ar.alloc([64, c.HI, 128], BF16) for _ in range(2)])
    sc_ch = [(ar.alloc([128, T], F32), Res(f"sc{i}")) for i in range(2)]
    work_ch = [(ar.alloc([128, T], F32), Res(f"work{i}")) for i in range(2)]

    def barrier():
        for e in ("pe", "act", "dve", "pool", "sp"):
            for n in ("pe", "act", "dve", "pool"):
                if kb.ecnt[n] > 0:
                    kb._wait(e, (kb.esem[n], kb.ecnt[n], None), True)
            for r in kb.dres:
                kb._wait(e, (r.dsem, r.dval, None), True)

    kb.dma("sp", ident_f[:], identd, writes=[r_identf])
    kb.dma("sp", mask_f[:], maskd, writes=[r_maskf])
    kb.dma("sp", relb_sb[:], relb15, writes=[r_relb])
    kb.op("dve", lambda: nc.vector.tensor_copy(out=ident_b[:], in_=ident_f[:]), reads=[r_identf], writes=[r_identb])
    kb.op("dve", lambda: nc.vector.memset(ones_f[:], 1.0), writes=[r_onesf])
    kb.op("dve", lambda: nc.vector.memset(ones_b[:], 1.0), writes=[r_onesb])
    kb.op("dve", lambda: nc.vector.memset(bd64_f[:], 0.0), writes=[r_bd64])
    kb.op("dve", lambda: nc.vector.memset(bd64_f[0:64, 0:64], 1.0), writes=[r_bd64])
    kb.op("dve", lambda: nc.vector.memset(bd64_f[64:128, 64:128], 1.0), writes=[r_bd64])
    kb.op("dve", lambda: nc.vector.memset(thr_c[:], -1.0e29), writes=[r_thrc])
    kb.op("dve", lambda: nc.vector.memset(epsb[:, 1:2], float(64 * EPS)), writes=[r_epsb])
    kb.op("dve", lambda: nc.vector.memset(epsb[:, 2:3], float(128 * EPS)), writes=[r_epsb])
    kb.op("dve", lambda: nc.vector.memset(epsb[:, 3:4], float(c.D * EPS)), writes=[r_epsb])

    dbg_n = [0]
    dbg_seen = set()

    def dbg_dump(name, ap, res, shape, dt=F32):
        if not debug or name in dbg_seen:
            return
        dbg_seen.add(name)
        d = nc.dram_tensor("dbg_" + name, list(shape), dt, kind="ExternalOutput").ap()
        kb.dma("sp", d, ap, reads=[res], writes=[Res("dbg_" + name)])

    bank_rot = [0]

    def next_bank(n=6):
        b = bank_rot[0] % n
        bank_rot[0] += 1
        return b


    def rsqrt_op(out_ap, out_res, in_ap, in_res, G):
        kb.op("act", lambda: nc.scalar.activation(out=out_ap, in_=in_ap, func=AF.Sqrt, bias=epsb[:, {0: 0, 64: 1, 128: 2}.get(G, 0):{0: 0, 64: 1, 128: 2}.get(G, 0) + 1] if G in (64, 128) else epsb[:, 3:4], scale=1.0),
              reads=[in_res, r_epsb], writes=[out_res])
        kb.op("dve", lambda: nc.vector.reciprocal(out=out_ap, in_=out_ap), reads=[out_res], writes=[out_res])

    def ssq_zero():
        kb.op("dve", lambda: nc.vector.memset(ssq_part[:], 0.0), writes=[r_ssq])

    def ssq_accum(src_ap, src_res, lo, n):
        sq, rsq = f32b.next()
        kb.op("act", lambda: nc.scalar.activation(out=sq[:, :n], in_=src_ap, func=AF.Square),
              reads=[src_res], writes=[rsq])
        kb.op("dve", lambda: nc.vector.tensor_tensor(out=ssq_part[:, lo:lo + n], in0=ssq_part[:, lo:lo + n],
                                                     in1=sq[:, :n], op=ALU.add),
              reads=[rsq, r_ssq], writes=[r_ssq])

    def finish_norm(G):
        for tb in range(NTB):
            b = 6
            kb.op("pe", lambda: nc.tensor.matmul(pb[b][:, :], lhsT=ones_f[:], rhs=ssq_part[:, tb * 512:(tb + 1) * 512],
                                                 start=True, stop=True),
                  reads=[r_onesf, r_ssq], writes=[pr[b]])
            rsqrt_op(rstd[:, tb * 512:(tb + 1) * 512], r_rstd, pb[b][:, :], pr[b], G)

    def phase_S(src, rsrc):
        ssq_zero()
        for k in range(KD):
            for tb in range(NTB):
                xk, rxk = xs5.next()
                kb.dma("sp", xk[:], src[k][:, tb * 512:(tb + 1) * 512], reads=[rsrc], writes=[rxk])
                ssq_accum(xk[:], rxk, tb * 512, 512)

    def phase_N(ps, src, rsrc, gcol, rg):
        for k in range(KD):
            for tb in range(NTBG):
                t0 = ps * TG + tb * 512
                xk, rxk = xs5.next()
                kb.dma("sp", xk[:], src[k][:, t0:t0 + 512], reads=[rsrc], writes=[rxk])
                kb.op("dve", lambda: nc.vector.scalar_tensor_tensor(out=hT[:, k, tb * 512:(tb + 1) * 512], in0=xk[:],
                                                                    scalar=gcol[:, k:k + 1], in1=rstd[:, t0:t0 + 512],
                                                                    op0=ALU.mult, op1=ALU.mult),
                      reads=[rxk, rg, r_rstd], writes=[hres[k]])

    def load_w(wdram, c0, M, nk):
        wt, rw = wsl.next()
        kb.dma("pool", wt[:, :nk, :M], wdram[:, c0:c0 + M].rearrange("(k p) c -> p k c", p=128), writes=[rw])
        return wt, rw

    def gemm_unit(bank, wt, rw, M, nk, tb):
        def fn():
            ins = None
            for k in range(nk):
                ins = nc.tensor.matmul(pb[bank][:M, :], lhsT=wt[:, k, :M], rhs=hT[:, k, tb * 512:(tb + 1) * 512],
                                       start=(k == 0), stop=(k == nk - 1))
            return ins
        kb.op("pe", fn, reads=[rw] + hres[:nk], writes=[pr[bank]])

    def stream_groups(wdram, groups, nk, unit_fn, inflight):
        loaded = []
        n = len(groups)

        def ld(i):
            loaded.append([load_w(wdram, c0, M, nk) + (M,) for (c0, M) in groups[i][1]])
        for i in range(min(inflight, n)):
            ld(i)
        for i in range(n):
            unit_fn(groups[i][0], loaded[i])
            if i + inflight < n:
                ld(i + inflight)

    def phase_P(l, ps):
        groups = []
        o = 0
        for h in range(c.HA): groups.append((("aq", h), [(o + h * 128, 128)]))
        o += c.AQ
        for h in range(c.HA): groups.append((("ak", h), [(o + h * 128, 128)]))
        o += c.AK
        for h in range(c.HA): groups.append((("av", h), [(o + h * 128, 128)]))
        o += c.AV
        for h in range(c.HB): groups.append((("bq", h), [(o + h * 128, 128)]))
        o += c.BQ
        groups.append((("bk", 0), [(o, 128)])); o += 128
        groups.append((("bv", 0), [(o, 128)])); o += 128
        for j in range(c.HI // 2): groups.append((("iq", j), [(o + j * 128, 128)]))
        o += c.IQ
        groups.append((("ikw", 0), [(o, c.IK + c.IW)]))

        def unit(tag, ws):
            kind, idx = tag
            wt, rw, M = ws[0]
            for tb in range(NTBG):
                b = next_bank(6)
                gemm_unit(b, wt, rw, M, KD, tb)
                t0 = ps * TG + tb * 512
                cols = slice(t0, t0 + 512)
                if kind in ("av", "bv", "iq"):
                    ob, rob = ob16.next()
                    kb.op("act", lambda: nc.scalar.copy(out=ob[:], in_=pb[b][:, :]), reads=[pr[b]], writes=[rob])
                    if kind == "av": dst, rd = vaT[idx][:, cols], R["vaT"]
                    elif kind == "bv": dst, rd = vbT[:, cols], R["vbT"]
                    else: dst, rd = qiT[idx][:, cols], R["qiT"]
                    kb.dma("sp", dst, ob[:], reads=[rob], writes=[rd])
                elif kind == "ikw":
                    ob, rob = ob16.next()
                    kb.op("act", lambda: nc.scalar.copy(out=ob[0:64, :], in_=pb[b][0:64, :]), reads=[pr[b]], writes=[rob])
                    kb.dma("sp", kiT[:, cols], ob[0:64, :], reads=[rob], writes=[R["kiT"]])
                    of, rof = f32a.next()
                    kb.op("act", lambda: nc.scalar.copy(out=of[64:64 + c.IW, :], in_=pb[b][64:64 + c.IW, :]),
                          reads=[pr[b]], writes=[rof])
                    kb.dma("sp", wiT[0:c.IW, cols], of[64:64 + c.IW, :], reads=[rof], writes=[R["wiT"]])
                else:
                    G = 64 if kind in ("aq", "ak") else 128
                    red, rred = (bd64_f, r_bd64) if G == 64 else (ones_f, r_onesf)
                    gi = {"aq": 0, "ak": 1, "bq": 2, "bk": 3}[kind]
                    sq, rsq = f32a.next()
                    kb.op("act", lambda: nc.scalar.activation(out=sq[:], in_=pb[b][:, :], func=AF.Square),
                          reads=[pr[b]], writes=[rsq])
                    sbk = 6
                    kb.op("pe", lambda: nc.tensor.matmul(pb[sbk][:, :], lhsT=red[:], rhs=sq[:], start=True, stop=True),
                          reads=[rred, rsq], writes=[pr[sbk]])
                    rs, rrs = f32b.next()
                    rsqrt_op(rs[:], rrs, pb[sbk][:, :], pr[sbk], G)
                    ob, rob = ob16.next()
                    kb.op("dve", lambda: nc.vector.scalar_tensor_tensor(out=ob[:], in0=pb[b][:, :], scalar=gs2[:, gi:gi + 1],
                                                                        in1=rs[:], op0=ALU.mult, op1=ALU.mult),
                          reads=[pr[b], rrs, r_gs2], writes=[rob])
                    if kind == "aq": dst, rd = qaT[idx][:, cols], R["qaT"]
                    elif kind == "ak": dst, rd = kaT[idx][:, cols], R["kaT"]
                    elif kind == "bq": dst, rd = qbT[idx][:, cols], R["qbT"]
                    else: dst, rd = kbT[:, cols], R["kbT"]
                    kb.dma("sp", dst, ob[:], reads=[rob], writes=[rd])

        stream_groups(w_in[l], groups, KD, unit, 3)

    def load_layer_params(l):
        lam_init = 0.8 - 0.6 * math.exp(-0.3 * l)
        kb.dma("sp", gA_sb[:], gA[l], writes=[r_gA])
        kb.dma("sp", gF_sb[:], gF[l], writes=[r_gF])
        kb.dma("sp", gsm_sb[:], gsm[l], writes=[r_gsm])
        kb.dma("sp", lq_sb[:], lqb[l], writes=[r_lq])
        kb.dma("sp", cw_sb[:], cwv[l], writes=[r_cw])
        kb.dma("sp", cb_sb[:], cbv[l], writes=[r_cb])
        sD = float(math.sqrt(c.D))
        kb.op("dve", lambda: nc.vector.tensor_scalar_mul(out=gA_sb[:], in0=gA_sb[:], scalar1=sD), reads=[r_gA], writes=[r_gA])
        kb.op("dve", lambda: nc.vector.tensor_scalar_mul(out=gF_sb[:], in0=gF_sb[:], scalar1=sD), reads=[r_gF], writes=[r_gF])
        facs = [1.0, 8.0, 1.0, math.sqrt(128.0), math.sqrt(128.0) * (1.0 - lam_init)]
        for i, f in enumerate(facs):
            kb.op("dve", lambda: nc.vector.tensor_scalar_mul(out=gs2[:, i:i + 1], in0=gsm_sb[:, i:i + 1], scalar1=float(f)),
                  reads=[r_gsm], writes=[r_gs2])
        kb.op("dve", lambda: nc.vector.tensor_tensor(out=lwork[:, 0:64], in0=lq_sb[:, 0:64], in1=lq_sb[:, 64:128], op=ALU.mult),
              reads=[r_lq], writes=[r_lwork])
        kb.op("dve", lambda: nc.vector.tensor_tensor(out=lwork[:, 64:128], in0=lq_sb[:, 128:192], in1=lq_sb[:, 192:256], op=ALU.mult),
              reads=[r_lq], writes=[r_lwork])
        kb.op("dve", lambda: nc.vector.reduce_sum(out=lsm[:, 0:1], in_=lwork[:, 0:64], axis=AX.X), reads=[r_lwork], writes=[r_lsm])
        kb.op("dve", lambda: nc.vector.reduce_sum(out=lsm[:, 1:2], in_=lwork[:, 64:128], axis=AX.X), reads=[r_lwork], writes=[r_lsm])
        kb.op("act", lambda: nc.scalar.activation(out=lsm[:, 2:4], in_=lsm[:, 0:2], func=AF.Exp), reads=[r_lsm], writes=[r_lsm])
        kb.op("dve", lambda: nc.vector.tensor_tensor(out=lsm[:, 4:5], in0=lsm[:, 3:4], in1=lsm[:, 2:3], op=ALU.subtract),
              reads=[r_lsm], writes=[r_lsm])
        kb.op("dve", lambda: nc.vector.tensor_scalar_add(out=lsm[:, 5:6], in0=lsm[:, 4:5], scalar1=float(-lam_init)),
              reads=[r_lsm], writes=[r_lsm])
    nlam = lsm[:, 5:6]

    def load_bias_tiles(hglob):
        bm, rbm = bms.next()
        kb.dma("pool", bm[:], btab[hglob], writes=[rbm])
        kb.op("dve", lambda: nc.vector.tensor_tensor(out=bm[:], in0=bm[:], in1=mask_f[:], op=ALU.add),
              reads=[rbm, r_maskf], writes=[rbm])
        return bm, rbm

    def transpose_v(src_dram, rsrc):
        vt, rvt = vTs.next()
        kb.dma("sp", vt[:], src_dram, reads=[rsrc], writes=[rvt])
        vh, rvh = vhs.next()
        for g in range(0, NT, 8):
            ng = min(8, NT - g)

            def fn():
                ins = None
                for i in range(ng):
                    ins = nc.tensor.transpose(out=pb7b[:, i * 128:(i + 1) * 128],
                                              in_=vt[:, (g + i) * 128:(g + i + 1) * 128], identity=ident_b[:])
                return ins
            kb.op("pe", fn, reads=[rvt, r_identb], writes=[pr[7]])
            kb.op("act", lambda: nc.scalar.copy(out=vh[:, g:g + ng, :],
                                                in_=pb7b[:, 0:ng * 128].rearrange("p (a b) -> p a b", b=128)),
                  reads=[pr[7]], writes=[rvh])
        return vh, rvh

    sbank = [0]

    def next_sbank():
        sbank[0] ^= 1
        return sbank[0]

    SB = [0, 1, 7]
    LA = 2
    ticker = [None]

    def tick():
        g = ticker[0]
        if g is not None:
            try:
                next(g)
            except StopIteration:
                ticker[0] = None

    def attn_block(hglob, qb, q_ap, rq, k_ap, rk, vh, rvh, bm, rbm, obank, dbank, use_mq):
        q0 = qb * 512
        nkt = 4 * qb + 4

        def emit_s(kt):
            cs = max(0, kt * 128 - q0)
            near = kt >= 4 * qb - 1
            sbank[0] = (sbank[0] + 1) % len(SB)
            sbk = SB[sbank[0]]
            cls = kt - 4 * qb + 1

            def fn_s():
                ins = nc.tensor.matmul(pb[sbk][:, cs:512], lhsT=k_ap[:, kt * 128:(kt + 1) * 128],
                                       rhs=q_ap[:, q0 + cs:q0 + 512], start=True, stop=(not near and not use_mq))
                if use_mq:
                    for qi in range(cs // 128, 4):
                        last = (qi == 3) and not near
                        ins = nc.tensor.matmul(pb[sbk][:, qi * 128:(qi + 1) * 128],
                                               lhsT=mq_all[:, 4 * qb + qi, kt * 128:(kt + 1) * 128],
                                               rhs=ident_b[:], start=False, stop=last)
                if near:
                    ins = nc.tensor.matmul(pb[sbk][:, cs:512], lhsT=ident_b[:], rhs=bm[:, cls, cs:512],
                                           start=False, stop=True)
                return ins
            rd = [rq, rk, r_identb]
            if near: rd.append(rbm)
            if use_mq: rd += [r_mq[4 * qb + qi] for qi in range(cs // 128, 4)]
            kb.op("pe", fn_s, reads=rd, writes=[pr[sbk]])
            eb, reb = ebs.next()
            if near:
                kb.op("act", lambda: nc.scalar.activation(out=eb[:, cs:512], in_=pb[sbk][:, cs:512], func=AF.Exp),
                      reads=[pr[sbk]], writes=[reb])
            else:
                kb.op("act", lambda: nc.scalar.activation(out=eb[:, cs:512], in_=pb[sbk][:, cs:512], func=AF.Exp,
                                                          bias=relb_sb[:, hglob:hglob + 1]),
                      reads=[pr[sbk], r_relb], writes=[reb])
            return (kt, cs, eb, reb)

        def emit_pv(st):
            kt, cs, eb, reb = st

            def fn_pv():
                nc.tensor.matmul(pb[obank][:, cs:512], lhsT=vh[:, kt, :], rhs=eb[:, cs:512],
                                 start=(kt == 0), stop=(kt == nkt - 1))
                return nc.tensor.matmul(pb[dbank][:, cs:512], lhsT=ones_b[:], rhs=eb[:, cs:512],
                                        start=(kt == 0), stop=(kt == nkt - 1))
            kb.op("pe", fn_pv, reads=[rvh, reb, r_onesb], writes=[pr[obank], pr[dbank]])

        pend = []
        for kt in range(nkt + LA):
            if kt < nkt:
                pend.append(emit_s(kt))
            if kt >= LA:
                emit_pv(pend.pop(0))
                tick()

    def phase_A_diff():
        for h in range(c.HA):
            qh, rqh = qhs.next(); kh, rkh = khs.next()
            kb.dma("sp", qh[:], qaT[h], reads=[R["qaT"]], writes=[rqh])
            kb.dma("sp", kh[:], kaT[h], reads=[R["kaT"]], writes=[rkh])
            bm, rbm = load_bias_tiles(h)
            vh, rvh = transpose_v(vaT[h], R["vaT"])
            for qb in range(NTB):
                attn_block(h, qb, qh[0:64, :], rqh, kh[0:64, :], rkh, vh, rvh, bm, rbm, 2, 4, False)
                attn_block(h, qb, qh[64:128, :], rqh, kh[64:128, :], rkh, vh, rvh, bm, rbm, 3, 5, False)
                r1, rr1 = f32a.next(); r2, rr2 = f32a.next()
                kb.op("dve", lambda: nc.vector.reciprocal(out=r1[:], in_=pb[4][:, :]), reads=[pr[4]], writes=[rr1])
                kb.op("dve", lambda: nc.vector.reciprocal(out=r2[:], in_=pb[5][:, :]), reads=[pr[5]], writes=[rr2])
                kb.op("dve", lambda: nc.vector.tensor_tensor(out=r1[:], in0=pb[2][:, :], in1=r1[:], op=ALU.mult),
                      reads=[pr[2], rr1], writes=[rr1])
                kb.op("dve", lambda: nc.vector.tensor_tensor(out=r2[:], in0=pb[3][:, :], in1=r2[:], op=ALU.mult),
                      reads=[pr[3], rr2], writes=[rr2])
                if h == 0 and qb == 0:
                    dbg_dump("t1", r1[:], rr1, [128, 512]); dbg_dump("t2", r2[:], rr2, [128, 512]); dbg_dump("lsm", lsm[:], r_lsm, [128, 8])
                oa, roa = f32b.next()
                kb.op("dve", lambda: nc.vector.scalar_tensor_tensor(out=oa[:], in0=r2[:], scalar=nlam, in1=r1[:],
                                                                    op0=ALU.mult, op1=ALU.add),
                      reads=[rr1, rr2, r_lsm], writes=[roa])
                if h == 0 and qb == 0:
                    dbg_dump("oa", oa[:], roa, [128, 512])
                sq, rsq = f32b.next()
                kb.op("act", lambda: nc.scalar.activation(out=sq[:], in_=oa[:], func=AF.Square), reads=[roa], writes=[rsq])
                kb.op("pe", lambda: nc.tensor.matmul(pb[6][:, :], lhsT=ones_f[:], rhs=sq[:], start=True, stop=True),
                      reads=[r_onesf, rsq], writes=[pr[6]])
                rs, rrs = f32a.next()
                rsqrt_op(rs[:], rrs, pb[6][:, :], pr[6], 128)
                ob, rob = ob16.next()
                kb.op("dve", lambda: nc.vector.scalar_tensor_tensor(out=ob[:], in0=oa[:], scalar=gs2[:, 4:5], in1=rs[:],
                                                                    op0=ALU.mult, op1=ALU.mult),
                      reads=[roa, rrs, r_gs2], writes=[rob])
                kb.dma("sp", mixT[h][:, qb * 512:(qb + 1) * 512], ob[:], reads=[rob], writes=[R["mixT"]])

    HOP = False

    def index_tile(qt, ch):
        sc, rsc = sc_ch[ch]
        work, r_work = work_ch[ch]
        m8s_, m8cs_ = m8s_ch[ch], m8cs_ch[ch]
        qiv = qiT.rearrange("b (two p) t -> p (b two) t", two=2)
        Lk = (qt + 1) * 128
        qi_t, rqi = qis.t[ch], qis.r[ch]
        kb.dma("sp", qi_t[:], qiv[:, :, qt * 128:(qt + 1) * 128], reads=[R["qiT"]], writes=[rqi])
        for kbk in range((Lk + 511) // 512):
            n = min(512, Lk - kbk * 512)
            for hi in range(c.HI):
                b = 6
                kb.op("pe", lambda: nc.tensor.matmul(pb[b][:, :n], lhsT=qi_t[:, hi, :], rhs=ki_sb[:, kbk * 512:kbk * 512 + n],
                                                     start=True, stop=True),
                      reads=[rqi, r_kisb], writes=[pr[b]])
                rl, rrl = f32b.next()
                kb.op("act", lambda: nc.scalar.activation(out=rl[:, :n], in_=pb[b][:, :n], func=AF.Relu),
                      reads=[pr[b]], writes=[rrl])
                dst = sc[:, kbk * 512:kbk * 512 + n]
                if hi == 0:
                    kb.op("dve", lambda: nc.vector.tensor_scalar_mul(out=dst, in0=rl[:, :n], scalar1=wi_tok[:, qt, 0:1]),
                          reads=[rrl, r_witok], writes=[rsc])
                else:
                    kb.op("dve", lambda: nc.vector.scalar_tensor_tensor(out=dst, in0=rl[:, :n], scalar=wi_tok[:, qt, hi:hi + 1],
                                                                        in1=dst, op0=ALU.mult, op1=ALU.add),
                          reads=[rrl, r_witok, rsc], writes=[rsc])
                yield
        kb.op("dve", lambda: nc.vector.memset(sc[0:64, Lk - 64:Lk], NINF), reads=[rsc], writes=[rsc])
        if qt == 2:
            dbg_dump("sc2", sc[:, :Lk], rsc, [128, Lk])
        if Lk > c.TOPK:
            m8c = None
            nr = c.TOPK // 8
            for r in range(nr):
                src = sc if r == 0 else work
                rsrc = rsc if r == 0 else r_work
                m8, rm8 = m8s_.next()
                kb.op("dve", lambda: nc.vector.max(out=m8[:], in_=src[:, :Lk]), reads=[rsrc], writes=[rm8], strict=True)
                if HOP:
                    m8c, rm8c = m8cs_.next()
                    kb.op("act", lambda: nc.scalar.copy(out=m8c[:], in_=m8[:]), reads=[rm8], writes=[rm8c])
                    yield
                else:
                    m8c, rm8c = m8, rm8
                if r < nr - 1:
                    kb.op("dve", lambda: nc.vector.match_replace(out=work[:, :Lk], in_to_replace=m8c[:], in_values=src[:, :Lk],
                                                                 imm_value=NINF),
                          reads=[rsrc, rm8c], writes=[r_work], strict=True)
            thr_ap, rthr = m8c[:, 7:8], rm8c
        else:
            thr_ap, rthr = thr_c[:, 0:1], r_thrc
        kb.op("dve", lambda: nc.vector.tensor_scalar(out=mq_all[:, qt, :Lk], in0=sc[:, :Lk], scalar1=thr_ap, scalar2=NEG,
                                                     op0=ALU.is_lt, op1=ALU.mult),
              reads=[rsc, rthr], writes=[r_mq[qt]], strict=True)
        if qt == 2:
            dbg_dump("m8", m8c[:], rm8c, [128, 8])
            dbg_dump("mq2", mq_all[:, qt, :Lk], r_mq[qt], [128, Lk], BF16)
        yield

    def index_chain(ch):
        for qt in range(ch, NT, 2):
            yield from index_tile(qt, ch)

    def phase_A_index():
        kb.dma("sp", ki_sb[:], kiT, reads=[R["kiT"]], writes=[r_kisb])
        for tb in range(NTB):
            wt_, rwt_ = f32a.next()
            kb.dma("sp", wt_[0:c.IW, :], wiT[0:c.IW, tb * 512:(tb + 1) * 512], reads=[R["wiT"]], writes=[rwt_])
            for t4 in range(4):
                t = tb * 4 + t4
                kb.op("pe", lambda: nc.tensor.transpose(out=pb[6][:, 0:c.IW], in_=wt_[0:c.IW, t4 * 128:(t4 + 1) * 128],
                                                        identity=ident_f[0:c.IW, 0:c.IW]),
                      reads=[rwt_, r_identf], writes=[pr[6]])
                kb.op("act", lambda: nc.scalar.copy(out=wi_tok[:, t, 0:c.IW], in_=pb[6][:, 0:c.IW]),
                      reads=[pr[6]], writes=[r_witok])
        yield
        gens = [index_chain(0), index_chain(1)]
        while gens:
            for g in list(gens):
                try:
                    next(g)
                except StopIteration:
                    gens.remove(g)
            yield

    def phase_A_dsa():
        kb.dma("sp", kb_sb[:], kbT, reads=[R["kbT"]], writes=[r_kbsb])
        vh, rvh = transpose_v(vbT, R["vbT"])
        for h in range(c.HB):
            qh, rqh = qhs.next()
            kb.dma("sp", qh[:], qbT[h], reads=[R["qbT"]], writes=[rqh])
            bm, rbm = load_bias_tiles(c.HA + h)
            for qb in range(NTB):
                ob_, db_ = (2, 4) if (qb % 2 == 0) else (3, 5)
                attn_block(c.HA + h, qb, qh, rqh, kb_sb, r_kbsb, vh, rvh, bm, rbm, ob_, db_, True)
                r1, rr1 = f32a.next()
                kb.op("dve", lambda: nc.vector.reciprocal(out=r1[:], in_=pb[db_][:, :]), reads=[pr[db_]], writes=[rr1])
                ob, rob = ob16.next()
                kb.op("dve", lambda: nc.vector.tensor_tensor(out=ob[:], in0=pb[ob_][:, :], in1=r1[:], op=ALU.mult),
                      reads=[pr[ob_], rr1], writes=[rob])
                kb.dma("sp", mixT[c.HA + h][:, qb * 512:(qb + 1) * 512], ob[:], reads=[rob], writes=[R["mixT"]])

    def residual_epilogue(b, src, rsrc, dst, rdst, cb, t0, accum):
        cols = slice(t0, t0 + 512)
        xin, rxin = xs5.next()
        kb.dma("sp", xin[:], src[cb][:, cols], reads=[rsrc], writes=[rxin])
        xn, rxn = f32a.next()
        kb.op("dve", lambda: nc.vector.tensor_tensor(out=xn[:], in0=pb[b][:, :], in1=xin[:], op=ALU.add),
              reads=[pr[b], rxin], writes=[rxn])
        kb.dma("sp", dst[cb][:, cols], xn[:], reads=[rxn], writes=[rdst])
        if accum:
            ssq_accum(xn[:], rxn, t0, 512)

    def phase_O(l, ps, src, rsrc):
        for k in range(KM):
            kb.dma("sp", hT[:, k, :], mixT[k][:, ps * TG:(ps + 1) * TG], reads=[R["mixT"]], writes=[hres[k]])
        groups = [(cb, [(cb * 128, 128)]) for cb in range(KD)]

        def unit(cb, ws):
            wt, rw, M = ws[0]
            for tb in range(NTBG):
                b = next_bank(6)
                gemm_unit(b, wt, rw, M, KM, tb)
                residual_epilogue(b, src, rsrc, xa, R["xa"], cb, ps * TG + tb * 512, True)
        stream_groups(w_out[l], groups, KM, unit, 3)

    def phase_F1(l, ps):
        groups = [(fc, [(fc * 128, 128), (c.DFF + fc * 128, 128)]) for fc in range(NFC)]

        def unit(fc, ws):
            (wg, rwg, _), (wu, rwu, _) = ws
            prev = None
            for tb in range(NTBG):
                bg = next_bank(6)
                gemm_unit(bg, wg, rwg, 128, KD, tb)
                bu = next_bank(6)
                gemm_unit(bu, wu, rwu, 128, KD, tb)
                gs, rgs = gsl.next()
                kb.op("act", lambda: nc.scalar.copy(out=gs[:, 2:514], in_=pb[bg][:, :]), reads=[pr[bg]], writes=[rgs])
                if tb == 0:
                    if ps == 0:
                        kb.op("dve", lambda: nc.vector.memset(gs[:, 0:2], 0.0), reads=[rgs], writes=[rgs])
                    else:
                        kb.op("dve", lambda: nc.vector.tensor_copy(out=gs[:, 0:2], in_=gcar[:, fc, :]), reads=[r_gcar, rgs], writes=[rgs])
                else:
                    pg, rpg = prev
                    kb.op("dve", lambda: nc.vector.tensor_copy(out=gs[:, 0:2], in_=pg[:, 512:514]), reads=[rpg, rgs], writes=[rgs])
                if tb == NTBG - 1 and ps < NPASS - 1:
                    kb.op("dve", lambda: nc.vector.tensor_copy(out=gcar[:, fc, :], in_=gs[:, 512:514]), reads=[rgs], writes=[r_gcar])
                prev = (gs, rgs)
                a, ra = f32a.next()
                kb.op("dve", lambda: nc.vector.tensor_scalar(out=a[:], in0=gs[:, 2:514], scalar1=cw_sb[:, fc, 2:3],
                                                             scalar2=cb_sb[:, fc:fc + 1], op0=ALU.mult, op1=ALU.add),
                      reads=[rgs, r_cw, r_cb], writes=[ra])
                kb.op("dve", lambda: nc.vector.scalar_tensor_tensor(out=a[:], in0=gs[:, 1:513], scalar=cw_sb[:, fc, 1:2], in1=a[:],
                                                                    op0=ALU.mult, op1=ALU.add),
                      reads=[rgs, r_cw, ra], writes=[ra])
                kb.op("dve", lambda: nc.vector.scalar_tensor_tensor(out=a[:], in0=gs[:, 0:512], scalar=cw_sb[:, fc, 0:1], in1=a[:],
                                                                    op0=ALU.mult, op1=ALU.add),
                      reads=[rgs, r_cw, ra], writes=[ra])
                s, rs_ = f32b.next()
                kb.op("act", lambda: nc.scalar.activation(out=s[:], in_=a[:], func=AF.Silu), reads=[ra], writes=[rs_])
                ob, rob = ob16.next()
                kb.op("dve", lambda: nc.vector.tensor_tensor(out=ob[:], in0=pb[bu][:, :], in1=s[:], op=ALU.mult),
                      reads=[pr[bu], rs_], writes=[rob])
                t0 = ps * TG + tb * 512
                kb.dma("sp", actT[fc][:, t0:t0 + 512], ob[:], reads=[rob], writes=[R["actT"]])
        stream_groups(w_gu[l], groups, KD, unit, 2)

    def phase_F2(l, dst, rdst, accum):
        if accum:
            ssq_zero()
        wl = w_dn[l]
        for tg in range(NTB // NTBF):
            for cg in range(KD // CCF):
                c0 = cg * CCF * 128
                wd = None
                for fc0 in range(0, NFC, AG):
                    if fc0 % GF == 0:
                        ng = min(GF, NFC - fc0)
                        wd, rwd = wds.next()
                        kb.dma("pool", wd[:, :ng, :], wl[fc0 * 128:(fc0 + ng) * 128, c0:c0 + CCF * 128].rearrange("(f p) c -> p f c", p=128),
                               writes=[rwd])
                    na = min(AG, NFC - fc0)
                    at, rat = asl.next()
                    kb.dma("sp", at[:, :na, :], actT[fc0:fc0 + na, :, tg * TF:(tg + 1) * TF].rearrange("f p t -> p f t"),
                           reads=[R["actT"]], writes=[rat])
                    for a in range(na):
                        fc = fc0 + a

                        def fn():
                            ins = None
                            for cc in range(CCF):
                                for tb in range(NTBF):
                                    ins = nc.tensor.matmul(pb[cc * NTBF + tb][:, :], lhsT=wd[:, fc % GF, cc * 128:(cc + 1) * 128],
                                                           rhs=at[:, a, tb * 512:(tb + 1) * 512], start=(fc == 0), stop=(fc == NFC - 1))
                            return ins
                        kb.op("pe", fn, reads=[rwd, rat], writes=[pr[i] for i in range(CCF * NTBF)])
                for cc in range(CCF):
                    for tb in range(NTBF):
                        residual_epilogue(cc * NTBF + tb, xa, R["xa"], dst, rdst, cg * CCF + cc, tg * TF + tb * 512, accum)

    cur, rcur = xT, R["xT"]
    for l in range(L):
        load_layer_params(l)
        if l == 0:
            phase_S(cur, rcur)
        finish_norm(c.D)
        for ps in range(NPASS):
            phase_N(ps, cur, rcur, gA_sb, r_gA)
            phase_P(l, ps)
        barrier()
        ticker[0] = phase_A_index()
        phase_A_diff()
        if ticker[0] is not None:
            for _ in ticker[0]:
                pass
            ticker[0] = None
        if l == 0:
            dbg_dump("mqall", big.t[:, 0:NT * T], r_mq[NT - 1], [128, NT * T], BF16)
        phase_A_dsa()
        barrier()
        ssq_zero()
        for ps in range(NPASS):
            phase_O(l, ps, cur, rcur)
        finish_norm(c.D)
        for ps in range(NPASS):
            phase_N(ps, xa, R["xa"], gF_sb, r_gF)
            phase_F1(l, ps)
        last = (l == L - 1)
        nxt, rnxt = (yT, R["yT"]) if last else (xb, R["xb"])
        phase_F2(l, nxt, rnxt, not last)
        cur, rcur = nxt, rnxt
    kb.final_wait("sp", R["yT"])
    return nc, kb


def _rel_bucket_np(rel):
    import jax
    import jax.numpy as jnp
    cpu = jax.devices("cpu")[0]
    with jax.default_device(cpu):
        rel = jnp.asarray(rel, dtype=jnp.int32)
        nb = 32 // 2
        max_exact = nb // 2
        bucket = jnp.where(rel > 0, nb, 0)
        n = jnp.abs(rel)
        nf = jnp.maximum(n, 1).astype(jnp.float32)
        large = max_exact + (jnp.log(nf / max_exact) / math.log(128 / max_exact) * (nb - max_exact)).astype(jnp.int32)
        large = jnp.minimum(large, nb - 1)
        out = bucket + jnp.where(n < max_exact, n, large)
        return np.asarray(out)


def prep_shared(cfg, inputs):
    c = cfg
    L = c.DEPTH
    f = lambda a: np.ascontiguousarray(np.asarray(a, dtype=np.float32))
    col = lambda v, k: np.ascontiguousarray(v.reshape(k, 128).T)
    sh = {}
    sh["w_in"] = f(inputs["w_in"]); sh["w_out"] = f(inputs["w_out"])
    sh["w_gu"] = f(inputs["w_gate_up"]); sh["w_dn"] = f(inputs["w_down"])
    an = f(inputs["attn_norm"]); fn = f(inputs["ffn_norm"])
    sh["gA"] = np.stack([col(an[l], c.KD) for l in range(L)])
    sh["gF"] = np.stack([col(fn[l], c.KD) for l in range(L)])
    gsm = np.zeros((L, 128, 8), np.float32)
    aq = f(inputs["a_q_norm"]); ak = f(inputs["a_k_norm"]); ao = f(inputs["a_out_norm"])
    bq = f(inputs["b_q_norm"]); bk = f(inputs["b_k_norm"])
    for l in range(L):
        gsm[l, :, 0] = np.concatenate([aq[l], aq[l]])
        gsm[l, :, 1] = np.concatenate([ak[l], ak[l]])
        gsm[l, :, 2] = bq[l]; gsm[l, :, 3] = bk[l]; gsm[l, :, 4] = ao[l]
    sh["gsm"] = gsm
    lq = f(inputs["lambda_qk"]).reshape(L, 1, 256)
    sh["lqb"] = np.ascontiguousarray(np.broadcast_to(lq, (L, 128, 256)))
    cw = f(inputs["conv_w"]); cb = f(inputs["conv_b"])
    sh["cwv"] = np.ascontiguousarray(cw.reshape(L, 3, c.NFC, 128).transpose(0, 3, 2, 1))
    sh["cbv"] = np.ascontiguousarray(cb.reshape(L, c.NFC, 128).transpose(0, 2, 1))
    rb = f(inputs["rel_bias"])
    sh["relb15"] = np.ascontiguousarray(np.broadcast_to(rb[15][None, :], (128, c.NH)))
    p = np.arange(128)[:, None, None]
    dl = (np.arange(5) * 128 - 128)[None, :, None]
    j = np.arange(512)[None, None, :]
    d = dl + p - j
    bidx = _rel_bucket_np(d)
    sh["btab"] = np.ascontiguousarray(rb[bidx].transpose(3, 0, 1, 2))
    vis = (dl // 64 + p // 64) <= (j // 64)
    sh["maskd"] = np.where(vis, 0.0, NEG).astype(np.float32)
    sh["identd"] = np.eye(128, dtype=np.float32)
    return sh


def run(cfg, inputs, debug=False):
    c = cfg
    sh = prep_shared(c, inputs)
    x = np.asarray(inputs["x"], dtype=np.float32)
    B = x.shape[0]
    in_maps = []
    for b in range(B):
        m = dict(sh)
        m["xT"] = np.ascontiguousarray(x[b].T.reshape(c.KD, 128, c.T))
        in_maps.append(m)
    nc, kb = build_program(c, debug=debug)
    res = run_bass_kernel_spmd(nc, in_maps, core_ids=list(range(B)))
    out = np.empty_like(x)
    for b in range(B):
        out[b] = res.results[b]["yT"].reshape(c.D, c.T).T
    if debug:
        return out, res
    return out


def kernel(**inputs):
    return run(FULL, inputs)
```
